# Optimizing a Trainium2 kernel written in Bass

```python
import math
import jax, jax.numpy as jnp
from jax import lax
import numpy as np

D_MODEL = 1024
BATCH = 8
SEQ = 8192
DEPTH = 4

CHUNK = 64
N_PAST_CHUNKS = 8
N_MIXERS = 2
N_HEADS = 16
HEAD_DIM = 64
MIX_WIDTH = N_HEADS * HEAD_DIM
MAX_REL = 128
Q_BLOCK = 128
N_A_LAYERS = (DEPTH + N_MIXERS - 1) // N_MIXERS
N_B_LAYERS = DEPTH // N_MIXERS
LN_EPS = 1e-5
NEG_INF = -1e30
DEEPNORM_ALPHA = (2.0 * DEPTH) ** 0.25
DEEPNORM_BETA = (8.0 * DEPTH) ** -0.25

kernel_name = "hybrid_chunk_relpos_fox_deepnorm"


def layer_norm(x, g, b):
    xf = x.astype(jnp.float32)
    mu = jnp.mean(xf, axis=-1, keepdims=True)
    var = jnp.mean(jnp.square(xf - mu), axis=-1, keepdims=True)
    y = (xf - mu) * lax.rsqrt(var + LN_EPS) * g.astype(jnp.float32) + b.astype(jnp.float32)
    return y.astype(x.dtype)


def split_heads(t):
    b, s, _ = t.shape
    return t.reshape(b, s, N_HEADS, HEAD_DIM).transpose(0, 2, 1, 3)


def merge_heads(t):
    b, h, s, d = t.shape
    return t.transpose(0, 2, 1, 3).reshape(b, s, h * d)


def chunk_relpos_attention(q, k, v, rel_bias):
    b, h, s, d = q.shape
    n_chunks = s // CHUNK
    pad = N_PAST_CHUNKS * CHUNK
    band = (N_PAST_CHUNKS + 1) * CHUNK
    kp = jnp.pad(k, ((0, 0), (0, 0), (pad, 0), (0, 0)))
    vp = jnp.pad(v, ((0, 0), (0, 0), (pad, 0), (0, 0)))
    qi = jnp.arange(CHUNK)[:, None]
    kj = jnp.arange(band)[None, :]
    rel_idx = jnp.clip(qi - kj + pad, -MAX_REL, MAX_REL) + MAX_REL
    bias = rel_bias.astype(jnp.float32)[:, rel_idx]
    scale = 1.0 / math.sqrt(d)

    def one_chunk(c):
        start = c * CHUNK
        qc = lax.dynamic_slice_in_dim(q, start, CHUNK, axis=2)
        kc = lax.dynamic_slice_in_dim(kp, start, band, axis=2)
        vc = lax.dynamic_slice_in_dim(vp, start, band, axis=2)
        logits = jnp.einsum('bhqd,bhkd->bhqk', qc, kc).astype(jnp.float32) * scale + bias
        valid = (start - pad + jnp.arange(band)) >= 0
        logits = jnp.where(valid[None, None, None, :], logits, NEG_INF)
        p = jax.nn.softmax(logits, axis=-1).astype(v.dtype)
        return jnp.einsum('bhqk,bhkd->bhqd', p, vc)

    out = lax.map(one_chunk, jnp.arange(n_chunks))
    return out.transpose(1, 2, 0, 3, 4).reshape(b, h, s, d)


def forgetting_attention(q, k, v, log_f):
    b, h, s, d = q.shape
    n_blocks = s // Q_BLOCK
    cum_f = jnp.cumsum(log_f, axis=-1)
    q_blocks = q.reshape(b, h, n_blocks, Q_BLOCK, d).transpose(2, 0, 1, 3, 4)
    f_blocks = cum_f.reshape(b, h, n_blocks, Q_BLOCK).transpose(2, 0, 1, 3)
    kpos = jnp.arange(s)
    scale = 1.0 / math.sqrt(d)

    def one_block(args):
        qb, fb, blk = args
        qpos = blk * Q_BLOCK + jnp.arange(Q_BLOCK)
        logits = jnp.einsum('bhqd,bhkd->bhqk', qb, k).astype(jnp.float32) * scale
        logits = logits + fb[..., :, None] - cum_f[..., None, :]
        mask = kpos[None, :] <= qpos[:, None]
        logits = jnp.where(mask[None, None], logits, NEG_INF)
        p = jax.nn.softmax(logits, axis=-1).astype(v.dtype)
        return jnp.einsum('bhqk,bhkd->bhqd', p, v)

    out = lax.map(one_block, (q_blocks, f_blocks, jnp.arange(n_blocks)))
    return out.transpose(1, 2, 0, 3, 4).reshape(b, h, s, d)


def mixer_a(x, w_in, rel_bias, w_out):
    h = x @ w_in
    q, k, v, z = jnp.split(h, 4, axis=-1)
    o = merge_heads(chunk_relpos_attention(split_heads(q), split_heads(k), split_heads(v), rel_bias))
    return (o * jax.nn.silu(z)) @ w_out


def mixer_b(x, w_in, w_f, b_f, w_out):
    h = x @ w_in
    q, k, v, z = jnp.split(h, 4, axis=-1)
    log_f = jax.nn.log_sigmoid((x @ w_f + b_f).astype(jnp.float32))
    log_f = log_f.transpose(0, 2, 1)
    o = merge_heads(forgetting_attention(split_heads(q), split_heads(k), split_heads(v), log_f))
    return (o * jax.nn.silu(z)) @ w_out


def _in_proj(key, n):
    w = jax.random.normal(key, (n, D_MODEL, 4 * MIX_WIDTH), jnp.float32) * D_MODEL ** -0.5
    col_scale = jnp.concatenate([
        jnp.ones((2 * MIX_WIDTH,), jnp.float32),
        jnp.full((MIX_WIDTH,), DEEPNORM_BETA, jnp.float32),
        jnp.ones((MIX_WIDTH,), jnp.float32)])
    return w * col_scale


def _out_proj(key, n):
    return jax.random.normal(key, (n, MIX_WIDTH, D_MODEL), jnp.float32) * (MIX_WIDTH ** -0.5) * DEEPNORM_BETA


def setup_inputs(seed: int = 0) -> dict:
    key = jax.random.key(seed)
    ks = jax.random.split(key, 10)
    x = jax.random.normal(ks[0], (BATCH, SEQ, D_MODEL), jnp.float32)
    w_in_a = _in_proj(ks[1], N_A_LAYERS)
    rel_bias_a = jax.random.normal(ks[2], (N_A_LAYERS, N_HEADS, 2 * MAX_REL + 1), jnp.float32) * 0.1
    w_out_a = _out_proj(ks[3], N_A_LAYERS)
    w_in_b = _in_proj(ks[4], N_B_LAYERS)
    w_f_b = jax.random.normal(ks[5], (N_B_LAYERS, D_MODEL, N_HEADS), jnp.float32) * D_MODEL ** -0.5
    b_f_b = 2.0 + 0.5 * jax.random.normal(ks[6], (N_B_LAYERS, N_HEADS), jnp.float32)
    w_out_b = _out_proj(ks[7], N_B_LAYERS)
    ln_g = 1.0 + 0.05 * jax.random.normal(ks[8], (DEPTH, D_MODEL), jnp.float32)
    ln_b = 0.02 * jax.random.normal(ks[9], (DEPTH, D_MODEL), jnp.float32)
    return {"x": x, "w_in_a": w_in_a, "rel_bias_a": rel_bias_a, "w_out_a": w_out_a,
            "w_in_b": w_in_b, "w_f_b": w_f_b, "b_f_b": b_f_b, "w_out_b": w_out_b,
            "ln_g": ln_g, "ln_b": ln_b}


def reference(x, w_in_a, rel_bias_a, w_out_a, w_in_b, w_f_b, b_f_b, w_out_b, ln_g, ln_b):
    for i in range(DEPTH):
        j = i // N_MIXERS
        if i % N_MIXERS == 0:
            y = mixer_a(x, w_in_a[j], rel_bias_a[j], w_out_a[j])
        else:
            y = mixer_b(x, w_in_b[j], w_f_b[j], b_f_b[j], w_out_b[j])
        x = layer_norm(DEEPNORM_ALPHA * x + y, ln_g[i], ln_b[i])
    return x
```

```python
import numpy as np
from contextlib import ExitStack
import concourse.bass as bass
import concourse.mybir as mybir
from concourse.bass_utils import run_bass_kernel_spmd

F32 = mybir.dt.float32
BF16 = mybir.dt.bfloat16
AF = mybir.ActivationFunctionType
ALU = mybir.AluOpType

D = 1024
H = 16
NEG = -30000.0
ALPHA = 8.0 ** 0.25
EPS = 1e-5
SEQ = 8192
NCORES = 8
import os
KSKIP = os.environ.get("KSKIP", "").split(",")


class Chan:
    def __init__(self, sem, name):
        self.sem = sem
        self.count = 0
        self.name = name


class Prog:
    ENG = ("sync", "act", "pe", "dve", "pool")

    def __init__(self, nc, stack):
        self.nc = nc
        self.stack = stack
        self.q = {e: [] for e in self.ENG}
        self.waited = {}
        self.chans = []
        self.ech = {e: self.chan("e_" + e) for e in ("act", "pe", "dve", "pool")}
        self.nops = 0

    def chan(self, name):
        sem = self.stack.enter_context(self.nc.semaphore(name))
        c = Chan(sem, name)
        self.chans.append(c)
        return c

    def add(self, eng, fn, waits=(), sig=None, inc=1):
        ws = []
        for w in waits:
            if w is None:
                continue
            ch, v = w
            if v <= 0:
                continue
            key = (eng, ch.name)
            if self.waited.get(key, 0) >= v:
                continue
            self.waited[key] = v
            ws.append((ch, v))
        ch = None
        if sig is True:
            ch = self.ech[eng]
        elif sig:
            ch = sig
        ret = None
        if ch is not None:
            ch.count += inc
            ret = (ch, ch.count)
        self.q[eng].append((fn, ws, ch, inc))
        self.nops += 1
        return ret

    def dma(self, eng, out, in_, ch, waits=()):
        return self.add(eng, lambda e: e.dma_start(out=out, in_=in_), waits=waits, sig=ch, inc=16)

    def barrier(self):
        ws = [(c, c.count) for c in self.chans if c.count > 0]
        for e in self.ENG:
            self.add(e, None, waits=ws)

    def emit(self):
        nc = self.nc
        q = self.q
        self.q = {e: [] for e in self.ENG}
        with nc.Block() as block:
            def mk(name):
                def run(eng):
                    for fn, ws, ch, inc in q[name]:
                        for c, v in ws:
                            eng.wait_ge(c.sem, v)
                        if fn is None:
                            continue
                        ins = fn(eng)
                        if ch is not None:
                            ins.then_inc(ch.sem, inc)
                return run
            block.sync(mk("sync"))
            block.scalar(mk("act"))
            block.tensor(mk("pe"))
            block.vector(mk("dve"))
            block.gpsimd(mk("pool"))


def build(S, ltypes, dbg=None):
    nc = bass.Bass("TRN2", target_bir_lowering=False)
    NL = len(ltypes)
    NT = S // 128
    NB = S // 512
    QB = min(1024, S)
    NQB = S // QB

    def din(name, shape, dt=F32):
        return nc.dram_tensor(name, shape, dt, kind="ExternalInput").ap()

    x_d = din("x", [S, D])
    w_in_a = din("w_in_a", [2, D, 4 * D])
    w_in_b = din("w_in_b", [2, D, 4 * D])
    w_out_a = din("w_out_a", [2, D, D])
    w_out_b = din("w_out_b", [2, D, D])
    w_f_b = din("w_f_b", [2, D, H])
    bfb_d = din("bfb", [2, H, 1])
    biasT_d = din("biasT", [2, H, 128, 640])
    lng_d = din("lng", [4, 128, D])
    lnb_d = din("lnb", [4, 128, D])
    ident_d = din("ident", [128, 128])
    tri_d = din("tri", [128, 128])
    maskA_d = din("maskA", [128, 640])
    cb_d = din("cbias", [2, 128, H])
    y_d = nc.dram_tensor("y", [S, D], F32, kind="ExternalOutput").ap()

    xbuf = nc.dram_tensor("xbuf", [S, D], F32).ap()
    QT = nc.dram_tensor("QT", [D, S], BF16).ap()
    KT = nc.dram_tensor("KT", [D, S], BF16).ap()
    VT = nc.dram_tensor("VT", [D, S], BF16).ap()
    ZT = nc.dram_tensor("ZT", [D, S], BF16).ap()
    GT = nc.dram_tensor("GT", [D, S], BF16).ap()
    FT = nc.dram_tensor("FT", [H, S], BF16).ap()

    with ExitStack() as gst:
        uid = [0]

        def mk_alloc(st):
            def sb(name, shape, dt):
                uid[0] += 1
                return st.enter_context(nc.sbuf_tensor("%s_%d" % (name, uid[0]), shape, dt))

            def ps(name, shape, dt):
                uid[0] += 1
                return st.enter_context(nc.psum_tensor("%s_%d" % (name, uid[0]), shape, dt))
            return sb, ps

        gsb, gps = mk_alloc(gst)
        P = Prog(nc, gst)

        identf = gsb("identf", [128, 128], F32)
        identb = gsb("identb", [128, 128], BF16)
        trib = gsb("trib", [128, 128], BF16)
        negF = gsb("negF", [128, NT * H], F32)
        c_const = P.chan("c_const")
        with ExitStack() as st:
            sb, ps = mk_alloc(st)
            trif = sb("trif", [128, 128], F32)
            P.dma("sync", identf[:, :], ident_d, c_const)
            w = P.dma("sync", trif[:, :], tri_d, c_const)
            P.add("dve", lambda e: e.tensor_copy(out=identb[:, :], in_=identf[:, :]), waits=[w], sig=True)
            P.add("dve", lambda e: e.tensor_copy(out=trib[:, :], in_=trif[:, :]), sig=True)
            P.barrier()
            P.emit()

        c_w = P.chan("c_w")
        c_x = [P.chan("c_x0"), P.chan("c_x1")]
        c_g = [P.chan("c_g0"), P.chan("c_g1")]
        c_st = [P.chan("c_st%d" % i) for i in range(4)]
        c_k = [P.chan("c_k0"), P.chan("c_k1")]
        c_q = [P.chan("c_q0"), P.chan("c_q1")]
        c_misc = P.chan("c_misc")

        def phase1(li, typ, x_src):
            j_l = li // 2
            w_in = (w_in_a if typ == "A" else w_in_b)[j_l]
            isB = typ == "B"
            with ExitStack() as st1:
                sb1, ps1 = mk_alloc(st1)
                zfT = sb1("zfT", [H, S], F32) if isB else None
                with ExitStack() as st:
                    sb, ps = mk_alloc(st)
                    Wb = sb("Wb", [128, 8, 4 * D], BF16)
                    xin = [sb("xin%d" % i, [128, 4, D], F32) for i in range(2)]
                    xT = [sb("xT%d" % i, [128, 8, 512], BF16) for i in range(2)]
                    osb = [sb("osb%d" % i, [128, 512], BF16) for i in range(4)]
                    tp = [ps("tp%d" % i, [128, 512], F32) for i in range(2)]
                    mm = [ps("mm%d" % i, [128, 512], F32) for i in range(4)]
                    if isB:
                        xTf = sb("xTf", [128, 8, 512], F32)
                        wf = sb("wf", [128, 8, H], F32)
                        bfs = sb("bfs", [H, 1], F32)
                        zps = ps("zps", [H, 512], F32)
                    w_w = None
                    for c in range(8):
                        for hh in range(2):
                            w_w = P.dma("pool", Wb[:, c, hh * 2048:(hh + 1) * 2048],
                                        w_in[c * 128:(c + 1) * 128, hh * 2048:(hh + 1) * 2048], c_w)
                    w_wf = None
                    if isB and "wfld" not in KSKIP:
                        P.dma("sync", wf[:, :, :], w_f_b[j_l].rearrange("(c p) h -> p c h", p=128), c_misc)
                    if isB and "bfld" not in KSKIP:
                        w_wf = P.dma("sync", bfs[:, :], bfb_d[j_l], c_misc)

                    def load_x(b):
                        return P.dma("sync", xin[b % 2][:, :, :],
                                     x_src[b * 512:(b + 1) * 512, :].rearrange("(t p) d -> p t d", p=128),
                                     c_x[b % 2], waits=[w_tr_done.get(b - 2)])
                    w_tr_done = {}
                    w_x = {}
                    w_x[0] = load_x(0)
                    if NB > 1:
                        w_x[1] = load_x(1)
                    ev_tp = {}
                    tpu = 0
                    mmu = 0
                    ev_mm = {}
                    st_w = {}
                    stu = 0
                    w_zf_mm_prev = None
                    w_zf_ev_prev = None
                    for b in range(NB):
                        xi = xin[b % 2]
                        xt = xT[b % 2]
                        w_ev_last = None
                        w_evf_last = None
                        for c in range(8):
                            tb = tp[tpu % 2]
                            for t in range(4):
                                w_t = P.add("pe", lambda e, tb=tb, xi=xi, t=t, c=c: e.transpose(
                                    out=tb[:, t * 128:(t + 1) * 128], in_=xi[:, t, c * 128:(c + 1) * 128], identity=identf[:, :]),
                                    waits=[w_x[b]] + (ev_tp.get(tpu - 2) or []), sig=(t == 3))
                            if c == 7:
                                w_tr_done[b] = w_t
                            wl = []
                            w_ev_last = P.add("dve", lambda e, tb=tb, xt=xt, c=c: e.tensor_copy(out=xt[:, c, :], in_=tb[:, :]),
                                              waits=[w_t], sig=True)
                            wl.append(w_ev_last)
                            if isB and "xtf" not in KSKIP:
                                w_evf_last = P.add("act", lambda e, tb=tb, c=c: e.activation(out=xTf[:, c, :], in_=tb[:, :], func=AF.Identity),
                                                   waits=[w_t, w_zf_mm_prev, w_ev_last], sig=True)
                                wl.append(w_evf_last)
                            ev_tp[tpu] = wl
                            tpu += 1
                        if b + 2 < NB:
                            w_x[b + 2] = load_x(b + 2)
                        if isB and "zf" not in KSKIP:
                            for c in range(8):
                                w_zf_mm = P.add("pe", lambda e, c=c: e.matmul(zps[:, :], lhsT=wf[:, c, :], rhs=xTf[:, c, :],
                                                                             start=(c == 0), stop=(c == 7)),
                                                waits=[w_evf_last, w_wf, w_zf_ev_prev], sig=(c == 7))
                            w_zf_mm_prev = w_zf_mm
                            w_zf_ev_prev = P.add("dve", lambda e, b=b: e.tensor_scalar(
                                out=zfT[:, b * 512:(b + 1) * 512], in0=zps[:, :], scalar1=bfs[:, 0:1], scalar2=None, op0=ALU.add),
                                waits=[w_zf_mm], sig=True)
                        for j in range(32):
                            mb = mm[mmu % 4]
                            for c in range(8):
                                w_m = P.add("pe", lambda e, mb=mb, xt=xt, j=j, c=c: e.matmul(
                                    mb[:, :], lhsT=Wb[:, c, j * 128:(j + 1) * 128], rhs=xt[:, c, :], start=(c == 0), stop=(c == 7)),
                                    waits=[w_w, w_ev_last, ev_mm.get(mmu - 4)], sig=(c == 7))
                            ob = osb[stu % 4]
                            wst = st_w.get(stu - 4)
                            if j < 8:
                                w_e = P.add("act", lambda e, mb=mb, ob=ob: e.activation(out=ob[:, :], in_=mb[:, :], func=AF.Copy, scale=0.125),
                                            waits=[w_m, wst], sig=True)
                                dst = QT
                            elif j < 24:
                                w_e = P.add("dve", lambda e, mb=mb, ob=ob: e.tensor_copy(out=ob[:, :], in_=mb[:, :]),
                                            waits=[w_m, wst], sig=True)
                                dst = KT if j < 16 else VT
                            else:
                                w_e = P.add("act", lambda e, mb=mb, ob=ob: e.activation(out=ob[:, :], in_=mb[:, :], func=AF.Silu),
                                            waits=[w_m, wst], sig=True)
                                dst = ZT
                            ev_mm[mmu] = w_e
                            mmu += 1
                            jj = j % 8
                            st_w[stu] = P.dma("sync", dst[jj * 128:(jj + 1) * 128, b * 512:(b + 1) * 512], ob[:, :], c_st[stu % 4], waits=[w_e])
                            stu += 1
                    P.barrier()
                    P.emit()
                if isB and "post" not in KSKIP:
                    with ExitStack() as st:
                        sb, ps = mk_alloc(st)
                        Fs = sb("Fs", [H, S], F32)
                        Fb = sb("Fb", [H, S], BF16)
                        CH = min(2048, S)
                        ones = sb("ones", [H, CH], F32)
                        fps = ps("fps", [128, NT * H], F32)
                        w_o = P.add("dve", lambda e: e.memset(ones[:, :], 1.0), sig=True)
                        w1 = P.add("act", lambda e: e.activation(out=zfT[:, :], in_=zfT[:, :], func=AF.Exp, scale=-1.0), sig=True)
                        w2 = P.add("act", lambda e: e.activation(out=zfT[:, :], in_=zfT[:, :], func=AF.Ln, bias=1.0), waits=[w1], sig=True)
                        wprev = w2
                        for ci in range(S // CH):
                            init = 0.0 if ci == 0 else Fs[:, ci * CH - 1:ci * CH]
                            wprev = P.add("dve", lambda e, ci=ci, init=init: e.tensor_tensor_scan(
                                out=Fs[:, ci * CH:(ci + 1) * CH], data0=ones[:, :], data1=zfT[:, ci * CH:(ci + 1) * CH],
                                initial=init, op0=ALU.mult, op1=ALU.subtract), waits=[wprev, w_o], sig=True)
                        w_fb = P.add("dve", lambda e: e.tensor_copy(out=Fb[:, :], in_=Fs[:, :]), waits=[wprev], sig=True)
                        P.dma("sync", FT, Fb[:, :], c_misc, waits=[w_fb])
                        for kt in range(NT if "posttr" not in KSKIP else 0):
                            w_t = P.add("pe", lambda e, kt=kt: e.transpose(out=fps[:, kt * H:(kt + 1) * H], in_=Fs[:, kt * 128:(kt + 1) * 128],
                                                                         identity=identf[0:H, 0:H]), waits=[wprev], sig=(kt == NT - 1))
                        wlast = w_t = (None if "posttr" in KSKIP else w_t)
                        for c0 in range(0, NT * H if "posttr" not in KSKIP else 0, 512):
                            c1 = min(NT * H, c0 + 512)
                            wlast = P.add("dve", lambda e, c0=c0, c1=c1: e.tensor_scalar(out=negF[:, c0:c1], in0=fps[:, c0:c1], scalar1=-1.0,
                                                                                      scalar2=None, op0=ALU.mult), waits=[w_t, wlast], sig=True)
                        P.barrier()
                        P.emit()

        def phase2(li, typ):
            j_l = li // 2
            isB = typ == "B"
            with ExitStack() as st:
                sb, ps = mk_alloc(st)
                Ka = [sb("Ka%d" % i, [128, S], BF16) for i in range(2)]
                VTb = [sb("VTb%d" % i, [64, S], BF16) for i in range(2)]
                Vones = [sb("Vones%d" % i, [128, NT, 128], BF16) for i in range(2)]
                Qa = [sb("Qa%d" % i, [128, QB], BF16) for i in range(2)]
                Zb = [sb("Zb%d" % i, [64, QB], BF16) for i in range(2)]
                PW = 1024 if isB else 1280
                PT = [sb("PT%d" % i, [128, PW], BF16) for i in range(3)]
                rinv = sb("rinv", [128, QB], F32)
                t1 = sb("t1", [64, QB], F32)
                Gs = [sb("Gs%d" % i, [64, QB], BF16) for i in range(2)]
                PSW = 1024 if isB else 1536
                NSB = 3 if isB else 2
                if isB:
                    Sps = [ps("Sps%d" % i, [128, 1024], F32) for i in range(3)]
                    Ops = ps("Ops", [128, 1024], F32)
                    Osb = sb("Osb", [128, QB], F32)
                else:
                    Sps = [ps("Sps%d" % i, [128, 1536], F32) for i in range(2)]
                    OpsA = [ps("OpsA%d" % i, [128, 512], F32) for i in range(2)]
                    OsbA = [sb("OsbA%d" % i, [64, QB], F32) for i in range(2)]
                    rsA = [sb("rsA%d" % i, [64, QB], F32) for i in range(2)]
                    Mb = sb("Mb", [128, H, 640], BF16)
                    bstage = [sb("bstage%d" % i, [128, 640], F32) for i in range(2)]
                    maskf = sb("maskf", [128, 640], F32)

                w_init = None
                for i in range(2):
                    P.add("dve", lambda e, i=i: e.memset(Ka[i][64:65, :], 1.0), sig=True)
                    P.add("dve", lambda e, i=i: e.memset(Vones[i][:, :, 64:128], 1.0), sig=True)
                    if not isB:
                        P.add("dve", lambda e, i=i: e.memset(Qa[i][64:65, :], 0.0), sig=True)
                w_init = (P.ech["dve"], P.ech["dve"].count)
                w_mb = None
                if not isB:
                    cbs = sb("cbs", [128, H], F32)
                    P.dma("sync", cbs[:, :], cb_d[j_l], c_misc)
                    w_mk = P.dma("sync", maskf[:, :], maskA_d, c_misc)
                    wb_prev = {}
                    for h in range(H):
                        wl = P.dma("sync", bstage[h % 2][:, :], biasT_d[j_l, h], c_x[h % 2], waits=[wb_prev.get(h - 2)])
                        wb_prev[h] = P.add("dve", lambda e, h=h: e.scalar_tensor_tensor(out=Mb[:, h, :], in0=bstage[h % 2][:, :], scalar=cbs[:, h:h + 1],
                                                                                      in1=maskf[:, :], op0=ALU.subtract, op1=ALU.add),
                                           waits=[wl, w_mk], sig=True)
                    w_mb = wb_prev[H - 1]

                pe_head_done = {}
                w_kv = {}
                w_t1_prev = [None]
                w_g_prev = [None]
                w_t1_blk = {}

                def load_kv(h):
                    hb = h % 2
                    P.dma("sync", Ka[hb][0:64, :], KT[h * 64:(h + 1) * 64, :], c_k[hb], waits=[pe_head_done.get(h - 2), w_init])
                    w_kv[h] = P.dma("sync", VTb[hb][:, :], VT[h * 64:(h + 1) * 64, :], c_k[hb], waits=[pe_head_done.get(h - 2)])

                qstate = {"n": 0}
                w_qload = {}
                w_qfree = {}
                w_gst = {}

                def load_q(h, qi):
                    qn = h * NQB + qi
                    qb = qn % 2
                    q0 = qi * QB
                    wf_ = w_qfree.get(qn - 2)
                    P.dma("sync", Qa[qb][0:64, :], QT[h * 64:(h + 1) * 64, q0:q0 + QB], c_q[qb], waits=[wf_, w_init])
                    if isB:
                        P.dma("sync", Qa[qb][64:65, :], FT[h:h + 1, q0:q0 + QB], c_q[qb], waits=[wf_])
                    w_qload[qn] = P.dma("sync", Zb[qb][:, :], ZT[h * 64:(h + 1) * 64, q0:q0 + QB], c_q[qb], waits=[wf_])

                load_kv(0)
                upe = 0
                w_exp = {}
                w_pv = {}
                w_ops_free = None
                w_opsA = {}
                for h in range(H):
                    hb = h % 2
                    if h + 1 < H:
                        load_kv(h + 1)
                    vviews = [Sps[upe % NSB][:, :].bitcast(BF16), Sps[(upe + 1) % NSB][:, :].bitcast(BF16)]
                    G0 = (PSW * 2) // 64
                    assert NT <= 2 * G0
                    for kt in range(NT):
                        vv = vviews[kt // G0]
                        ko = kt % G0
                        w_t = P.add("pe", lambda e, kt=kt, vv=vv, ko=ko, hb=hb: e.transpose(
                            out=vv[:, ko * 64:(ko + 1) * 64],
                            in_=VTb[hb][0:64, kt * 128:(kt + 1) * 128], identity=identb[0:64, 0:64]),
                            waits=[w_kv[h], w_exp.get(upe - 3), w_exp.get(upe - 2), w_exp.get(upe - 1), pe_head_done.get(h - 2)], sig=(kt == NT - 1))
                    w_vlast = None
                    for g in range(0, NT, 16):
                        n = min(16, NT - g)
                        vv = vviews[g // G0]
                        go = g % G0
                        w_vlast = P.add("dve", lambda e, g=g, n=n, vv=vv, go=go, hb=hb: e.tensor_copy(
                            out=Vones[hb][:, g:g + n, 0:64],
                            in_=vv[:, go * 64:(go + n) * 64].rearrange("p (k d) -> p k d", d=64)),
                            waits=[w_t, w_init], sig=True)

                    units = []
                    if isB:
                        for qi in range(NQB):
                            nsub = QB // 128
                            nk = (qi + 1) * nsub
                            for kt in range(nk):
                                r = kt - qi * nsub
                                c0 = 128 * r if r > 0 else 0
                                u = dict(qi=qi, kt=kt, c0=c0, diag=(r >= 0), first=(kt == 0), last=(kt == nk - 1))
                                units.append(u)
                    else:
                        for m2 in range(NT // 2):
                            m0 = 2 * m2
                            qi = (m0 * 128) // QB
                            qoff = m0 * 128 - qi * QB
                            rows = []
                            for (dk, sc0, ncl, qlo, jlo, bseg) in ((-3, 256, 256, 0, 3, (128, 128)), (-2, 512, 256, 0, 2, None),
                                                                  (-1, 768, 256, 0, 1, (0, 128)), (0, 1024, 256, 0, 0, (0, 256)),
                                                                  (-4, 0, 128, 0, 4, (0, 128)), (1, 128, 128, 128, 0, (0, 128))):
                                kt = m0 + dk
                                if kt < 0:
                                    continue
                                rows.append((kt, sc0, ncl, qlo, jlo, bseg))
                            units.append(dict(qi=qi, qoff=qoff, rows=rows, m2=m2, last=((m0 + 2) * 128) % QB == 0))

                    def qn_of(u):
                        return h * NQB + u["qi"]

                    def emit_qk(ui, u):
                        g = upe + ui
                        sbuf_ = Sps[g % NSB]
                        qn = qn_of(u)
                        qa = Qa[qn % 2]
                        wts = [w_qload[qn], w_vlast, w_exp.get(g - NSB), w_init, w_mb]
                        if isB:
                            kt, c0 = u["kt"], u["c0"]
                            for lo in (0, 512):
                                hi = lo + 512
                                a = max(lo, c0)
                                if a >= hi or a >= QB:
                                    continue
                                hi = min(hi, QB)
                                dg = u["diag"] and (lo <= c0 < hi)
                                wq = P.add("pe", lambda e, sbuf_=sbuf_, kt=kt, a=a, hi=hi, qa=qa, dg=dg, hb=hb: e.matmul(
                                    sbuf_[:, a:hi], lhsT=Ka[hb][0:65, kt * 128:(kt + 1) * 128], rhs=qa[0:65, a:hi], start=True, stop=not dg),
                                    waits=wts, sig=True)
                                if dg:
                                    wq = P.add("pe", lambda e, sbuf_=sbuf_, c0=c0: e.matmul(
                                        sbuf_[:, c0:c0 + 128], lhsT=identb[:, :], rhs=trib[:, :], start=False, stop=True), sig=True)
                            return wq
                        else:
                            wq = None
                            nr = len(u["rows"])
                            for ri, (kt, sc0, ncl, qlo, jlo, bseg) in enumerate(u["rows"]):
                                qc = u["qoff"] + qlo
                                lastrow = ri == nr - 1
                                wq = P.add("pe", lambda e, sbuf_=sbuf_, kt=kt, sc0=sc0, ncl=ncl, qc=qc, qa=qa, hb=hb, bseg=bseg: e.matmul(
                                    sbuf_[:, sc0:sc0 + ncl], lhsT=Ka[hb][0:65, kt * 128:(kt + 1) * 128], rhs=qa[0:65, qc:qc + ncl],
                                    start=True, stop=(bseg is None)), waits=wts, sig=(lastrow and bseg is None))
                                if bseg is not None:
                                    bo, bn = bseg
                                    wq = P.add("pe", lambda e, sbuf_=sbuf_, sc0=sc0, bo=bo, bn=bn, jlo=jlo, h=h: e.matmul(
                                        sbuf_[:, sc0 + bo:sc0 + bo + bn], lhsT=identb[:, :], rhs=Mb[:, h, jlo * 128 + bo: jlo * 128 + bo + bn],
                                        start=False, stop=True), sig=lastrow)
                            return wq

                    def emit_exp(ui, u, wq):
                        g = upe + ui
                        sbuf_ = Sps[g % NSB]
                        pt = PT[g % 3]
                        wts = [wq, w_pv.get(g - 3)]
                        if isB:
                            kt, c0 = u["kt"], u["c0"]
                            w_exp[g] = P.add("act", lambda e, sbuf_=sbuf_, pt=pt, kt=kt, c0=c0, h=h: e.activation(
                                out=pt[:, c0:QB], in_=sbuf_[:, c0:QB], func=AF.Exp, bias=negF[:, kt * H + h: kt * H + h + 1], scale=1.0),
                                waits=wts, sig=True)
                        else:
                            segs = sorted((sc0, sc0 + ncl) for (kt, sc0, ncl, qlo, jlo, bseg) in u["rows"])
                            merged = []
                            for a, b_ in segs:
                                if merged and merged[-1][1] == a:
                                    merged[-1][1] = b_
                                else:
                                    merged.append([a, b_])
                            for a, b_ in merged:
                                w_exp[g] = P.add("act", lambda e, sbuf_=sbuf_, pt=pt, a=a, b_=b_: e.activation(
                                    out=pt[:, a:b_], in_=sbuf_[:, a:b_], func=AF.Exp), waits=wts, sig=True)

                    def emit_pv(ui, u):
                        nonlocal w_ops_free
                        g = upe + ui
                        pt = PT[g % 3]
                        qn = qn_of(u)
                        if isB:
                            kt, c0 = u["kt"], u["c0"]
                            wp = None
                            for lo in (0, 512):
                                hi = min(lo + 512, QB)
                                a = max(lo, c0)
                                if a >= hi:
                                    continue
                                nsub_ = QB // 128
                                last_kt = u["qi"] * nsub_ + min(nsub_, (lo // 512 + 1) * 4) - 1
                                wp = P.add("pe", lambda e, pt=pt, kt=kt, a=a, hi=hi, u=u, hb=hb, last_kt=last_kt: e.matmul(
                                    Ops[:, a:hi], lhsT=Vones[hb][:, kt, :], rhs=pt[:, a:hi], start=u["first"], stop=(kt == last_kt)),
                                    waits=[w_exp[g], w_ops_free if u["first"] else None], sig=True)
                            w_pv[g] = wp
                            if u["last"]:
                                pending.append(lambda u=u, qn=qn, wp=wp: finish_block_B(u, qn, wp))
                        else:
                            ob = u["m2"] % 2
                            rows = u["rows"]
                            wp = None
                            for ri, (kt, sc0, ncl, qlo, jlo, bseg) in enumerate(rows):
                                wp = P.add("pe", lambda e, pt=pt, kt=kt, sc0=sc0, ncl=ncl, qlo=qlo, ob=ob, ri=ri, nr=len(rows), hb=hb: e.matmul(
                                    OpsA[ob][:, qlo: qlo + ncl], lhsT=Vones[hb][:, kt, :], rhs=pt[:, sc0:sc0 + ncl],
                                    start=(ri == 0), stop=(ri == nr - 1)),
                                    waits=[w_exp[g], w_opsA.get(u["m2"] + h * (NT // 2) - 2) if ri == 0 else None], sig=True)
                            w_pv[g] = wp
                            pending.append(lambda u=u, qn=qn, wp=wp, ob=ob: finish_unit_A(u, qn, wp, ob))

                    def finish_block_B(u, qn, wp):
                        nonlocal w_ops_free
                        qb = qn % 2
                        q0 = u["qi"] * QB
                        wcp = P.add("dve", lambda e: e.tensor_copy(out=Osb[:, :], in_=Ops[:, :]), waits=[wp, w_t1_prev[0]], sig=True)
                        w_ops_free = wcp
                        wr = P.add("dve", lambda e: e.reciprocal(out=rinv[0:64, :], in_=Osb[64:128, :]), waits=[wcp], sig=True)
                        wt1 = P.add("dve", lambda e: e.tensor_tensor(out=t1[0:64, :], in0=Osb[0:64, :], in1=rinv[0:64, :], op=ALU.mult),
                                    waits=[wr, w_g_prev[0]], sig=True)
                        w_t1_prev[0] = wt1
                        wg = P.add("dve", lambda e, qb=qb: e.tensor_tensor(out=Gs[qb][:, :], in0=t1[0:64, :], in1=Zb[qb][:, :], op=ALU.mult),
                                   waits=[wt1, w_qload[qn], w_gst.get(qn - 2)], sig=True)
                        w_g_prev[0] = wg
                        w_qfree[qn] = wg
                        w_gst[qn] = P.dma("sync", GT[h * 64:(h + 1) * 64, q0:q0 + QB], Gs[qb][:, :], c_g[qb], waits=[wg])

                    def finish_unit_A(u, qn, wp, ob):
                        qb = qn % 2
                        q0 = u["qi"] * QB
                        qoff = u["qoff"]
                        gi = u["m2"] + h * (NT // 2)
                        wc1 = P.add("dve", lambda e, ob=ob, qb=qb, qoff=qoff: e.tensor_copy(out=OsbA[qb][0:64, qoff:qoff + 256], in_=OpsA[ob][0:64, 0:256]),
                                    waits=[wp, w_t1_blk.get(qn - 2)], sig=True)
                        wc2 = P.add("dve", lambda e, ob=ob, qb=qb, qoff=qoff: e.tensor_copy(out=rsA[qb][0:64, qoff:qoff + 256], in_=OpsA[ob][64:128, 0:256]),
                                    waits=[wc1], sig=True)
                        w_opsA[gi] = wc2
                        if u["last"]:
                            wr0 = P.add("act", lambda e, qb=qb: e.activation(out=rsA[qb][0:64, :], in_=rsA[qb][0:64, :], func=AF.Ln), waits=[wc2], sig=True)
                            wr = P.add("act", lambda e, qb=qb: e.activation(out=rsA[qb][0:64, :], in_=rsA[qb][0:64, :], func=AF.Exp, scale=-1.0),
                                       waits=[wr0], sig=True)
                            wt1 = P.add("dve", lambda e, qb=qb: e.tensor_tensor(out=t1[0:64, :], in0=OsbA[qb][0:64, :], in1=rsA[qb][0:64, :], op=ALU.mult),
                                        waits=[wr, w_g_prev[0]], sig=True)
                            w_t1_blk[qn] = wt1
                            wg = P.add("pool", lambda e, qb=qb: e.tensor_tensor(out=Gs[qb][:, :], in0=t1[0:64, :], in1=Zb[qb][:, :], op=ALU.mult),
                                       waits=[wt1, w_qload[qn], w_gst.get(qn - 2)], sig=True)
                            w_g_prev[0] = wg
                            w_qfree[qn] = wg
                            w_gst[qn] = P.dma("sync", GT[h * 64:(h + 1) * 64, q0:q0 + QB], Gs[qb][:, :], c_g[qb], waits=[wg])

                    nU = len(units)
                    wq_list = {}

                    def ensure_q(ui):
                        u = units[ui]
                        qn = h * NQB + u["qi"]
                        for qq in (qn, qn + 1):
                            if qq >= H * NQB or qq in loaded_all:
                                continue
                            if qq - 2 >= 0 and (qq - 2) not in w_qfree:
                                assert qq != qn, "q block needed but predecessor not finished"
                                continue
                            load_q(qq // NQB, qq % NQB)
                            loaded_all.add(qq)

                    for ui in range(min(NSB, nU)):
                        ensure_q(ui)
                        wq_list[ui] = emit_qk(ui, units[ui])
                        emit_exp(ui, units[ui], wq_list[ui])
                    pending = []
                    for ui in range(nU):
                        emit_pv(ui, units[ui])
                        if ui + NSB < nU:
                            ensure_q(ui + NSB)
                            wq_list[ui + NSB] = emit_qk(ui + NSB, units[ui + NSB])
                            emit_exp(ui + NSB, units[ui + NSB], wq_list[ui + NSB])
                        for f_ in pending:
                            f_()
                        pending.clear()
                    upe += nU
                    pe_head_done[h] = w_pv[upe - 1]
                P.barrier()
                P.emit()

        loaded_all = set()

        def phase3(li, typ, x_src, x_dst):
            j_l = li // 2
            w_out = (w_out_a if typ == "A" else w_out_b)[j_l]
            with ExitStack() as st:
                sb, ps = mk_alloc(st)
                NU = 5
                Wo = sb("Wo", [128, 8, D], BF16)
                GTb = [sb("GTb%d" % i, [128, 8, 512], BF16) for i in range(2)]
                xres = [sb("xres%d" % i, [128, 4, D], F32) for i in range(2)]
                ub = [sb("ub%d" % i, [128, D], F32) for i in range(NU)]
                xo = [sb("xo%d" % i, [128, 4, D], F32) for i in range(2)]
                gsb_ = sb("lng_s", [128, D], F32)
                bsb_ = sb("lnb_s", [128, D], F32)
                stt = [sb("stt%d" % i, [128, 2, 6], F32) for i in range(NU)]
                mv = [sb("mv%d" % i, [128, 2], F32) for i in range(NU)]
                sm = [sb("sm%d" % i, [128, 4], F32) for i in range(NU)]
                yps = [ps("yps%d" % i, [128, D], F32) for i in range(2)]
                w_w = None
                for c in range(8):
                    w_w = P.dma("pool", Wo[:, c, :], w_out[c * 128:(c + 1) * 128, :], c_w)
                P.dma("sync", gsb_[:, :], lng_d[li], c_misc)
                w_gb = P.dma("sync", bsb_[:, :], lnb_d[li], c_misc)
                w_ld = {}
                w_blk_pe = {}
                w_blk_res = {}
                w_sto = {}

                def load(b):
                    P.dma("sync", GTb[b % 2][:, :, :], GT[:, b * 512:(b + 1) * 512].rearrange("(c p) s -> p c s", p=128), c_x[b % 2],
                          waits=[w_blk_pe.get(b - 2)])
                    w_ld[b] = P.dma("sync", xres[b % 2][:, :, :], x_src[b * 512:(b + 1) * 512, :].rearrange("(t p) d -> p t d", p=128),
                                    c_x[b % 2], waits=[w_blk_res.get(b - 2)])
                w_u_free = {}
                w_y_free = {}
                w_stA = {}

                def stageA(tu):
                    b, t = divmod(tu, 4)
                    gt = GTb[b % 2]
                    xr = xres[b % 2]
                    yp = yps[tu % 2]
                    k = tu % NU
                    u_ = ub[k]
                    for half in range(2):
                        for c in range(8):
                            w_m = P.add("pe", lambda e, yp=yp, gt=gt, t=t, half=half, c=c: e.matmul(
                                yp[:, half * 512:(half + 1) * 512], lhsT=gt[:, c, t * 128:(t + 1) * 128],
                                rhs=Wo[:, c, half * 512:(half + 1) * 512], start=(c == 0), stop=(c == 7)),
                                waits=[w_w, w_ld[b], w_y_free.get(tu - 2)], sig=(c == 7 and half == 1))
                    if t == 3:
                        w_blk_pe[b] = w_m
                    w_r = P.add("dve", lambda e, u_=u_, xr=xr, t=t, yp=yp: e.scalar_tensor_tensor(
                        out=u_[:, :], in0=xr[:, t, :], scalar=ALPHA, in1=yp[:, :], op0=ALU.mult, op1=ALU.add),
                        waits=[w_m, w_ld[b], w_u_free.get(tu - NU)], sig=True)
                    w_y_free[tu] = w_r
                    if t == 3:
                        w_blk_res[b] = w_r
                    P.add("dve", lambda e, u_=u_, k=k: e.bn_stats(out=stt[k][:, 0, :], in_=u_[:, 0:512]), waits=[w_r], sig=True)
                    w_s = P.add("dve", lambda e, u_=u_, k=k: e.bn_stats(out=stt[k][:, 1, :], in_=u_[:, 512:1024]), sig=True)
                    w_a = P.add("dve", lambda e, k=k: e.bn_aggr(out=mv[k][:, :], in_=stt[k][:, :, :].rearrange("p a b -> p (a b)")), waits=[w_s], sig=True)
                    w_1 = P.add("dve", lambda e, k=k: e.tensor_scalar(out=sm[k][:, 0:1], in0=mv[k][:, 1:2], scalar1=EPS, scalar2=None, op0=ALU.add),
                                waits=[w_a], sig=True)
                    w_2 = P.add("act", lambda e, k=k: e.activation(out=sm[k][:, 1:2], in_=sm[k][:, 0:1], func=AF.Ln), waits=[w_1], sig=True)
                    w_3 = P.add("act", lambda e, k=k: e.activation(out=sm[k][:, 2:3], in_=sm[k][:, 1:2], func=AF.Exp, scale=-0.5), waits=[w_2], sig=True)
                    w_stA[tu] = w_3

                def stageB(tu):
                    b, t = divmod(tu, 4)
                    k = tu % NU
                    u_ = ub[k]
                    xob = xo[b % 2]
                    w_4 = P.add("dve", lambda e, k=k: e.scalar_tensor_tensor(out=sm[k][:, 3:4], in0=mv[k][:, 0:1], scalar=-1.0, in1=sm[k][:, 2:3],
                                                                          op0=ALU.mult, op1=ALU.mult), waits=[w_stA[tu]], sig=True)
                    w_n = P.add("act", lambda e, u_=u_, k=k: e.activation(out=u_[:, :], in_=u_[:, :], func=AF.Identity,
                                                                         bias=sm[k][:, 3:4], scale=sm[k][:, 2:3]), waits=[w_4], sig=True)
                    w_g = P.add("dve", lambda e, u_=u_: e.tensor_tensor(out=u_[:, :], in0=u_[:, :], in1=gsb_[:, :], op=ALU.mult),
                                waits=[w_n, w_gb], sig=True)
                    w_b = P.add("pool", lambda e, u_=u_, xob=xob, t=t: e.tensor_tensor(out=xob[:, t, :], in0=u_[:, :], in1=bsb_[:, :], op=ALU.add),
                                waits=[w_g, w_sto.get(b - 2)], sig=True)
                    w_u_free[tu] = w_b
                    if t == 3:
                        w_sto[b] = P.dma("sync", x_dst[b * 512:(b + 1) * 512, :].rearrange("(t p) d -> p t d", p=128), xob[:, :, :],
                                         c_g[b % 2], waits=[w_b])
                        if b + 2 < NB:
                            load(b + 2)

                load(0)
                if NB > 1:
                    load(1)
                NTI = NB * 4
                LAG = 2
                for tu in range(NTI + LAG):
                    if tu < NTI:
                        stageA(tu)
                    if tu - LAG >= 0:
                        stageB(tu - LAG)
                P.barrier()
                P.emit()

        import os
        kstop = int(os.environ.get("KSTOP", "99"))
        x_src = x_d
        for li, typ in enumerate(ltypes):
            last = li == NL - 1
            if kstop >= 1:
                phase1(li, typ, x_src)
            loaded_all.clear()
            if kstop >= 2:
                phase2(li, typ)
            if kstop >= 3:
                phase3(li, typ, x_src, y_d if last else xbuf)
            x_src = xbuf
        print("ops recorded:", P.nops)
    return nc


def host_consts():
    ident = np.eye(128, dtype=np.float32)
    k = np.arange(128)[:, None]
    q = np.arange(128)[None, :]
    tri = np.where(k > q, NEG, 0.0).astype(np.float32)
    maskA = np.zeros((128, 5, 128), np.float32)
    maskA[64:, 0, :64] = NEG
    maskA[:64, 4, 64:] = NEG
    return ident, tri, maskA.reshape(128, 640)


def host_bias_layout(rel_bias_a):
    k = np.arange(128)[:, None, None]
    j = np.arange(5)[None, :, None]
    q = np.arange(128)[None, None, :]
    idx = np.clip(128 * j + q - k, -128, 128) + 128
    out = rel_bias_a[:, :, idx]
    return np.ascontiguousarray(out.reshape(rel_bias_a.shape[0], H, 128, 640)).astype(np.float32)


def make_common(w_in_a, rel_bias_a, w_out_a, w_in_b, w_f_b, b_f_b, w_out_b, ln_g, ln_b):
    ident, tri, maskA = host_consts()
    f = lambda a: np.ascontiguousarray(np.asarray(a, dtype=np.float32))
    return {
        "w_in_a": f(w_in_a), "w_in_b": f(w_in_b), "w_out_a": f(w_out_a), "w_out_b": f(w_out_b),
        "w_f_b": f(w_f_b), "bfb": f(np.asarray(b_f_b)[:, :, None]),
        "biasT": host_bias_layout(np.asarray(rel_bias_a, dtype=np.float32)),
        "lng": f(np.broadcast_to(np.asarray(ln_g)[:, None, :], (4, 128, D))),
        "lnb": f(np.broadcast_to(np.asarray(ln_b)[:, None, :], (4, 128, D))),
        "ident": ident, "tri": tri, "maskA": maskA,
        "cbias": f(np.broadcast_to(np.asarray(rel_bias_a, dtype=np.float32)[:, None, :, 256], (2, 128, H))),
    }


def kernel(x, w_in_a, rel_bias_a, w_out_a, w_in_b, w_f_b, b_f_b, w_out_b, ln_g, ln_b):
    x = np.asarray(x, dtype=np.float32)
    common = make_common(w_in_a, rel_bias_a, w_out_a, w_in_b, w_f_b, b_f_b, w_out_b, ln_g, ln_b)
    nc = build(SEQ, "ABAB")
    in_maps = []
    for c in range(NCORES):
        m = dict(common)
        m["x"] = np.ascontiguousarray(x[c])
        in_maps.append(m)
    res = run_bass_kernel_spmd(nc, in_maps, core_ids=list(range(NCORES)))
    return np.stack([np.asarray(r["y"], dtype=np.float32) for r in res.results], axis=0)
```

```python
import numpy as np
from contextlib import ExitStack
import concourse.bass as bass
import concourse.mybir as mybir
from concourse.bass_utils import run_bass_kernel_spmd

F32 = mybir.dt.float32
BF16 = mybir.dt.bfloat16
AF = mybir.ActivationFunctionType
ALU = mybir.AluOpType

D = 1024
H = 16
NEG = -30000.0
ALPHA = 8.0 ** 0.25
EPS = 1e-5
SEQ = 8192
NCORES = 8
import os
KSKIP = os.environ.get("KSKIP", "").split(",")


class Chan:
    def __init__(self, sem, name):
        self.sem = sem
        self.count = 0
        self.name = name


class Prog:
    ENG = ("sync", "act", "pe", "dve", "pool")

    def __init__(self, nc, stack):
        self.nc = nc
        self.stack = stack
        self.q = {e: [] for e in self.ENG}
        self.waited = {}
        self.chans = []
        self.ech = {e: self.chan("e_" + e) for e in ("act", "pe", "dve", "pool")}
        self.nops = 0

    def chan(self, name):
        sem = self.stack.enter_context(self.nc.semaphore(name))
        c = Chan(sem, name)
        self.chans.append(c)
        return c

    def add(self, eng, fn, waits=(), sig=None, inc=1):
        ws = []
        for w in waits:
            if w is None:
                continue
            ch, v = w
            if v <= 0:
                continue
            key = (eng, ch.name)
            if self.waited.get(key, 0) >= v:
                continue
            self.waited[key] = v
            ws.append((ch, v))
        ch = None
        if sig is True:
            ch = self.ech[eng]
        elif sig:
            ch = sig
        ret = None
        if ch is not None:
            ch.count += inc
            ret = (ch, ch.count)
        self.q[eng].append((fn, ws, ch, inc))
        self.nops += 1
        return ret

    def dma(self, eng, out, in_, ch, waits=()):
        return self.add(eng, lambda e: e.dma_start(out=out, in_=in_), waits=waits, sig=ch, inc=16)

    def barrier(self):
        ws = [(c, c.count) for c in self.chans if c.count > 0]
        for e in self.ENG:
            self.add(e, None, waits=ws)

    def emit(self):
        nc = self.nc
        q = self.q
        self.q = {e: [] for e in self.ENG}
        with nc.Block() as block:
            def mk(name):
                def run(eng):
                    for fn, ws, ch, inc in q[name]:
                        for c, v in ws:
                            eng.wait_ge(c.sem, v)
                        if fn is None:
                            continue
                        ins = fn(eng)
                        if ch is not None:
                            ins.then_inc(ch.sem, inc)
                return run
            block.sync(mk("sync"))
            block.scalar(mk("act"))
            block.tensor(mk("pe"))
            block.vector(mk("dve"))
            block.gpsimd(mk("pool"))


def build(S, ltypes, dbg=None):
    nc = bass.Bass("TRN2", target_bir_lowering=False)
    NL = len(ltypes)
    NT = S // 128
    NB = S // 512
    QB = min(1024, S)
    NQB = S // QB

    def din(name, shape, dt=F32):
        return nc.dram_tensor(name, shape, dt, kind="ExternalInput").ap()

    x_d = din("x", [S, D])
    w_in_a = din("w_in_a", [2, D, 4 * D])
    w_in_b = din("w_in_b", [2, D, 4 * D])
    w_out_a = din("w_out_a", [2, D, D])
    w_out_b = din("w_out_b", [2, D, D])
    w_f_b = din("w_f_b", [2, D, H])
    bfb_d = din("bfb", [2, H, 1])
    biasT_d = din("biasT", [2, H, 128, 640])
    lng_d = din("lng", [4, 128, D])
    lnb_d = din("lnb", [4, 128, D])
    ident_d = din("ident", [128, 128])
    tri_d = din("tri", [128, 128])
    maskA_d = din("maskA", [128, 640])
    cb_d = din("cbias", [2, 128, H])
    y_d = nc.dram_tensor("y", [S, D], F32, kind="ExternalOutput").ap()

    xbuf = nc.dram_tensor("xbuf", [S, D], F32).ap()
    QT = nc.dram_tensor("QT", [D, S], BF16).ap()
    KT = nc.dram_tensor("KT", [D, S], BF16).ap()
    VT = nc.dram_tensor("VT", [D, S], BF16).ap()
    ZT = nc.dram_tensor("ZT", [D, S], BF16).ap()
    GT = nc.dram_tensor("GT", [D, S], BF16).ap()
    FT = nc.dram_tensor("FT", [H, S], BF16).ap()

    with ExitStack() as gst:
        uid = [0]

        def mk_alloc(st):
            def sb(name, shape, dt):
                uid[0] += 1
                return st.enter_context(nc.sbuf_tensor("%s_%d" % (name, uid[0]), shape, dt))

            def ps(name, shape, dt):
                uid[0] += 1
                return st.enter_context(nc.psum_tensor("%s_%d" % (name, uid[0]), shape, dt))
            return sb, ps

        gsb, gps = mk_alloc(gst)
        P = Prog(nc, gst)

        identf = gsb("identf", [128, 128], F32)
        identb = gsb("identb", [128, 128], BF16)
        trib = gsb("trib", [128, 128], BF16)
        negF = gsb("negF", [128, NT * H], F32)
        c_const = P.chan("c_const")
        with ExitStack() as st:
            sb, ps = mk_alloc(st)
            trif = sb("trif", [128, 128], F32)
            P.dma("sync", identf[:, :], ident_d, c_const)
            w = P.dma("sync", trif[:, :], tri_d, c_const)
            P.add("dve", lambda e: e.tensor_copy(out=identb[:, :], in_=identf[:, :]), waits=[w], sig=True)
            P.add("dve", lambda e: e.tensor_copy(out=trib[:, :], in_=trif[:, :]), sig=True)
            P.barrier()
            P.emit()

        c_w = P.chan("c_w")
        c_x = [P.chan("c_x0"), P.chan("c_x1")]
        c_g = [P.chan("c_g0"), P.chan("c_g1")]
        c_st = [P.chan("c_st%d" % i) for i in range(4)]
        c_k = [P.chan("c_k0"), P.chan("c_k1")]
        c_q3 = [P.chan("c_q3_%d" % i) for i in range(3)]
        c_g3 = [P.chan("c_g3_%d" % i) for i in range(3)]
        c_misc = P.chan("c_misc")

        def phase1(li, typ, x_src):
            j_l = li // 2
            w_in = (w_in_a if typ == "A" else w_in_b)[j_l]
            isB = typ == "B"
            with ExitStack() as st1:
                sb1, ps1 = mk_alloc(st1)
                zfT = sb1("zfT", [H, S], F32) if isB else None
                with ExitStack() as st:
                    sb, ps = mk_alloc(st)
                    Wb = sb("Wb", [128, 8, 4 * D], BF16)
                    xin = [sb("xin%d" % i, [128, 4, D], F32) for i in range(2)]
                    xT = [sb("xT%d" % i, [128, 8, 512], BF16) for i in range(2)]
                    osb = [sb("osb%d" % i, [128, 512], BF16) for i in range(4)]
                    tp = [ps("tp%d" % i, [128, 512], F32) for i in range(2)]
                    mm = [ps("mm%d" % i, [128, 512], F32) for i in range(4)]
                    if isB:
                        xTf = sb("xTf", [128, 8, 512], F32)
                        wf = sb("wf", [128, 8, H], F32)
                        bfs = sb("bfs", [H, 1], F32)
                        zps = ps("zps", [H, 512], F32)
                    w_w = None
                    for c in range(8):
                        for hh in range(2):
                            w_w = P.dma("pool", Wb[:, c, hh * 2048:(hh + 1) * 2048],
                                        w_in[c * 128:(c + 1) * 128, hh * 2048:(hh + 1) * 2048], c_w)
                    w_wf = None
                    if isB and "wfld" not in KSKIP:
                        P.dma("sync", wf[:, :, :], w_f_b[j_l].rearrange("(c p) h -> p c h", p=128), c_misc)
                    if isB and "bfld" not in KSKIP:
                        w_wf = P.dma("sync", bfs[:, :], bfb_d[j_l], c_misc)

                    def load_x(b):
                        return P.dma("sync", xin[b % 2][:, :, :],
                                     x_src[b * 512:(b + 1) * 512, :].rearrange("(t p) d -> p t d", p=128),
                                     c_x[b % 2], waits=[w_tr_done.get(b - 2)])
                    w_tr_done = {}
                    w_x = {}
                    w_x[0] = load_x(0)
                    if NB > 1:
                        w_x[1] = load_x(1)
                    ev_tp = {}
                    tpu = 0
                    mmu = 0
                    ev_mm = {}
                    st_w = {}
                    stu = 0
                    w_zf_mm_prev = None
                    w_zf_ev_prev = None
                    for b in range(NB):
                        xi = xin[b % 2]
                        xt = xT[b % 2]
                        w_ev_last = None
                        w_evf_last = None
                        for c in range(8):
                            tb = tp[tpu % 2]
                            for t in range(4):
                                w_t = P.add("pe", lambda e, tb=tb, xi=xi, t=t, c=c: e.transpose(
                                    out=tb[:, t * 128:(t + 1) * 128], in_=xi[:, t, c * 128:(c + 1) * 128], identity=identf[:, :]),
                                    waits=[w_x[b]] + (ev_tp.get(tpu - 2) or []), sig=(t == 3))
                            if c == 7:
                                w_tr_done[b] = w_t
                            wl = []
                            w_ev_last = P.add("dve", lambda e, tb=tb, xt=xt, c=c: e.tensor_copy(out=xt[:, c, :], in_=tb[:, :]),
                                              waits=[w_t], sig=True)
                            wl.append(w_ev_last)
                            if isB and "xtf" not in KSKIP:
                                w_evf_last = P.add("act", lambda e, tb=tb, c=c: e.activation(out=xTf[:, c, :], in_=tb[:, :], func=AF.Identity),
                                                   waits=[w_t, w_zf_mm_prev, w_ev_last], sig=True)
                                wl.append(w_evf_last)
                            ev_tp[tpu] = wl
                            tpu += 1
                        if b + 2 < NB:
                            w_x[b + 2] = load_x(b + 2)
                        if isB and "zf" not in KSKIP:
                            for c in range(8):
                                w_zf_mm = P.add("pe", lambda e, c=c: e.matmul(zps[:, :], lhsT=wf[:, c, :], rhs=xTf[:, c, :],
                                                                             start=(c == 0), stop=(c == 7)),
                                                waits=[w_evf_last, w_wf, w_zf_ev_prev], sig=(c == 7))
                            w_zf_mm_prev = w_zf_mm
                            w_zf_ev_prev = P.add("dve", lambda e, b=b: e.tensor_scalar(
                                out=zfT[:, b * 512:(b + 1) * 512], in0=zps[:, :], scalar1=bfs[:, 0:1], scalar2=None, op0=ALU.add),
                                waits=[w_zf_mm], sig=True)
                        for j in range(32):
                            mb = mm[mmu % 4]
                            for c in range(8):
                                w_m = P.add("pe", lambda e, mb=mb, xt=xt, j=j, c=c: e.matmul(
                                    mb[:, :], lhsT=Wb[:, c, j * 128:(j + 1) * 128], rhs=xt[:, c, :], start=(c == 0), stop=(c == 7)),
                                    waits=[w_w, w_ev_last, ev_mm.get(mmu - 4)], sig=(c == 7))
                            ob = osb[stu % 4]
                            wst = st_w.get(stu - 4)
                            if j < 8:
                                w_e = P.add("act", lambda e, mb=mb, ob=ob: e.activation(out=ob[:, :], in_=mb[:, :], func=AF.Copy, scale=0.125),
                                            waits=[w_m, wst], sig=True)
                                dst = QT
                            elif j < 24:
                                w_e = P.add("dve", lambda e, mb=mb, ob=ob: e.tensor_copy(out=ob[:, :], in_=mb[:, :]),
                                            waits=[w_m, wst], sig=True)
                                dst = KT if j < 16 else VT
                            else:
                                w_e = P.add("act", lambda e, mb=mb, ob=ob: e.activation(out=ob[:, :], in_=mb[:, :], func=AF.Silu),
                                            waits=[w_m, wst], sig=True)
                                dst = ZT
                            ev_mm[mmu] = w_e
                            mmu += 1
                            jj = j % 8
                            st_w[stu] = P.dma("sync", dst[jj * 128:(jj + 1) * 128, b * 512:(b + 1) * 512], ob[:, :], c_st[stu % 4], waits=[w_e])
                            stu += 1
                    P.barrier()
                    P.emit()
                if isB and "post" not in KSKIP:
                    with ExitStack() as st:
                        sb, ps = mk_alloc(st)
                        Fs = sb("Fs", [H, S], F32)
                        Fb = sb("Fb", [H, S], BF16)
                        CH = min(2048, S)
                        ones = sb("ones", [H, CH], F32)
                        fps = ps("fps", [128, NT * H], F32)
                        w_o = P.add("dve", lambda e: e.memset(ones[:, :], 1.0), sig=True)
                        w1 = P.add("act", lambda e: e.activation(out=zfT[:, :], in_=zfT[:, :], func=AF.Exp, scale=-1.0), sig=True)
                        w2 = P.add("act", lambda e: e.activation(out=zfT[:, :], in_=zfT[:, :], func=AF.Ln, bias=1.0), waits=[w1], sig=True)
                        wprev = w2
                        for ci in range(S // CH):
                            init = 0.0 if ci == 0 else Fs[:, ci * CH - 1:ci * CH]
                            wprev = P.add("dve", lambda e, ci=ci, init=init: e.tensor_tensor_scan(
                                out=Fs[:, ci * CH:(ci + 1) * CH], data0=ones[:, :], data1=zfT[:, ci * CH:(ci + 1) * CH],
                                initial=init, op0=ALU.mult, op1=ALU.subtract), waits=[wprev, w_o], sig=True)
                        w_fb = P.add("dve", lambda e: e.tensor_copy(out=Fb[:, :], in_=Fs[:, :]), waits=[wprev], sig=True)
                        P.dma("sync", FT, Fb[:, :], c_misc, waits=[w_fb])
                        for kt in range(NT if "posttr" not in KSKIP else 0):
                            w_t = P.add("pe", lambda e, kt=kt: e.transpose(out=fps[:, kt * H:(kt + 1) * H], in_=Fs[:, kt * 128:(kt + 1) * 128],
                                                                         identity=identf[0:H, 0:H]), waits=[wprev], sig=(kt == NT - 1))
                        wlast = w_t = (None if "posttr" in KSKIP else w_t)
                        for c0 in range(0, NT * H if "posttr" not in KSKIP else 0, 512):
                            c1 = min(NT * H, c0 + 512)
                            wlast = P.add("dve", lambda e, c0=c0, c1=c1: e.tensor_scalar(out=negF[:, c0:c1], in0=fps[:, c0:c1], scalar1=-1.0,
                                                                                      scalar2=None, op0=ALU.mult), waits=[w_t, wlast], sig=True)
                        P.barrier()
                        P.emit()

        def phase2(li, typ):
            j_l = li // 2
            isB = typ == "B"
            with ExitStack() as st:
                sb, ps = mk_alloc(st)
                Ka = [sb("Ka%d" % i, [128, S], BF16) for i in range(2)]
                VTb = [sb("VTb%d" % i, [64, S], BF16) for i in range(2)]
                Vones = [sb("Vones%d" % i, [128, NT, 128], BF16) for i in range(2)]
                Qa = [sb("Qa%d" % i, [128, QB], BF16) for i in range(3)]
                Zb = [sb("Zb%d" % i, [64, QB], BF16) for i in range(3)]
                PW = 1024 if isB else 1280
                PT = [sb("PT%d" % i, [128, PW], BF16) for i in range(3)]
                rinv = sb("rinv", [128, QB], F32)
                t1 = sb("t1", [64, QB], F32)
                Gs = [sb("Gs%d" % i, [64, QB], BF16) for i in range(3)]
                PSW = 1024 if isB else 1536
                NSB = 3 if isB else 2
                if isB:
                    Sps = [ps("Sps%d" % i, [128, 1024], F32) for i in range(3)]
                    Ops = ps("Ops", [128, 1024], F32)
                    Osb = sb("Osb", [128, QB], F32)
                else:
                    Sps = [ps("Sps%d" % i, [128, 1536], F32) for i in range(2)]
                    OpsA = [ps("OpsA%d" % i, [128, 512], F32) for i in range(2)]
                    OsbA = [sb("OsbA%d" % i, [64, QB], F32) for i in range(2)]
                    rsA = [sb("rsA%d" % i, [64, QB], F32) for i in range(2)]
                    Mb = sb("Mb", [128, H, 640], BF16)
                    bstage = [sb("bstage%d" % i, [128, 640], F32) for i in range(2)]
                    maskf = sb("maskf", [128, 640], F32)

                w_init = None
                for i in range(2):
                    P.add("dve", lambda e, i=i: e.memset(Ka[i][64:65, :], 1.0), sig=True)
                    P.add("dve", lambda e, i=i: e.memset(Vones[i][:, :, 64:128], 1.0), sig=True)
                if not isB:
                    for i in range(3):
                        P.add("dve", lambda e, i=i: e.memset(Qa[i][64:65, :], 0.0), sig=True)
                w_init = (P.ech["dve"], P.ech["dve"].count)
                w_mb = None
                if not isB:
                    cbs = sb("cbs", [128, H], F32)
                    P.dma("sync", cbs[:, :], cb_d[j_l], c_misc)
                    w_mk = P.dma("sync", maskf[:, :], maskA_d, c_misc)
                    wb_prev = {}
                    for h in range(H):
                        wl = P.dma("sync", bstage[h % 2][:, :], biasT_d[j_l, h], c_x[h % 2], waits=[wb_prev.get(h - 2)])
                        wb_prev[h] = P.add("dve", lambda e, h=h: e.scalar_tensor_tensor(out=Mb[:, h, :], in0=bstage[h % 2][:, :], scalar=cbs[:, h:h + 1],
                                                                                      in1=maskf[:, :], op0=ALU.subtract, op1=ALU.add),
                                           waits=[wl, w_mk], sig=True)
                    w_mb = wb_prev[H - 1]

                pe_head_done = {}
                w_kv = {}
                w_t1_prev = [None]
                w_g_prev = [None]
                w_t1_blk = {}

                def load_kv(h):
                    hb = h % 2
                    P.dma("sync", Ka[hb][0:64, :], KT[h * 64:(h + 1) * 64, :], c_k[hb], waits=[pe_head_done.get(h - 2), w_init])
                    w_kv[h] = P.dma("sync", VTb[hb][:, :], VT[h * 64:(h + 1) * 64, :], c_k[hb], waits=[pe_head_done.get(h - 2)])

                qstate = {"n": 0}
                w_qload = {}
                w_qfree = {}
                w_gst = {}

                def load_q(h, qi):
                    qn = h * NQB + qi
                    qb = qn % 3
                    q0 = qi * QB
                    wf_ = w_qfree.get(qn - 3)
                    P.dma("sync", Qa[qb][0:64, :], QT[h * 64:(h + 1) * 64, q0:q0 + QB], c_q3[qb], waits=[wf_, w_init])
                    if isB:
                        P.dma("sync", Qa[qb][64:65, :], FT[h:h + 1, q0:q0 + QB], c_q3[qb], waits=[wf_])
                    w_qload[qn] = P.dma("sync", Zb[qb][:, :], ZT[h * 64:(h + 1) * 64, q0:q0 + QB], c_q3[qb], waits=[wf_])

                load_kv(0)
                upe = 0
                w_exp = {}
                w_pv = {}
                w_ops_free = None
                w_opsA = {}
                for h in range(H):
                    hb = h % 2
                    if h + 1 < H:
                        load_kv(h + 1)
                    vviews = [Sps[upe % NSB][:, :].bitcast(BF16), Sps[(upe + 1) % NSB][:, :].bitcast(BF16)]
                    G0 = (PSW * 2) // 64
                    assert NT <= 2 * G0
                    for kt in range(NT):
                        vv = vviews[kt // G0]
                        ko = kt % G0
                        w_t = P.add("pe", lambda e, kt=kt, vv=vv, ko=ko, hb=hb: e.transpose(
                            out=vv[:, ko * 64:(ko + 1) * 64],
                            in_=VTb[hb][0:64, kt * 128:(kt + 1) * 128], identity=identb[0:64, 0:64]),
                            waits=[w_kv[h], w_exp.get(upe - 3), w_exp.get(upe - 2), w_exp.get(upe - 1), pe_head_done.get(h - 2)], sig=(kt == NT - 1))
                    w_vlast = None
                    for g in range(0, NT, 16):
                        n = min(16, NT - g)
                        vv = vviews[g // G0]
                        go = g % G0
                        w_vlast = P.add("dve", lambda e, g=g, n=n, vv=vv, go=go, hb=hb: e.tensor_copy(
                            out=Vones[hb][:, g:g + n, 0:64],
                            in_=vv[:, go * 64:(go + n) * 64].rearrange("p (k d) -> p k d", d=64)),
                            waits=[w_t, w_init], sig=True)

                    units = []
                    if isB:
                        for qi in range(NQB):
                            nsub = QB // 128
                            nk = (qi + 1) * nsub
                            for kt in range(nk):
                                r = kt - qi * nsub
                                c0 = 128 * r if r > 0 else 0
                                u = dict(qi=qi, kt=kt, c0=c0, diag=(r >= 0), first=(kt == 0), last=(kt == nk - 1))
                                units.append(u)
                    else:
                        for m2 in range(NT // 2):
                            m0 = 2 * m2
                            qi = (m0 * 128) // QB
                            qoff = m0 * 128 - qi * QB
                            rows = []
                            for (dk, sc0, ncl, qlo, jlo, bseg) in ((-3, 256, 256, 0, 3, (128, 128)), (-2, 512, 256, 0, 2, None),
                                                                  (-1, 768, 256, 0, 1, (0, 128)), (0, 1024, 256, 0, 0, (0, 256)),
                                                                  (-4, 0, 128, 0, 4, (0, 128)), (1, 128, 128, 128, 0, (0, 128))):
                                kt = m0 + dk
                                if kt < 0:
                                    continue
                                rows.append((kt, sc0, ncl, qlo, jlo, bseg))
                            units.append(dict(qi=qi, qoff=qoff, rows=rows, m2=m2, last=((m0 + 2) * 128) % QB == 0))

                    def qn_of(u):
                        return h * NQB + u["qi"]

                    def emit_qk(ui, u):
                        g = upe + ui
                        sbuf_ = Sps[g % NSB]
                        qn = qn_of(u)
                        qa = Qa[qn % 3]
                        wts = [w_qload[qn], w_vlast, w_exp.get(g - NSB), w_init, w_mb]
                        if isB:
                            kt, c0 = u["kt"], u["c0"]
                            for lo in (0, 512):
                                hi = lo + 512
                                a = max(lo, c0)
                                if a >= hi or a >= QB:
                                    continue
                                hi = min(hi, QB)
                                dg = u["diag"] and (lo <= c0 < hi)
                                wq = P.add("pe", lambda e, sbuf_=sbuf_, kt=kt, a=a, hi=hi, qa=qa, dg=dg, hb=hb: e.matmul(
                                    sbuf_[:, a:hi], lhsT=Ka[hb][0:65, kt * 128:(kt + 1) * 128], rhs=qa[0:65, a:hi], start=True, stop=not dg),
                                    waits=wts, sig=True)
                                if dg:
                                    wq = P.add("pe", lambda e, sbuf_=sbuf_, c0=c0: e.matmul(
                                        sbuf_[:, c0:c0 + 128], lhsT=identb[:, :], rhs=trib[:, :], start=False, stop=True), sig=True)
                            return wq
                        else:
                            wq = None
                            nr = len(u["rows"])
                            for ri, (kt, sc0, ncl, qlo, jlo, bseg) in enumerate(u["rows"]):
                                qc = u["qoff"] + qlo
                                lastrow = ri == nr - 1
                                wq = P.add("pe", lambda e, sbuf_=sbuf_, kt=kt, sc0=sc0, ncl=ncl, qc=qc, qa=qa, hb=hb, bseg=bseg: e.matmul(
                                    sbuf_[:, sc0:sc0 + ncl], lhsT=Ka[hb][0:65, kt * 128:(kt + 1) * 128], rhs=qa[0:65, qc:qc + ncl],
                                    start=True, stop=(bseg is None)), waits=wts, sig=(lastrow and bseg is None))
                                if bseg is not None:
                                    bo, bn = bseg
                                    wq = P.add("pe", lambda e, sbuf_=sbuf_, sc0=sc0, bo=bo, bn=bn, jlo=jlo, h=h: e.matmul(
                                        sbuf_[:, sc0 + bo:sc0 + bo + bn], lhsT=identb[:, :], rhs=Mb[:, h, jlo * 128 + bo: jlo * 128 + bo + bn],
                                        start=False, stop=True), sig=lastrow)
                            return wq

                    def emit_exp(ui, u, wq):
                        g = upe + ui
                        sbuf_ = Sps[g % NSB]
                        pt = PT[g % 3]
                        wts = [wq, w_pv.get(g - 3)]
                        if isB:
                            kt, c0 = u["kt"], u["c0"]
                            w_exp[g] = P.add("act", lambda e, sbuf_=sbuf_, pt=pt, kt=kt, c0=c0, h=h: e.activation(
                                out=pt[:, c0:QB], in_=sbuf_[:, c0:QB], func=AF.Exp, bias=negF[:, kt * H + h: kt * H + h + 1], scale=1.0),
                                waits=wts, sig=True)
                        else:
                            segs = sorted((sc0, sc0 + ncl) for (kt, sc0, ncl, qlo, jlo, bseg) in u["rows"])
                            merged = []
                            for a, b_ in segs:
                                if merged and merged[-1][1] == a:
                                    merged[-1][1] = b_
                                else:
                                    merged.append([a, b_])
                            for a, b_ in merged:
                                w_exp[g] = P.add("act", lambda e, sbuf_=sbuf_, pt=pt, a=a, b_=b_: e.activation(
                                    out=pt[:, a:b_], in_=sbuf_[:, a:b_], func=AF.Exp), waits=wts, sig=True)

                    def emit_pv(ui, u):
                        nonlocal w_ops_free
                        g = upe + ui
                        pt = PT[g % 3]
                        qn = qn_of(u)
                        if isB:
                            kt, c0 = u["kt"], u["c0"]
                            wp = None
                            for lo in (0, 512):
                                hi = min(lo + 512, QB)
                                a = max(lo, c0)
                                if a >= hi:
                                    continue
                                nsub_ = QB // 128
                                last_kt = u["qi"] * nsub_ + min(nsub_, (lo // 512 + 1) * 4) - 1
                                wp = P.add("pe", lambda e, pt=pt, kt=kt, a=a, hi=hi, u=u, hb=hb, last_kt=last_kt: e.matmul(
                                    Ops[:, a:hi], lhsT=Vones[hb][:, kt, :], rhs=pt[:, a:hi], start=u["first"], stop=(kt == last_kt)),
                                    waits=[w_exp[g], w_ops_free if u["first"] else None], sig=True)
                            w_pv[g] = wp
                            if u["last"]:
                                pending.append(lambda u=u, qn=qn, wp=wp: finish_block_B(u, qn, wp))
                        else:
                            ob = u["m2"] % 2
                            rows = u["rows"]
                            wp = None
                            for ri, (kt, sc0, ncl, qlo, jlo, bseg) in enumerate(rows):
                                wp = P.add("pe", lambda e, pt=pt, kt=kt, sc0=sc0, ncl=ncl, qlo=qlo, ob=ob, ri=ri, nr=len(rows), hb=hb: e.matmul(
                                    OpsA[ob][:, qlo: qlo + ncl], lhsT=Vones[hb][:, kt, :], rhs=pt[:, sc0:sc0 + ncl],
                                    start=(ri == 0), stop=(ri == nr - 1)),
                                    waits=[w_exp[g], w_opsA.get(u["m2"] + h * (NT // 2) - 2) if ri == 0 else None], sig=True)
                            w_pv[g] = wp
                            pending.append(lambda u=u, qn=qn, wp=wp, ob=ob: finish_unit_A(u, qn, wp, ob))

                    def finish_block_B(u, qn, wp):
                        nonlocal w_ops_free
                        qb = qn % 3
                        q0 = u["qi"] * QB
                        wcp = P.add("dve", lambda e: e.tensor_copy(out=Osb[:, :], in_=Ops[:, :]), waits=[wp, w_t1_prev[0]], sig=True)
                        w_ops_free = wcp
                        wr = P.add("dve", lambda e: e.reciprocal(out=rinv[0:64, :], in_=Osb[64:128, :]), waits=[wcp], sig=True)
                        wt1 = P.add("dve", lambda e: e.tensor_tensor(out=t1[0:64, :], in0=Osb[0:64, :], in1=rinv[0:64, :], op=ALU.mult),
                                    waits=[wr, w_g_prev[0]], sig=True)
                        w_t1_prev[0] = wt1
                        wg = P.add("dve", lambda e, qb=qb: e.tensor_tensor(out=Gs[qb][:, :], in0=t1[0:64, :], in1=Zb[qb][:, :], op=ALU.mult),
                                   waits=[wt1, w_qload[qn], w_gst.get(qn - 3)], sig=True)
                        w_g_prev[0] = wg
                        w_qfree[qn] = wg
                        w_gst[qn] = P.dma("sync", GT[h * 64:(h + 1) * 64, q0:q0 + QB], Gs[qb][:, :], c_g3[qb], waits=[wg])

                    def finish_unit_A(u, qn, wp, ob):
                        qb = qn % 2
                        q0 = u["qi"] * QB
                        qoff = u["qoff"]
                        gi = u["m2"] + h * (NT // 2)
                        wc1 = P.add("dve", lambda e, ob=ob, qb=qb, qoff=qoff: e.tensor_copy(out=OsbA[qb][0:64, qoff:qoff + 256], in_=OpsA[ob][0:64, 0:256]),
                                    waits=[wp, w_t1_blk.get(qn - 2)], sig=True)
                        wc2 = P.add("dve", lambda e, ob=ob, qb=qb, qoff=qoff: e.tensor_copy(out=rsA[qb][0:64, qoff:qoff + 256], in_=OpsA[ob][64:128, 0:256]),
                                    waits=[wc1], sig=True)
                        w_opsA[gi] = wc2
                        if u["last"]:
                            wr0 = P.add("act", lambda e, qb=qb: e.activation(out=rsA[qb][0:64, :], in_=rsA[qb][0:64, :], func=AF.Ln), waits=[wc2], sig=True)
                            wr = P.add("act", lambda e, qb=qb: e.activation(out=rsA[qb][0:64, :], in_=rsA[qb][0:64, :], func=AF.Exp, scale=-1.0),
                                       waits=[wr0], sig=True)
                            wt1 = P.add("dve", lambda e, qb=qb: e.tensor_tensor(out=t1[0:64, :], in0=OsbA[qb][0:64, :], in1=rsA[qb][0:64, :], op=ALU.mult),
                                        waits=[wr, w_g_prev[0]], sig=True)
                            w_t1_blk[qn] = wt1
                            q3 = qn % 3
                            wg = P.add("pool", lambda e, q3=q3: e.tensor_tensor(out=Gs[q3][:, :], in0=t1[0:64, :], in1=Zb[q3][:, :], op=ALU.mult),
                                       waits=[wt1, w_qload[qn], w_gst.get(qn - 3)], sig=True)
                            w_g_prev[0] = wg
                            w_qfree[qn] = wg
                            w_gst[qn] = P.dma("sync", GT[h * 64:(h + 1) * 64, q0:q0 + QB], Gs[q3][:, :], c_g3[q3], waits=[wg])

                    nU = len(units)
                    wq_list = {}

                    def ensure_q(ui):
                        u = units[ui]
                        qn = h * NQB + u["qi"]
                        for qq in (qn, qn + 1):
                            if qq >= H * NQB or qq in loaded_all:
                                continue
                            if qq - 3 >= 0 and (qq - 3) not in w_qfree:
                                assert qq != qn, "q block needed but predecessor not finished"
                                continue
                            load_q(qq // NQB, qq % NQB)
                            loaded_all.add(qq)

                    for ui in range(min(NSB, nU)):
                        ensure_q(ui)
                        wq_list[ui] = emit_qk(ui, units[ui])
                        emit_exp(ui, units[ui], wq_list[ui])
                    pending = []
                    for ui in range(nU):
                        emit_pv(ui, units[ui])
                        if ui + NSB < nU:
                            ensure_q(ui + NSB)
                            wq_list[ui + NSB] = emit_qk(ui + NSB, units[ui + NSB])
                            emit_exp(ui + NSB, units[ui + NSB], wq_list[ui + NSB])
                        for f_ in pending:
                            f_()
                        pending.clear()
                    upe += nU
                    pe_head_done[h] = w_pv[upe - 1]
                P.barrier()
                P.emit()

        loaded_all = set()

        def phase3(li, typ, x_src, x_dst):
            j_l = li // 2
            w_out = (w_out_a if typ == "A" else w_out_b)[j_l]
            with ExitStack() as st:
                sb, ps = mk_alloc(st)
                NU = 5
                Wo = sb("Wo", [128, 8, D], BF16)
                GTb = [sb("GTb%d" % i, [128, 8, 512], BF16) for i in range(2)]
                xres = [sb("xres%d" % i, [128, 4, D], F32) for i in range(2)]
                ub = [sb("ub%d" % i, [128, D], F32) for i in range(NU)]
                xo = [sb("xo%d" % i, [128, 4, D], F32) for i in range(2)]
                gsb_ = sb("lng_s", [128, D], F32)
                bsb_ = sb("lnb_s", [128, D], F32)
                sm = [sb("sm%d" % i, [128, 8], F32) for i in range(NU)]
                sqs = sb("sqs", [128, D], BF16)
                w_sq_prev = [None]
                w_stB = {}
                yps = [ps("yps%d" % i, [128, D], F32) for i in range(2)]
                w_w = None
                for c in range(8):
                    w_w = P.dma("pool", Wo[:, c, :], w_out[c * 128:(c + 1) * 128, :], c_w)
                P.dma("sync", gsb_[:, :], lng_d[li], c_misc)
                w_gb = P.dma("sync", bsb_[:, :], lnb_d[li], c_misc)
                w_ld = {}
                w_blk_pe = {}
                w_blk_res = {}
                w_sto = {}

                def load(b):
                    P.dma("sync", GTb[b % 2][:, :, :], GT[:, b * 512:(b + 1) * 512].rearrange("(c p) s -> p c s", p=128), c_x[b % 2],
                          waits=[w_blk_pe.get(b - 2)])
                    w_ld[b] = P.dma("sync", xres[b % 2][:, :, :], x_src[b * 512:(b + 1) * 512, :].rearrange("(t p) d -> p t d", p=128),
                                    c_x[b % 2], waits=[w_blk_res.get(b - 2)])
                w_u_free = {}
                w_y_free = {}
                w_stA = {}

                def stageA(tu):
                    b, t = divmod(tu, 4)
                    gt = GTb[b % 2]
                    xr = xres[b % 2]
                    yp = yps[tu % 2]
                    k = tu % NU
                    u_ = ub[k]
                    for half in range(2):
                        for c in range(8):
                            w_m = P.add("pe", lambda e, yp=yp, gt=gt, t=t, half=half, c=c: e.matmul(
                                yp[:, half * 512:(half + 1) * 512], lhsT=gt[:, c, t * 128:(t + 1) * 128],
                                rhs=Wo[:, c, half * 512:(half + 1) * 512], start=(c == 0), stop=(c == 7)),
                                waits=[w_w, w_ld[b], w_y_free.get(tu - 2)], sig=(c == 7 and half == 1))
                    if t == 3:
                        w_blk_pe[b] = w_m
                    w_r = P.add("dve", lambda e, u_=u_, xr=xr, t=t, yp=yp, k=k: e.scalar_tensor_tensor(
                        out=u_[:, :], in0=xr[:, t, :], scalar=ALPHA, in1=yp[:, :], op0=ALU.mult, op1=ALU.add, accum_out=sm[k][:, 0:1]),
                        waits=[w_m, w_ld[b], w_u_free.get(tu - NU)], sig=True)
                    w_y_free[tu] = w_r
                    if t == 3:
                        w_blk_res[b] = w_r
                    w_q = P.add("act", lambda e, u_=u_, k=k: e.activation(out=sqs[:, :], in_=u_[:, :], func=AF.Square, accum_out=sm[k][:, 1:2]),
                                waits=[w_r, w_sq_prev[0]], sig=True)
                    w_sq_prev[0] = w_q
                    w_1 = P.add("dve", lambda e, k=k: e.tensor_scalar(out=sm[k][:, 2:3], in0=sm[k][:, 0:1], scalar1=1.0 / D, scalar2=None, op0=ALU.mult),
                                waits=[w_r], sig=True)
                    w_2 = P.add("dve", lambda e, k=k: e.scalar_tensor_tensor(out=sm[k][:, 3:4], in0=sm[k][:, 2:3], scalar=-1.0, in1=sm[k][:, 2:3],
                                                                          op0=ALU.mult, op1=ALU.mult), waits=[w_1], sig=True)
                    w_3 = P.add("dve", lambda e, k=k: e.scalar_tensor_tensor(out=sm[k][:, 4:5], in0=sm[k][:, 1:2], scalar=1.0 / D, in1=sm[k][:, 3:4],
                                                                          op0=ALU.mult, op1=ALU.add), waits=[w_2, w_q], sig=True)
                    w_4 = P.add("dve", lambda e, k=k: e.tensor_scalar(out=sm[k][:, 4:5], in0=sm[k][:, 4:5], scalar1=EPS, scalar2=None, op0=ALU.add),
                                waits=[w_3], sig=True)
                    w_5 = P.add("act", lambda e, k=k: e.activation(out=sm[k][:, 5:6], in_=sm[k][:, 4:5], func=AF.Ln), waits=[w_4], sig=True)
                    w_6 = P.add("act", lambda e, k=k: e.activation(out=sm[k][:, 6:7], in_=sm[k][:, 5:6], func=AF.Exp, scale=-0.5), waits=[w_5], sig=True)
                    w_stA[tu] = w_6

                def stageB1(tu):
                    k = tu % NU
                    u_ = ub[k]
                    w_7 = P.add("dve", lambda e, k=k: e.scalar_tensor_tensor(out=sm[k][:, 7:8], in0=sm[k][:, 2:3], scalar=-1.0, in1=sm[k][:, 6:7],
                                                                          op0=ALU.mult, op1=ALU.mult), waits=[w_stA[tu]], sig=True)
                    w_stB[tu] = P.add("act", lambda e, u_=u_, k=k: e.activation(out=u_[:, :], in_=u_[:, :], func=AF.Identity,
                                                                                bias=sm[k][:, 7:8], scale=sm[k][:, 6:7]), waits=[w_7], sig=True)

                def stageB2(tu):
                    b, t = divmod(tu, 4)
                    k = tu % NU
                    u_ = ub[k]
                    xob = xo[b % 2]
                    w_g = P.add("dve", lambda e, u_=u_: e.tensor_tensor(out=u_[:, :], in0=u_[:, :], in1=gsb_[:, :], op=ALU.mult),
                                waits=[w_stB[tu], w_gb], sig=True)
                    w_b = P.add("pool", lambda e, u_=u_, xob=xob, t=t: e.tensor_tensor(out=xob[:, t, :], in0=u_[:, :], in1=bsb_[:, :], op=ALU.add),
                                waits=[w_g, w_sto.get(b - 2)], sig=True)
                    w_u_free[tu] = w_b
                    if t == 3:
                        w_sto[b] = P.dma("sync", x_dst[b * 512:(b + 1) * 512, :].rearrange("(t p) d -> p t d", p=128), xob[:, :, :],
                                         c_g[b % 2], waits=[w_b])
                        if b + 2 < NB:
                            load(b + 2)

                load(0)
                if NB > 1:
                    load(1)
                NTI = NB * 4
                for tu in range(NTI + 2):
                    if tu < NTI:
                        stageA(tu)
                    if 0 <= tu - 1 < NTI:
                        stageB1(tu - 1)
                    if 0 <= tu - 2 < NTI:
                        stageB2(tu - 2)
                P.barrier()
                P.emit()

        import os
        kstop = int(os.environ.get("KSTOP", "99"))
        x_src = x_d
        for li, typ in enumerate(ltypes):
            last = li == NL - 1
            if kstop >= 1:
                phase1(li, typ, x_src)
            loaded_all.clear()
            if kstop >= 2:
                phase2(li, typ)
            if kstop >= 3:
                phase3(li, typ, x_src, y_d if last else xbuf)
            x_src = xbuf
        print("ops recorded:", P.nops)
    return nc


def host_consts():
    ident = np.eye(128, dtype=np.float32)
    k = np.arange(128)[:, None]
    q = np.arange(128)[None, :]
    tri = np.where(k > q, NEG, 0.0).astype(np.float32)
    maskA = np.zeros((128, 5, 128), np.float32)
    maskA[64:, 0, :64] = NEG
    maskA[:64, 4, 64:] = NEG
    return ident, tri, maskA.reshape(128, 640)


def host_bias_layout(rel_bias_a):
    k = np.arange(128)[:, None, None]
    j = np.arange(5)[None, :, None]
    q = np.arange(128)[None, None, :]
    idx = np.clip(128 * j + q - k, -128, 128) + 128
    out = rel_bias_a[:, :, idx]
    return np.ascontiguousarray(out.reshape(rel_bias_a.shape[0], H, 128, 640)).astype(np.float32)


def make_common(w_in_a, rel_bias_a, w_out_a, w_in_b, w_f_b, b_f_b, w_out_b, ln_g, ln_b):
    ident, tri, maskA = host_consts()
    f = lambda a: np.ascontiguousarray(np.asarray(a, dtype=np.float32))
    return {
        "w_in_a": f(w_in_a), "w_in_b": f(w_in_b), "w_out_a": f(w_out_a), "w_out_b": f(w_out_b),
        "w_f_b": f(w_f_b), "bfb": f(np.asarray(b_f_b)[:, :, None]),
        "biasT": host_bias_layout(np.asarray(rel_bias_a, dtype=np.float32)),
        "lng": f(np.broadcast_to(np.asarray(ln_g)[:, None, :], (4, 128, D))),
        "lnb": f(np.broadcast_to(np.asarray(ln_b)[:, None, :], (4, 128, D))),
        "ident": ident, "tri": tri, "maskA": maskA,
        "cbias": f(np.broadcast_to(np.asarray(rel_bias_a, dtype=np.float32)[:, None, :, 256], (2, 128, H))),
    }


def kernel(x, w_in_a, rel_bias_a, w_out_a, w_in_b, w_f_b, b_f_b, w_out_b, ln_g, ln_b):
    x = np.asarray(x, dtype=np.float32)
    common = make_common(w_in_a, rel_bias_a, w_out_a, w_in_b, w_f_b, b_f_b, w_out_b, ln_g, ln_b)
    nc = build(SEQ, "ABAB")
    in_maps = []
    for c in range(NCORES):
        m = dict(common)
        m["x"] = np.ascontiguousarray(x[c])
        in_maps.append(m)
    res = run_bass_kernel_spmd(nc, in_maps, core_ids=list(range(NCORES)))
    return np.stack([np.asarray(r["y"], dtype=np.float32) for r in res.results], axis=0)
```

```python
import numpy as np
from contextlib import ExitStack
import concourse.bass as bass
import concourse.mybir as mybir
from concourse.bass_utils import run_bass_kernel_spmd

F32 = mybir.dt.float32
BF16 = mybir.dt.bfloat16
I32 = mybir.dt.int32
AF = mybir.ActivationFunctionType
ALU = mybir.AluOpType

D = 1024
H = 16
NEG = -30000.0
ALPHA = 8.0 ** 0.25
EPS = 1e-5
SEQ = 8192
NCORES = 8
import os
KSKIP = os.environ.get("KSKIP", "").split(",")


class Chan:
    def __init__(self, sem, name):
        self.sem = sem
        self.count = 0
        self.name = name


class Prog:
    ENG = ("sync", "act", "pe", "dve", "pool")

    def __init__(self, nc, stack):
        self.nc = nc
        self.stack = stack
        self.q = {e: [] for e in self.ENG}
        self.waited = {}
        self.chans = []
        self.ech = {e: self.chan("e_" + e) for e in ("act", "pe", "dve", "pool")}
        self.nops = 0

    def chan(self, name):
        sem = self.stack.enter_context(self.nc.semaphore(name))
        c = Chan(sem, name)
        self.chans.append(c)
        return c

    def add(self, eng, fn, waits=(), sig=None, inc=1):
        ws = []
        for w in waits:
            if w is None:
                continue
            ch, v = w
            if v <= 0:
                continue
            key = (eng, ch.name)
            if self.waited.get(key, 0) >= v:
                continue
            self.waited[key] = v
            ws.append((ch, v))
        ch = None
        if sig is True:
            ch = self.ech[eng]
        elif sig:
            ch = sig
        ret = None
        if ch is not None:
            ch.count += inc
            ret = (ch, ch.count)
        self.q[eng].append((fn, ws, ch, inc))
        self.nops += 1
        return ret

    def dma(self, eng, out, in_, ch, waits=()):
        return self.add(eng, lambda e: e.dma_start(out=out, in_=in_), waits=waits, sig=ch, inc=16)

    def barrier(self):
        ws = [(c, c.count) for c in self.chans if c.count > 0]
        for e in self.ENG:
            self.add(e, None, waits=ws)

    def emit(self):
        nc = self.nc
        q = self.q
        self.q = {e: [] for e in self.ENG}
        with nc.Block() as block:
            def mk(name):
                def run(eng):
                    for fn, ws, ch, inc in q[name]:
                        for c, v in ws:
                            eng.wait_ge(c.sem, v)
                        if fn is None:
                            continue
                        ins = fn(eng)
                        if ch is not None:
                            ins.then_inc(ch.sem, inc)
                return run
            block.sync(mk("sync"))
            block.scalar(mk("act"))
            block.tensor(mk("pe"))
            block.vector(mk("dve"))
            block.gpsimd(mk("pool"))


def build(S, ltypes, dbg=None):
    nc = bass.Bass("TRN2", target_bir_lowering=False)
    NL = len(ltypes)
    NT = S // 128
    NB = S // 512
    QB = min(1024, S)
    NQB = S // QB

    def din(name, shape, dt=F32):
        return nc.dram_tensor(name, shape, dt, kind="ExternalInput").ap()

    x_d = din("x", [S, D])
    w_in_a = din("w_in_a", [2, D, 4 * D])
    w_in_b = din("w_in_b", [2, D, 4 * D])
    w_out_a = din("w_out_a", [2, D, D])
    w_out_b = din("w_out_b", [2, D, D])
    w_f_b = din("w_f_b", [2, D, H])
    bfb_d = din("bfb", [2, H, 1])
    biasT_d = din("biasT", [2, H, 128, 640])
    lng_d = din("lng", [4, 128, D])
    lnb_d = din("lnb", [4, 128, D])
    ident_d = din("ident", [128, 128])
    tri_d = din("tri", [128, 128])
    maskA_d = din("maskA", [128, 640])
    cb_d = din("cbias", [2, 128, H])
    y_d = nc.dram_tensor("y", [S, D], F32, kind="ExternalOutput").ap()

    xbuf = nc.dram_tensor("xbuf", [S, D], F32).ap()
    QT = nc.dram_tensor("QT", [D, S], BF16).ap()
    KT = nc.dram_tensor("KT", [D, S], BF16).ap()
    VT = nc.dram_tensor("VT", [D, S], BF16).ap()
    ZT = nc.dram_tensor("ZT", [D, S], BF16).ap()
    GT = nc.dram_tensor("GT", [D, S], BF16).ap()
    FT = nc.dram_tensor("FT", [H, S], BF16).ap()

    with ExitStack() as gst:
        uid = [0]

        def mk_alloc(st):
            def sb(name, shape, dt):
                uid[0] += 1
                return st.enter_context(nc.sbuf_tensor("%s_%d" % (name, uid[0]), shape, dt))

            def ps(name, shape, dt):
                uid[0] += 1
                return st.enter_context(nc.psum_tensor("%s_%d" % (name, uid[0]), shape, dt))
            return sb, ps

        gsb, gps = mk_alloc(gst)
        P = Prog(nc, gst)

        identf = gsb("identf", [128, 128], F32)
        identb = gsb("identb", [128, 128], BF16)
        trib = gsb("trib", [128, 128], BF16)
        negF = gsb("negF", [128, NT * H], F32)
        c_const = P.chan("c_const")
        with ExitStack() as st:
            sb, ps = mk_alloc(st)
            trif = sb("trif", [128, 128], F32)
            P.dma("sync", identf[:, :], ident_d, c_const)
            w = P.dma("sync", trif[:, :], tri_d, c_const)
            P.add("dve", lambda e: e.tensor_copy(out=identb[:, :], in_=identf[:, :]), waits=[w], sig=True)
            P.add("dve", lambda e: e.tensor_copy(out=trib[:, :], in_=trif[:, :]), sig=True)
            P.barrier()
            P.emit()

        c_w = P.chan("c_w")
        c_x = [P.chan("c_x0"), P.chan("c_x1")]
        c_g = [P.chan("c_g0"), P.chan("c_g1")]
        c_st = [P.chan("c_st%d" % i) for i in range(4)]
        c_k = [P.chan("c_k0"), P.chan("c_k1")]
        c_q3 = [P.chan("c_q3_%d" % i) for i in range(3)]
        c_g3 = [P.chan("c_g3_%d" % i) for i in range(3)]
        c_misc = P.chan("c_misc")

        ZF = nc.dram_tensor("ZF", [H, S], F32).ap()
        c1_w = P.chan("c1_w")
        c1_x = [P.chan("c1_x0"), P.chan("c1_x1")]
        c1_st = [P.chan("c1_st%d" % i) for i in range(4)]
        c1_zf = [P.chan("c1_zf0"), P.chan("c1_zf1")]
        c1_m = P.chan("c1_m")

        def phase1_main(li, typ, x_src, st, lean, x_ready=None):
            j_l = li // 2
            w_in = (w_in_a if typ == "A" else w_in_b)[j_l]
            isB = typ == "B"
            sb, ps = mk_alloc(st)
            NXIN = 1 if lean else 2
            NMM = 3 if lean else 4
            Wb = sb("Wb", [128, 8, 4 * D], BF16)
            xin = [sb("xin%d" % i, [128, 4, D], F32) for i in range(NXIN)]
            xT = [sb("xT%d" % i, [128, 8, 512], BF16) for i in range(2)]
            osb = [sb("osb%d" % i, [128, 512], BF16) for i in range(4)]
            tp = [ps("tp%d" % i, [128, 512], F32) for i in range(2)]
            mm = [ps("mm%d" % i, [128, 512], F32) for i in range(NMM)]
            if isB:
                xTf = sb("xTf", [128, 8, 512], F32)
                wf = sb("wf", [128, 8, H], F32)
                bfs = sb("bfs", [H, 1], F32)
                zfs = [sb("zfs%d" % i, [H, 512], F32) for i in range(2)]
                zps = ps("zps", [H, 512], F32)
            w_w = None
            for c in range(8):
                for hh in range(2):
                    w_w = P.dma("pool", Wb[:, c, hh * 2048:(hh + 1) * 2048],
                                w_in[c * 128:(c + 1) * 128, hh * 2048:(hh + 1) * 2048], c1_w)
            w_wf = None
            if isB:
                P.dma("sync", wf[:, :, :], w_f_b[j_l].rearrange("(c p) h -> p c h", p=128), c1_m)
                w_wf = P.dma("sync", bfs[:, :], bfb_d[j_l], c1_m)
            w_tr_done = {}
            w_x = {}

            def load_x(b):
                ws = [w_tr_done.get(b - NXIN)]
                if x_ready is not None:
                    assert b in x_ready, "x block %d not yet produced" % b
                    ws = ws + list(x_ready[b])
                w_x[b] = P.dma("sync", xin[b % NXIN][:, :, :],
                               x_src[b * 512:(b + 1) * 512, :].rearrange("(t p) d -> p t d", p=128),
                               c1_x[b % 2], waits=ws)
            for b in range(min(NXIN, NB)):
                load_x(b)
            ev_tp = {}
            tpu = 0
            mmu = 0
            ev_mm = {}
            st_w = {}
            stu = 0
            w_zf_mm_prev = None
            w_zf_ev_prev = None
            w_zf_st = {}
            for b in range(NB):
                xi = xin[b % NXIN]
                xt = xT[b % 2]
                w_ev_last = None
                w_evf_last = None
                for c in range(8):
                    tb = tp[tpu % 2]
                    for t in range(4):
                        w_t = P.add("pe", lambda e, tb=tb, xi=xi, t=t, c=c: e.transpose(
                            out=tb[:, t * 128:(t + 1) * 128], in_=xi[:, t, c * 128:(c + 1) * 128], identity=identf[:, :]),
                            waits=[w_x[b]] + (ev_tp.get(tpu - 2) or []), sig=(t == 3))
                    if c == 7:
                        w_tr_done[b] = w_t
                    wl = []
                    w_ev_last = P.add("dve", lambda e, tb=tb, xt=xt, c=c: e.tensor_copy(out=xt[:, c, :], in_=tb[:, :]),
                                      waits=[w_t], sig=True)
                    wl.append(w_ev_last)
                    if isB:
                        w_evf_last = P.add("act", lambda e, tb=tb, c=c: e.activation(out=xTf[:, c, :], in_=tb[:, :], func=AF.Identity),
                                           waits=[w_t, w_zf_mm_prev, w_ev_last], sig=True)
                        wl.append(w_evf_last)
                    ev_tp[tpu] = wl
                    tpu += 1
                if b + NXIN < NB:
                    load_x(b + NXIN)
                if isB:
                    for c in range(8):
                        w_zf_mm = P.add("pe", lambda e, c=c: e.matmul(zps[:, :], lhsT=wf[:, c, :], rhs=xTf[:, c, :],
                                                                     start=(c == 0), stop=(c == 7)),
                                        waits=[w_evf_last, w_wf, w_zf_ev_prev], sig=(c == 7))
                    w_zf_mm_prev = w_zf_mm
                    zs = zfs[b % 2]
                    w_zf_ev_prev = P.add("dve", lambda e, zs=zs: e.tensor_scalar(
                        out=zs[:, :], in0=zps[:, :], scalar1=bfs[:, 0:1], scalar2=None, op0=ALU.add),
                        waits=[w_zf_mm, w_zf_st.get(b - 2)], sig=True)
                    w_zf_st[b] = P.dma("sync", ZF[:, b * 512:(b + 1) * 512], zs[:, :], c1_zf[b % 2], waits=[w_zf_ev_prev])
                for j in range(32):
                    mb = mm[mmu % NMM]
                    for c in range(8):
                        w_m = P.add("pe", lambda e, mb=mb, xt=xt, j=j, c=c: e.matmul(
                            mb[:, :], lhsT=Wb[:, c, j * 128:(j + 1) * 128], rhs=xt[:, c, :], start=(c == 0), stop=(c == 7)),
                            waits=[w_w, w_ev_last, ev_mm.get(mmu - NMM)], sig=(c == 7))
                    ob = osb[stu % 4]
                    wst = st_w.get(stu - 4)
                    if j < 8:
                        w_e = P.add("act", lambda e, mb=mb, ob=ob: e.activation(out=ob[:, :], in_=mb[:, :], func=AF.Copy, scale=0.125),
                                    waits=[w_m, wst], sig=True)
                        dst = QT
                    elif j < 24:
                        w_e = P.add("dve", lambda e, mb=mb, ob=ob: e.tensor_copy(out=ob[:, :], in_=mb[:, :]),
                                    waits=[w_m, wst], sig=True)
                        dst = KT if j < 16 else VT
                    else:
                        w_e = P.add("act", lambda e, mb=mb, ob=ob: e.activation(out=ob[:, :], in_=mb[:, :], func=AF.Silu),
                                    waits=[w_m, wst], sig=True)
                        dst = ZT
                    ev_mm[mmu] = w_e
                    mmu += 1
                    jj = j % 8
                    st_w[stu] = P.dma("sync", dst[jj * 128:(jj + 1) * 128, b * 512:(b + 1) * 512], ob[:, :], c1_st[stu % 4], waits=[w_e])
                    stu += 1
                    if j % 8 == 7:
                        yield

        def phase1_post(li):
            with ExitStack() as st:
                sb, ps = mk_alloc(st)
                zfT = sb("zfT", [H, S], F32)
                Fs = sb("Fs", [H, S], F32)
                Fb = sb("Fb", [H, S], BF16)
                CH = min(2048, S)
                ones = sb("ones", [H, CH], F32)
                fps = ps("fps", [128, NT * H], F32)
                w_ld = P.dma("sync", zfT[:, :], ZF, c1_m)
                w_o = P.add("dve", lambda e: e.memset(ones[:, :], 1.0), sig=True)
                w1 = P.add("act", lambda e: e.activation(out=zfT[:, :], in_=zfT[:, :], func=AF.Exp, scale=-1.0), waits=[w_ld], sig=True)
                w2 = P.add("act", lambda e: e.activation(out=zfT[:, :], in_=zfT[:, :], func=AF.Ln, bias=1.0), waits=[w1], sig=True)
                wprev = w2
                for ci in range(S // CH):
                    init = 0.0 if ci == 0 else Fs[:, ci * CH - 1:ci * CH]
                    wprev = P.add("dve", lambda e, ci=ci, init=init: e.tensor_tensor_scan(
                        out=Fs[:, ci * CH:(ci + 1) * CH], data0=ones[:, :], data1=zfT[:, ci * CH:(ci + 1) * CH],
                        initial=init, op0=ALU.mult, op1=ALU.subtract), waits=[wprev, w_o], sig=True)
                w_fb = P.add("dve", lambda e: e.tensor_copy(out=Fb[:, :], in_=Fs[:, :]), waits=[wprev], sig=True)
                P.dma("sync", FT, Fb[:, :], c1_m, waits=[w_fb])
                for kt in range(NT):
                    w_t = P.add("pe", lambda e, kt=kt: e.transpose(out=fps[:, kt * H:(kt + 1) * H], in_=Fs[:, kt * 128:(kt + 1) * 128],
                                                                 identity=identf[0:H, 0:H]), waits=[wprev], sig=(kt == NT - 1))
                wlast = w_t
                for c0 in range(0, NT * H, 512):
                    c1 = min(NT * H, c0 + 512)
                    wlast = P.add("dve", lambda e, c0=c0, c1=c1: e.tensor_scalar(out=negF[:, c0:c1], in0=fps[:, c0:c1], scalar1=-1.0,
                                                                              scalar2=None, op0=ALU.mult), waits=[w_t, wlast], sig=True)
                P.barrier()
                P.emit()

        def phase2(li, typ):
            j_l = li // 2
            isB = typ == "B"
            with ExitStack() as st:
                sb, ps = mk_alloc(st)
                Ka = [sb("Ka%d" % i, [128, S], BF16) for i in range(2)]
                VTb = [sb("VTb%d" % i, [64, S], BF16) for i in range(2)]
                Vones = [sb("Vones%d" % i, [128, NT, 128], BF16) for i in range(2)]
                Qa = [sb("Qa%d" % i, [128, QB], BF16) for i in range(3)]
                Zb = [sb("Zb%d" % i, [64, QB], BF16) for i in range(3)]
                PW = 1024 if isB else 1280
                PT = [sb("PT%d" % i, [128, PW], BF16) for i in range(3)]
                rinv = sb("rinv", [128, QB], F32)
                t1 = sb("t1", [64, QB], F32)
                Gs = [sb("Gs%d" % i, [64, QB], BF16) for i in range(3)]
                PSW = 1024 if isB else 1536
                NSB = 3 if isB else 2
                if isB:
                    Sps = [ps("Sps%d" % i, [128, 1024], F32) for i in range(3)]
                    Ops = ps("Ops", [128, 1024], F32)
                    Osb = sb("Osb", [128, QB], F32)
                else:
                    Sps = [ps("Sps%d" % i, [128, 1536], F32) for i in range(2)]
                    OpsA = [ps("OpsA%d" % i, [128, 512], F32) for i in range(2)]
                    OsbA = [sb("OsbA%d" % i, [64, QB], F32) for i in range(2)]
                    rsA = [sb("rsA%d" % i, [64, QB], F32) for i in range(2)]
                    Mb = sb("Mb", [128, H, 640], BF16)
                    bstage = [sb("bstage%d" % i, [128, 640], F32) for i in range(2)]
                    maskf = sb("maskf", [128, 640], F32)

                w_init = None
                for i in range(2):
                    P.add("dve", lambda e, i=i: e.memset(Ka[i][64:65, :], 1.0), sig=True)
                    P.add("dve", lambda e, i=i: e.memset(Vones[i][:, :, 64:128], 1.0), sig=True)
                if not isB:
                    for i in range(3):
                        P.add("dve", lambda e, i=i: e.memset(Qa[i][64:65, :], 0.0), sig=True)
                w_init = (P.ech["dve"], P.ech["dve"].count)
                w_mb = None
                if not isB:
                    cbs = sb("cbs", [128, H], F32)
                    P.dma("sync", cbs[:, :], cb_d[j_l], c_misc)
                    w_mk = P.dma("sync", maskf[:, :], maskA_d, c_misc)
                    wb_prev = {}
                    for h in range(H):
                        wl = P.dma("sync", bstage[h % 2][:, :], biasT_d[j_l, h], c_x[h % 2], waits=[wb_prev.get(h - 2)])
                        wb_prev[h] = P.add("dve", lambda e, h=h: e.scalar_tensor_tensor(out=Mb[:, h, :], in0=bstage[h % 2][:, :], scalar=cbs[:, h:h + 1],
                                                                                      in1=maskf[:, :], op0=ALU.subtract, op1=ALU.add),
                                           waits=[wl, w_mk], sig=True)
                    w_mb = wb_prev[H - 1]

                pe_head_done = {}
                w_kv = {}
                w_t1_prev = [None]
                w_g_prev = [None]
                w_t1_blk = {}

                def load_kv(h):
                    hb = h % 2
                    P.dma("sync", Ka[hb][0:64, :], KT[h * 64:(h + 1) * 64, :], c_k[hb], waits=[pe_head_done.get(h - 2), w_init])
                    w_kv[h] = P.dma("sync", VTb[hb][:, :], VT[h * 64:(h + 1) * 64, :], c_k[hb], waits=[pe_head_done.get(h - 2)])

                qstate = {"n": 0}
                w_qload = {}
                w_qfree = {}
                w_gst = {}

                def load_q(h, qi):
                    qn = h * NQB + qi
                    qb = qn % 3
                    q0 = qi * QB
                    wf_ = w_qfree.get(qn - 3)
                    P.dma("sync", Qa[qb][0:64, :], QT[h * 64:(h + 1) * 64, q0:q0 + QB], c_q3[qb], waits=[wf_, w_init])
                    if isB:
                        P.dma("sync", Qa[qb][64:65, :], FT[h:h + 1, q0:q0 + QB], c_q3[qb], waits=[wf_])
                    w_qload[qn] = P.dma("sync", Zb[qb][:, :], ZT[h * 64:(h + 1) * 64, q0:q0 + QB], c_q3[qb], waits=[wf_])

                load_kv(0)
                upe = 0
                w_exp = {}
                w_pv = {}
                w_ops_free = None
                w_opsA = {}
                for h in range(H):
                    hb = h % 2
                    if h + 1 < H:
                        load_kv(h + 1)
                    vviews = [Sps[upe % NSB][:, :].bitcast(BF16), Sps[(upe + 1) % NSB][:, :].bitcast(BF16)]
                    G0 = (PSW * 2) // 64
                    assert NT <= 2 * G0
                    for kt in range(NT):
                        vv = vviews[kt // G0]
                        ko = kt % G0
                        w_t = P.add("pe", lambda e, kt=kt, vv=vv, ko=ko, hb=hb: e.transpose(
                            out=vv[:, ko * 64:(ko + 1) * 64],
                            in_=VTb[hb][0:64, kt * 128:(kt + 1) * 128], identity=identb[0:64, 0:64]),
                            waits=[w_kv[h], w_exp.get(upe - 3), w_exp.get(upe - 2), w_exp.get(upe - 1), pe_head_done.get(h - 2)], sig=(kt == NT - 1))
                    w_vlast = None
                    for g in range(0, NT, 16):
                        n = min(16, NT - g)
                        vv = vviews[g // G0]
                        go = g % G0
                        w_vlast = P.add("dve", lambda e, g=g, n=n, vv=vv, go=go, hb=hb: e.tensor_copy(
                            out=Vones[hb][:, g:g + n, 0:64],
                            in_=vv[:, go * 64:(go + n) * 64].rearrange("p (k d) -> p k d", d=64)),
                            waits=[w_t, w_init], sig=True)

                    units = []
                    if isB:
                        for qi in range(NQB):
                            nsub = QB // 128
                            nk = (qi + 1) * nsub
                            for kt in range(nk):
                                r = kt - qi * nsub
                                c0 = 128 * r if r > 0 else 0
                                u = dict(qi=qi, kt=kt, c0=c0, diag=(r >= 0), first=(kt == 0), last=(kt == nk - 1))
                                units.append(u)
                    else:
                        for m2 in range(NT // 2):
                            m0 = 2 * m2
                            qi = (m0 * 128) // QB
                            qoff = m0 * 128 - qi * QB
                            rows = []
                            for (dk, sc0, ncl, qlo, jlo, bseg) in ((-3, 256, 256, 0, 3, (128, 128)), (-2, 512, 256, 0, 2, None),
                                                                  (-1, 768, 256, 0, 1, (0, 128)), (0, 1024, 256, 0, 0, (0, 256)),
                                                                  (-4, 0, 128, 0, 4, (0, 128)), (1, 128, 128, 128, 0, (0, 128))):
                                kt = m0 + dk
                                if kt < 0:
                                    continue
                                rows.append((kt, sc0, ncl, qlo, jlo, bseg))
                            units.append(dict(qi=qi, qoff=qoff, rows=rows, m2=m2, last=((m0 + 2) * 128) % QB == 0))

                    def qn_of(u):
                        return h * NQB + u["qi"]

                    def emit_qk(ui, u):
                        g = upe + ui
                        sbuf_ = Sps[g % NSB]
                        qn = qn_of(u)
                        qa = Qa[qn % 3]
                        wts = [w_qload[qn], w_vlast, w_exp.get(g - NSB), w_init, w_mb]
                        if isB:
                            kt, c0 = u["kt"], u["c0"]
                            for lo in (0, 512):
                                hi = lo + 512
                                a = max(lo, c0)
                                if a >= hi or a >= QB:
                                    continue
                                hi = min(hi, QB)
                                dg = u["diag"] and (lo <= c0 < hi)
                                wq = P.add("pe", lambda e, sbuf_=sbuf_, kt=kt, a=a, hi=hi, qa=qa, dg=dg, hb=hb: e.matmul(
                                    sbuf_[:, a:hi], lhsT=Ka[hb][0:65, kt * 128:(kt + 1) * 128], rhs=qa[0:65, a:hi], start=True, stop=not dg),
                                    waits=wts, sig=True)
                                if dg:
                                    wq = P.add("pe", lambda e, sbuf_=sbuf_, c0=c0: e.matmul(
                                        sbuf_[:, c0:c0 + 128], lhsT=identb[:, :], rhs=trib[:, :], start=False, stop=True), sig=True)
                            return wq
                        else:
                            wq = None
                            nr = len(u["rows"])
                            for ri, (kt, sc0, ncl, qlo, jlo, bseg) in enumerate(u["rows"]):
                                qc = u["qoff"] + qlo
                                lastrow = ri == nr - 1
                                wq = P.add("pe", lambda e, sbuf_=sbuf_, kt=kt, sc0=sc0, ncl=ncl, qc=qc, qa=qa, hb=hb, bseg=bseg: e.matmul(
                                    sbuf_[:, sc0:sc0 + ncl], lhsT=Ka[hb][0:65, kt * 128:(kt + 1) * 128], rhs=qa[0:65, qc:qc + ncl],
                                    start=True, stop=(bseg is None)), waits=wts, sig=(lastrow and bseg is None))
                                if bseg is not None:
                                    bo, bn = bseg
                                    wq = P.add("pe", lambda e, sbuf_=sbuf_, sc0=sc0, bo=bo, bn=bn, jlo=jlo, h=h: e.matmul(
                                        sbuf_[:, sc0 + bo:sc0 + bo + bn], lhsT=identb[:, :], rhs=Mb[:, h, jlo * 128 + bo: jlo * 128 + bo + bn],
                                        start=False, stop=True), sig=lastrow)
                            return wq

                    def emit_exp(ui, u, wq):
                        g = upe + ui
                        sbuf_ = Sps[g % NSB]
                        pt = PT[g % 3]
                        wts = [wq, w_pv.get(g - 3)]
                        if isB:
                            kt, c0 = u["kt"], u["c0"]
                            w_exp[g] = P.add("act", lambda e, sbuf_=sbuf_, pt=pt, kt=kt, c0=c0, h=h: e.activation(
                                out=pt[:, c0:QB], in_=sbuf_[:, c0:QB], func=AF.Exp, bias=negF[:, kt * H + h: kt * H + h + 1], scale=1.0),
                                waits=wts, sig=True)
                        else:
                            segs = sorted((sc0, sc0 + ncl) for (kt, sc0, ncl, qlo, jlo, bseg) in u["rows"])
                            merged = []
                            for a, b_ in segs:
                                if merged and merged[-1][1] == a:
                                    merged[-1][1] = b_
                                else:
                                    merged.append([a, b_])
                            for a, b_ in merged:
                                w_exp[g] = P.add("act", lambda e, sbuf_=sbuf_, pt=pt, a=a, b_=b_: e.activation(
                                    out=pt[:, a:b_], in_=sbuf_[:, a:b_], func=AF.Exp), waits=wts, sig=True)

                    def emit_pv(ui, u):
                        nonlocal w_ops_free
                        g = upe + ui
                        pt = PT[g % 3]
                        qn = qn_of(u)
                        if isB:
                            kt, c0 = u["kt"], u["c0"]
                            wp = None
                            for lo in (0, 512):
                                hi = min(lo + 512, QB)
                                a = max(lo, c0)
                                if a >= hi:
                                    continue
                                nsub_ = QB // 128
                                last_kt = u["qi"] * nsub_ + min(nsub_, (lo // 512 + 1) * 4) - 1
                                wp = P.add("pe", lambda e, pt=pt, kt=kt, a=a, hi=hi, u=u, hb=hb, last_kt=last_kt: e.matmul(
                                    Ops[:, a:hi], lhsT=Vones[hb][:, kt, :], rhs=pt[:, a:hi], start=u["first"], stop=(kt == last_kt)),
                                    waits=[w_exp[g], w_ops_free if u["first"] else None], sig=True)
                            w_pv[g] = wp
                            if u["last"]:
                                pending.append(lambda u=u, qn=qn, wp=wp: finish_block_B(u, qn, wp))
                        else:
                            ob = u["m2"] % 2
                            rows = u["rows"]
                            wp = None
                            for ri, (kt, sc0, ncl, qlo, jlo, bseg) in enumerate(rows):
                                wp = P.add("pe", lambda e, pt=pt, kt=kt, sc0=sc0, ncl=ncl, qlo=qlo, ob=ob, ri=ri, nr=len(rows), hb=hb: e.matmul(
                                    OpsA[ob][:, qlo: qlo + ncl], lhsT=Vones[hb][:, kt, :], rhs=pt[:, sc0:sc0 + ncl],
                                    start=(ri == 0), stop=(ri == nr - 1)),
                                    waits=[w_exp[g], w_opsA.get(u["m2"] + h * (NT // 2) - 2) if ri == 0 else None], sig=True)
                            w_pv[g] = wp
                            pending.append(lambda u=u, qn=qn, wp=wp, ob=ob: finish_unit_A(u, qn, wp, ob))

                    def finish_block_B(u, qn, wp):
                        nonlocal w_ops_free
                        qb = qn % 3
                        q0 = u["qi"] * QB
                        wcp = P.add("dve", lambda e: e.tensor_copy(out=Osb[:, :], in_=Ops[:, :]), waits=[wp, w_t1_prev[0]], sig=True)
                        w_ops_free = wcp
                        wr = P.add("dve", lambda e: e.reciprocal(out=rinv[0:64, :], in_=Osb[64:128, :]), waits=[wcp], sig=True)
                        wt1 = P.add("dve", lambda e: e.tensor_tensor(out=t1[0:64, :], in0=Osb[0:64, :], in1=rinv[0:64, :], op=ALU.mult),
                                    waits=[wr, w_g_prev[0]], sig=True)
                        w_t1_prev[0] = wt1
                        wg = P.add("dve", lambda e, qb=qb: e.tensor_tensor(out=Gs[qb][:, :], in0=t1[0:64, :], in1=Zb[qb][:, :], op=ALU.mult),
                                   waits=[wt1, w_qload[qn], w_gst.get(qn - 3)], sig=True)
                        w_g_prev[0] = wg
                        w_qfree[qn] = wg
                        w_gst[qn] = P.dma("sync", GT[h * 64:(h + 1) * 64, q0:q0 + QB], Gs[qb][:, :], c_g3[qb], waits=[wg])

                    def finish_unit_A(u, qn, wp, ob):
                        qb = qn % 2
                        q0 = u["qi"] * QB
                        qoff = u["qoff"]
                        gi = u["m2"] + h * (NT // 2)
                        wc1 = P.add("dve", lambda e, ob=ob, qb=qb, qoff=qoff: e.tensor_copy(out=OsbA[qb][0:64, qoff:qoff + 256], in_=OpsA[ob][0:64, 0:256]),
                                    waits=[wp, w_t1_blk.get(qn - 2)], sig=True)
                        wc2 = P.add("dve", lambda e, ob=ob, qb=qb, qoff=qoff: e.tensor_copy(out=rsA[qb][0:64, qoff:qoff + 256], in_=OpsA[ob][64:128, 0:256]),
                                    waits=[wc1], sig=True)
                        w_opsA[gi] = wc2
                        if u["last"]:
                            wr0 = P.add("act", lambda e, qb=qb: e.activation(out=rsA[qb][0:64, :], in_=rsA[qb][0:64, :], func=AF.Ln), waits=[wc2], sig=True)
                            wr = P.add("act", lambda e, qb=qb: e.activation(out=rsA[qb][0:64, :], in_=rsA[qb][0:64, :], func=AF.Exp, scale=-1.0),
                                       waits=[wr0], sig=True)
                            wt1 = P.add("dve", lambda e, qb=qb: e.tensor_tensor(out=t1[0:64, :], in0=OsbA[qb][0:64, :], in1=rsA[qb][0:64, :], op=ALU.mult),
                                        waits=[wr, w_g_prev[0]], sig=True)
                            w_t1_blk[qn] = wt1
                            q3 = qn % 3
                            wg = P.add("pool", lambda e, q3=q3: e.tensor_tensor(out=Gs[q3][:, :], in0=t1[0:64, :], in1=Zb[q3][:, :], op=ALU.mult),
                                       waits=[wt1, w_qload[qn], w_gst.get(qn - 3)], sig=True)
                            w_g_prev[0] = wg
                            w_qfree[qn] = wg
                            w_gst[qn] = P.dma("sync", GT[h * 64:(h + 1) * 64, q0:q0 + QB], Gs[q3][:, :], c_g3[q3], waits=[wg])

                    nU = len(units)
                    wq_list = {}

                    def ensure_q(ui):
                        u = units[ui]
                        qn = h * NQB + u["qi"]
                        for qq in (qn, qn + 1):
                            if qq >= H * NQB or qq in loaded_all:
                                continue
                            if qq - 3 >= 0 and (qq - 3) not in w_qfree:
                                assert qq != qn, "q block needed but predecessor not finished"
                                continue
                            load_q(qq // NQB, qq % NQB)
                            loaded_all.add(qq)

                    for ui in range(min(NSB, nU)):
                        ensure_q(ui)
                        wq_list[ui] = emit_qk(ui, units[ui])
                        emit_exp(ui, units[ui], wq_list[ui])
                    pending = []
                    for ui in range(nU):
                        emit_pv(ui, units[ui])
                        if ui + NSB < nU:
                            ensure_q(ui + NSB)
                            wq_list[ui + NSB] = emit_qk(ui + NSB, units[ui + NSB])
                            emit_exp(ui + NSB, units[ui + NSB], wq_list[ui + NSB])
                        for f_ in pending:
                            f_()
                        pending.clear()
                    upe += nU
                    pe_head_done[h] = w_pv[upe - 1]
                P.barrier()
                P.emit()

        loaded_all = set()

        c3_w = P.chan("c3_w")
        c3_x = [P.chan("c3_x%d" % i) for i in range(3)]
        c3_gt = [P.chan("c3_gt0"), P.chan("c3_gt1")]
        NU3 = 6
        c3_st = [P.chan("c3_st%d" % i) for i in range(NU3)]
        c3_m = P.chan("c3_m")

        def phase3_main(li, typ, x_src, x_dst, st, lean, x_ready):
            j_l = li // 2
            w_out = (w_out_a if typ == "A" else w_out_b)[j_l]
            sb, ps = mk_alloc(st)
            NU = NU3
            NY = 1 if lean else 2
            Wo = sb("Wo", [128, 8, D], BF16)
            GTb = [sb("GTb%d" % i, [128, 8, 512], BF16) for i in range(2)]
            xres = [sb("xres%d" % i, [128, D], F32) for i in range(3)]
            ub = [sb("ub%d" % i, [128, D], F32) for i in range(NU)]
            gsb_ = sb("lng_s", [128, D], F32)
            bsb_ = sb("lnb_s", [128, D], F32)
            sm = [sb("sm%d" % i, [128, 8], F32) for i in range(NU)]
            sn = [sb("sn%d" % i, [128, 9], F32) for i in range(NU)]
            smi = [sb("smi%d" % i, [128, 2], I32) for i in range(NU)]
            sqs = sb("sqs", [128, D], BF16)
            yps = [ps("yps%d" % i, [128, D], F32) for i in range(NY)]
            w_sq_prev = [None]
            w_stA = {}
            w_stB = {}
            w_w = None
            for c in range(8):
                w_w = P.dma("pool", Wo[:, c, :], w_out[c * 128:(c + 1) * 128, :], c3_w)
            P.dma("sync", gsb_[:, :], lng_d[li], c3_m)
            w_gb = P.dma("sync", bsb_[:, :], lnb_d[li], c3_m)
            w_gt = {}
            w_xl = {}
            w_blk_pe = {}
            w_res = {}
            w_u_free = {}
            w_y_free = {}
            NTI = NB * 4

            def load_gt(b):
                w_gt[b] = P.dma("sync", GTb[b % 2][:, :, :], GT[:, b * 512:(b + 1) * 512].rearrange("(c p) s -> p c s", p=128),
                                c3_gt[b % 2], waits=[w_blk_pe.get(b - 2)])

            def load_xt(tu):
                w_xl[tu] = P.dma("sync", xres[tu % 3][:, :], x_src[tu * 128:(tu + 1) * 128, :], c3_x[tu % 3], waits=[w_res.get(tu - 3)])

            def stageA(tu):
                b, t = divmod(tu, 4)
                gt = GTb[b % 2]
                xr = xres[tu % 3]
                yp = yps[tu % NY]
                k = tu % NU
                u_ = ub[k]
                for half in range(2):
                    for c in range(8):
                        w_m = P.add("pe", lambda e, yp=yp, gt=gt, t=t, half=half, c=c: e.matmul(
                            yp[:, half * 512:(half + 1) * 512], lhsT=gt[:, c, t * 128:(t + 1) * 128],
                            rhs=Wo[:, c, half * 512:(half + 1) * 512], start=(c == 0), stop=(c == 7)),
                            waits=[w_w, w_gt[b], w_y_free.get(tu - NY)], sig=(c == 7 and half == 1))
                if t == 3:
                    w_blk_pe[b] = w_m
                    if b + 2 < NB:
                        load_gt(b + 2)
                w_r = P.add("dve", lambda e, u_=u_, xr=xr, yp=yp, k=k: e.scalar_tensor_tensor(
                    out=u_[:, :], in0=xr[:, :], scalar=ALPHA, in1=yp[:, :], op0=ALU.mult, op1=ALU.add, accum_out=sm[k][:, 0:1]),
                    waits=[w_m, w_xl[tu], w_u_free.get(tu - NU)], sig=True)
                w_y_free[tu] = w_r
                w_res[tu] = w_r
                if tu + 3 < NTI:
                    load_xt(tu + 3)
                w_q = P.add("act", lambda e, u_=u_, k=k: e.activation(out=sqs[:, :], in_=u_[:, :], func=AF.Square, accum_out=sm[k][:, 1:2]),
                            waits=[w_r, w_sq_prev[0]], sig=True)
                w_sq_prev[0] = w_q
                w_1 = P.add("dve", lambda e, k=k: e.tensor_scalar(out=sm[k][:, 2:3], in0=sm[k][:, 0:1], scalar1=1.0 / D, scalar2=None, op0=ALU.mult),
                            waits=[w_r], sig=True)
                w_2 = P.add("dve", lambda e, k=k: e.scalar_tensor_tensor(out=sm[k][:, 3:4], in0=sm[k][:, 2:3], scalar=-1.0, in1=sm[k][:, 2:3],
                                                                      op0=ALU.mult, op1=ALU.mult), waits=[w_1], sig=True)
                w_3 = P.add("dve", lambda e, k=k: e.scalar_tensor_tensor(out=sm[k][:, 4:5], in0=sm[k][:, 1:2], scalar=1.0 / D, in1=sm[k][:, 3:4],
                                                                      op0=ALU.mult, op1=ALU.add), waits=[w_2, w_q], sig=True)
                w_4 = P.add("dve", lambda e, k=k: e.tensor_scalar(out=sm[k][:, 4:5], in0=sm[k][:, 4:5], scalar1=EPS, scalar2=None, op0=ALU.add),
                            waits=[w_3], sig=True)
                ve = sm[k][:, 4:5]
                w_5 = P.add("dve", lambda e, k=k, ve=ve: e.tensor_scalar(out=smi[k][:, 0:1], in0=ve.bitcast(I32), scalar1=1, scalar2=None,
                                                                      op0=ALU.logical_shift_right), waits=[w_4], sig=True)
                w_6 = P.add("dve", lambda e, k=k: e.tensor_scalar(out=smi[k][:, 1:2], in0=smi[k][:, 0:1], scalar1=-1.0, scalar2=float(0x5f3759df),
                                                               op0=ALU.mult, op1=ALU.add), waits=[w_5], sig=True)
                ycur = smi[k][:, 1:2].bitcast(F32)
                wprev = w_6
                for it in range(3):
                    ta_ = sn[k][:, 3 * it:3 * it + 1]
                    tb_ = sn[k][:, 3 * it + 1:3 * it + 2]
                    ynew = sm[k][:, 6:7] if it == 2 else sn[k][:, 3 * it + 2:3 * it + 3]
                    wa = P.add("dve", lambda e, ycur=ycur, ve=ve, ta_=ta_: e.scalar_tensor_tensor(out=ta_, in0=ycur, scalar=ve, in1=ycur,
                                                                                              op0=ALU.mult, op1=ALU.mult), waits=[wprev], sig=True)
                    wb = P.add("dve", lambda e, ta_=ta_, tb_=tb_: e.tensor_scalar(out=tb_, in0=ta_, scalar1=-0.5, scalar2=1.5,
                                                                               op0=ALU.mult, op1=ALU.add), waits=[wa], sig=True)
                    wprev = P.add("dve", lambda e, ycur=ycur, tb_=tb_, ynew=ynew: e.tensor_tensor(out=ynew, in0=ycur, in1=tb_, op=ALU.mult),
                                  waits=[wb], sig=True)
                    ycur = ynew
                w_stA[tu] = wprev

            def stageB1(tu):
                k = tu % NU
                u_ = ub[k]
                w_7 = P.add("dve", lambda e, k=k: e.scalar_tensor_tensor(out=sm[k][:, 7:8], in0=sm[k][:, 2:3], scalar=-1.0, in1=sm[k][:, 6:7],
                                                                      op0=ALU.mult, op1=ALU.mult), waits=[w_stA[tu]], sig=True)
                w_stB[tu] = P.add("act", lambda e, u_=u_, k=k: e.activation(out=u_[:, :], in_=u_[:, :], func=AF.Identity,
                                                                            bias=sm[k][:, 7:8], scale=sm[k][:, 6:7]), waits=[w_7], sig=True)

            def stageB2(tu):
                b, t = divmod(tu, 4)
                k = tu % NU
                u_ = ub[k]
                w_g = P.add("dve", lambda e, u_=u_: e.tensor_tensor(out=u_[:, :], in0=u_[:, :], in1=gsb_[:, :], op=ALU.mult),
                            waits=[w_stB[tu], w_gb], sig=True)
                w_b = P.add("pool", lambda e, u_=u_: e.tensor_tensor(out=u_[:, :], in0=u_[:, :], in1=bsb_[:, :], op=ALU.add),
                            waits=[w_g], sig=True)
                w_s = P.dma("sync", x_dst[tu * 128:(tu + 1) * 128, :], u_[:, :], c3_st[k], waits=[w_b])
                w_u_free[tu] = w_s
                x_ready.setdefault(b, []).append(w_s)

            load_gt(0)
            if NB > 1:
                load_gt(1)
            for tu in range(min(3, NTI)):
                load_xt(tu)
            for tu in range(NTI + 2):
                if tu < NTI:
                    stageA(tu)
                if 0 <= tu - 1 < NTI:
                    stageB1(tu - 1)
                if 0 <= tu - 2 < NTI:
                    stageB2(tu - 2)
                yield

        import os
        kstop = int(os.environ.get("KSTOP", "99"))
        nofuse = bool(os.environ.get("KNOFUSE"))

        def run_all(gen):
            for _ in gen:
                pass

        x_src = x_d
        with ExitStack() as st:
            run_all(phase1_main(0, ltypes[0], x_src, st, False))
            P.barrier()
            P.emit()
        if ltypes[0] == "B":
            phase1_post(0)
        for li, typ in enumerate(ltypes):
            last = li == NL - 1
            loaded_all.clear()
            if kstop >= 2:
                phase2(li, typ)
            x_dst = y_d if last else xbuf
            if kstop >= 3:
                if last:
                    with ExitStack() as st:
                        run_all(phase3_main(li, typ, x_src, x_dst, st, False, {}))
                        P.barrier()
                        P.emit()
                elif nofuse:
                    with ExitStack() as st:
                        run_all(phase3_main(li, typ, x_src, x_dst, st, False, {}))
                        P.barrier()
                        P.emit()
                    with ExitStack() as st:
                        run_all(phase1_main(li + 1, ltypes[li + 1], xbuf, st, False))
                        P.barrier()
                        P.emit()
                else:
                    with ExitStack() as st:
                        x_ready = {}
                        g3 = phase3_main(li, typ, x_src, x_dst, st, True, x_ready)
                        g1 = phase1_main(li + 1, ltypes[li + 1], xbuf, st, True, x_ready)
                        LEAD = 14
                        done3 = False
                        done1 = False
                        for _ in range(LEAD):
                            if next(g3, "end") == "end":
                                done3 = True
                                break
                        while not (done1 and done3):
                            if not done1:
                                if next(g1, "end") == "end":
                                    done1 = True
                            if not done3:
                                if next(g3, "end") == "end":
                                    done3 = True
                        P.barrier()
                        P.emit()
                if not last and ltypes[li + 1] == "B":
                    phase1_post(li + 1)
            x_src = xbuf
        print("ops recorded:", P.nops)
    return nc


def host_consts():
    ident = np.eye(128, dtype=np.float32)
    k = np.arange(128)[:, None]
    q = np.arange(128)[None, :]
    tri = np.where(k > q, NEG, 0.0).astype(np.float32)
    maskA = np.zeros((128, 5, 128), np.float32)
    maskA[64:, 0, :64] = NEG
    maskA[:64, 4, 64:] = NEG
    return ident, tri, maskA.reshape(128, 640)


def host_bias_layout(rel_bias_a):
    k = np.arange(128)[:, None, None]
    j = np.arange(5)[None, :, None]
    q = np.arange(128)[None, None, :]
    idx = np.clip(128 * j + q - k, -128, 128) + 128
    out = rel_bias_a[:, :, idx]
    return np.ascontiguousarray(out.reshape(rel_bias_a.shape[0], H, 128, 640)).astype(np.float32)


def make_common(w_in_a, rel_bias_a, w_out_a, w_in_b, w_f_b, b_f_b, w_out_b, ln_g, ln_b):
    ident, tri, maskA = host_consts()
    f = lambda a: np.ascontiguousarray(np.asarray(a, dtype=np.float32))
    return {
        "w_in_a": f(w_in_a), "w_in_b": f(w_in_b), "w_out_a": f(w_out_a), "w_out_b": f(w_out_b),
        "w_f_b": f(w_f_b), "bfb": f(np.asarray(b_f_b)[:, :, None]),
        "biasT": host_bias_layout(np.asarray(rel_bias_a, dtype=np.float32)),
        "lng": f(np.broadcast_to(np.asarray(ln_g)[:, None, :], (4, 128, D))),
        "lnb": f(np.broadcast_to(np.asarray(ln_b)[:, None, :], (4, 128, D))),
        "ident": ident, "tri": tri, "maskA": maskA,
        "cbias": f(np.broadcast_to(np.asarray(rel_bias_a, dtype=np.float32)[:, None, :, 256], (2, 128, H))),
    }


def kernel(x, w_in_a, rel_bias_a, w_out_a, w_in_b, w_f_b, b_f_b, w_out_b, ln_g, ln_b):
    x = np.asarray(x, dtype=np.float32)
    common = make_common(w_in_a, rel_bias_a, w_out_a, w_in_b, w_f_b, b_f_b, w_out_b, ln_g, ln_b)
    nc = build(SEQ, "ABAB")
    in_maps = []
    for c in range(NCORES):
        m = dict(common)
        m["x"] = np.ascontiguousarray(x[c])
        in_maps.append(m)
    res = run_bass_kernel_spmd(nc, in_maps, core_ids=list(range(NCORES)))
    return np.stack([np.asarray(r["y"], dtype=np.float32) for r in res.results], axis=0)
```

```python
import numpy as np
from contextlib import ExitStack
import concourse.bass as bass
import concourse.mybir as mybir
from concourse.bass_utils import run_bass_kernel_spmd

F32 = mybir.dt.float32
BF16 = mybir.dt.bfloat16
I32 = mybir.dt.int32
AF = mybir.ActivationFunctionType
ALU = mybir.AluOpType

D = 1024
H = 16
NEG = -30000.0
ALPHA = 8.0 ** 0.25
EPS = 1e-5
SEQ = 8192
NCORES = 8
import os
KSKIP = os.environ.get("KSKIP", "").split(",")


class Chan:
    def __init__(self, sem, name):
        self.sem = sem
        self.count = 0
        self.name = name


class Prog:
    ENG = ("sync", "act", "pe", "dve", "pool")

    def __init__(self, nc, stack):
        self.nc = nc
        self.stack = stack
        self.q = {e: [] for e in self.ENG}
        self.waited = {}
        self.chans = []
        self.ech = {e: self.chan("e_" + e) for e in ("act", "pe", "dve", "pool")}
        self.nops = 0

    def chan(self, name):
        sem = self.stack.enter_context(self.nc.semaphore(name))
        c = Chan(sem, name)
        self.chans.append(c)
        return c

    def add(self, eng, fn, waits=(), sig=None, inc=1):
        ws = []
        for w in waits:
            if w is None:
                continue
            ch, v = w
            if v <= 0:
                continue
            key = (eng, ch.name)
            if self.waited.get(key, 0) >= v:
                continue
            self.waited[key] = v
            ws.append((ch, v))
        ch = None
        if sig is True:
            ch = self.ech[eng]
        elif sig:
            ch = sig
        ret = None
        if ch is not None:
            ch.count += inc
            ret = (ch, ch.count)
        self.q[eng].append((fn, ws, ch, inc))
        self.nops += 1
        return ret

    def dma(self, eng, out, in_, ch, waits=()):
        return self.add(eng, lambda e: e.dma_start(out=out, in_=in_), waits=waits, sig=ch, inc=16)

    def barrier(self):
        ws = [(c, c.count) for c in self.chans if c.count > 0]
        for e in self.ENG:
            self.add(e, None, waits=ws)

    def emit(self):
        nc = self.nc
        q = self.q
        self.q = {e: [] for e in self.ENG}
        with nc.Block() as block:
            def mk(name):
                def run(eng):
                    for fn, ws, ch, inc in q[name]:
                        for c, v in ws:
                            eng.wait_ge(c.sem, v)
                        if fn is None:
                            continue
                        ins = fn(eng)
                        if ch is not None:
                            ins.then_inc(ch.sem, inc)
                return run
            block.sync(mk("sync"))
            block.scalar(mk("act"))
            block.tensor(mk("pe"))
            block.vector(mk("dve"))
            block.gpsimd(mk("pool"))


def build(S, ltypes, dbg=None):
    nc = bass.Bass("TRN2", target_bir_lowering=False)
    NL = len(ltypes)
    NT = S // 128
    NB = S // 512
    QB = min(1024, S)
    NQB = S // QB

    def din(name, shape, dt=F32):
        return nc.dram_tensor(name, shape, dt, kind="ExternalInput").ap()

    x_d = din("x", [S, D])
    w_in_a = din("w_in_a", [2, D, 4 * D])
    w_in_b = din("w_in_b", [2, D, 4 * D])
    w_out_a = din("w_out_a", [2, D, D])
    w_out_b = din("w_out_b", [2, D, D])
    w_f_b = din("w_f_b", [2, D, H])
    bfb_d = din("bfb", [2, H, 1])
    biasT_d = din("biasT", [2, H, 128, 640])
    lng_d = din("lng", [4, 128, D])
    lnb_d = din("lnb", [4, 128, D])
    ident_d = din("ident", [128, 128])
    tri_d = din("tri", [128, 128])
    maskA_d = din("maskA", [128, 640])
    cb_d = din("cbias", [2, 128, H])
    y_d = nc.dram_tensor("y", [S, D], F32, kind="ExternalOutput").ap()

    xbuf = nc.dram_tensor("xbuf", [S, D], F32).ap()
    QT = nc.dram_tensor("QT", [D, S], BF16).ap()
    KT = nc.dram_tensor("KT", [D, S], BF16).ap()
    VT = nc.dram_tensor("VT", [D, S], BF16).ap()
    ZT = nc.dram_tensor("ZT", [D, S], BF16).ap()
    GT = nc.dram_tensor("GT", [D, S], BF16).ap()
    FT = nc.dram_tensor("FT", [H, S], BF16).ap()

    with ExitStack() as gst:
        uid = [0]

        def mk_alloc(st):
            def sb(name, shape, dt):
                uid[0] += 1
                return st.enter_context(nc.sbuf_tensor("%s_%d" % (name, uid[0]), shape, dt))

            def ps(name, shape, dt):
                uid[0] += 1
                return st.enter_context(nc.psum_tensor("%s_%d" % (name, uid[0]), shape, dt))
            return sb, ps

        gsb, gps = mk_alloc(gst)
        P = Prog(nc, gst)

        identf = gsb("identf", [128, 128], F32)
        identb = gsb("identb", [128, 128], BF16)
        trib = gsb("trib", [128, 128], BF16)
        negF = gsb("negF", [128, NT * H], F32)
        c_const = P.chan("c_const")
        with ExitStack() as st:
            sb, ps = mk_alloc(st)
            trif = sb("trif", [128, 128], F32)
            P.dma("sync", identf[:, :], ident_d, c_const)
            w = P.dma("sync", trif[:, :], tri_d, c_const)
            P.add("dve", lambda e: e.tensor_copy(out=identb[:, :], in_=identf[:, :]), waits=[w], sig=True)
            P.add("dve", lambda e: e.tensor_copy(out=trib[:, :], in_=trif[:, :]), sig=True)
            P.barrier()
            P.emit()

        c_w = P.chan("c_w")
        c_x = [P.chan("c_x0"), P.chan("c_x1")]
        c_g = [P.chan("c_g0"), P.chan("c_g1")]
        c_st = [P.chan("c_st%d" % i) for i in range(4)]
        c_k = [P.chan("c_k0"), P.chan("c_k1")]
        c_q3 = [P.chan("c_q3_%d" % i) for i in range(3)]
        c_g3 = [P.chan("c_g3_%d" % i) for i in range(3)]
        c_misc = P.chan("c_misc")

        ZF = nc.dram_tensor("ZF", [H, S], F32).ap()
        c1_w = P.chan("c1_w")
        c1_x = [P.chan("c1_x0"), P.chan("c1_x1")]
        c1_st = [P.chan("c1_st%d" % i) for i in range(4)]
        c1_zf = [P.chan("c1_zf0"), P.chan("c1_zf1")]
        c1_m = P.chan("c1_m")

        def phase1_main(li, typ, x_src, st, lean, x_ready=None):
            j_l = li // 2
            w_in = (w_in_a if typ == "A" else w_in_b)[j_l]
            isB = typ == "B"
            sb, ps = mk_alloc(st)
            NXIN = 1 if lean else 2
            NMM = 3 if lean else 4
            Wb = sb("Wb", [128, 8, 4 * D], BF16)
            xin = [sb("xin%d" % i, [128, 4, D], F32) for i in range(NXIN)]
            xT = [sb("xT%d" % i, [128, 8, 512], BF16) for i in range(2)]
            osb = [sb("osb%d" % i, [128, 512], BF16) for i in range(4)]
            tp = [ps("tp%d" % i, [128, 512], F32) for i in range(2)]
            mm = [ps("mm%d" % i, [128, 512], F32) for i in range(NMM)]
            if isB:
                xTf = sb("xTf", [128, 8, 512], F32)
                wf = sb("wf", [128, 8, H], F32)
                bfs = sb("bfs", [H, 1], F32)
                zfs = [sb("zfs%d" % i, [H, 512], F32) for i in range(2)]
                zps = ps("zps", [H, 512], F32)
            w_w = None
            for c in range(8):
                for hh in range(2):
                    w_w = P.dma("pool", Wb[:, c, hh * 2048:(hh + 1) * 2048],
                                w_in[c * 128:(c + 1) * 128, hh * 2048:(hh + 1) * 2048], c1_w)
            w_wf = None
            if isB:
                P.dma("sync", wf[:, :, :], w_f_b[j_l].rearrange("(c p) h -> p c h", p=128), c1_m)
                w_wf = P.dma("sync", bfs[:, :], bfb_d[j_l], c1_m)
            w_tr_done = {}
            w_x = {}

            def load_x(b):
                ws = [w_tr_done.get(b - NXIN)]
                if x_ready is not None:
                    assert b in x_ready, "x block %d not yet produced" % b
                    ws = ws + list(x_ready[b])
                w_x[b] = P.dma("sync", xin[b % NXIN][:, :, :],
                               x_src[b * 512:(b + 1) * 512, :].rearrange("(t p) d -> p t d", p=128),
                               c1_x[b % 2], waits=ws)
            for b in range(min(NXIN, NB)):
                load_x(b)
            ev_tp = {}
            cnt = {"tpu": 0, "mmu": 0, "stu": 0}
            ev_mm = {}
            st_w = {}
            zst = {"mm_prev": None, "ev_prev": None}
            w_zf_st = {}
            w_ev_blk = {}

            def do_transposes(b):
                xi = xin[b % NXIN]
                xt = xT[b % 2]
                w_ev_last = None
                w_evf_last = None
                for c in range(8):
                    tpu = cnt["tpu"]
                    tb = tp[tpu % 2]
                    for t in range(4):
                        w_t = P.add("pe", lambda e, tb=tb, xi=xi, t=t, c=c: e.transpose(
                            out=tb[:, t * 128:(t + 1) * 128], in_=xi[:, t, c * 128:(c + 1) * 128], identity=identf[:, :]),
                            waits=[w_x[b]] + (ev_tp.get(tpu - 2) or []), sig=(t == 3))
                    if c == 7:
                        w_tr_done[b] = w_t
                    wl = []
                    w_ev_last = P.add("dve", lambda e, tb=tb, xt=xt, c=c: e.tensor_copy(out=xt[:, c, :], in_=tb[:, :]),
                                      waits=[w_t], sig=True)
                    wl.append(w_ev_last)
                    if isB:
                        w_evf_last = P.add("act", lambda e, tb=tb, c=c: e.activation(out=xTf[:, c, :], in_=tb[:, :], func=AF.Identity),
                                           waits=[w_t, zst["mm_prev"], w_ev_last], sig=True)
                        wl.append(w_evf_last)
                    ev_tp[tpu] = wl
                    cnt["tpu"] += 1
                w_ev_blk[b] = w_ev_last
                if b + NXIN < NB:
                    load_x(b + NXIN)
                if isB:
                    for c in range(8):
                        w_zf_mm = P.add("pe", lambda e, c=c: e.matmul(zps[:, :], lhsT=wf[:, c, :], rhs=xTf[:, c, :],
                                                                     start=(c == 0), stop=(c == 7)),
                                        waits=[w_evf_last, w_wf, zst["ev_prev"]], sig=(c == 7))
                    zst["mm_prev"] = w_zf_mm
                    zs = zfs[b % 2]
                    zst["ev_prev"] = P.add("dve", lambda e, zs=zs: e.tensor_scalar(
                        out=zs[:, :], in0=zps[:, :], scalar1=bfs[:, 0:1], scalar2=None, op0=ALU.add),
                        waits=[w_zf_mm, w_zf_st.get(b - 2)], sig=True)
                    w_zf_st[b] = P.dma("sync", ZF[:, b * 512:(b + 1) * 512], zs[:, :], c1_zf[b % 2], waits=[zst["ev_prev"]])

            do_transposes(0)
            for b in range(NB):
                xt = xT[b % 2]
                for j in range(32):
                    mmu = cnt["mmu"]
                    stu = cnt["stu"]
                    mb = mm[mmu % NMM]
                    for c in range(8):
                        w_m = P.add("pe", lambda e, mb=mb, xt=xt, j=j, c=c: e.matmul(
                            mb[:, :], lhsT=Wb[:, c, j * 128:(j + 1) * 128], rhs=xt[:, c, :], start=(c == 0), stop=(c == 7)),
                            waits=[w_w, w_ev_blk[b], ev_mm.get(mmu - NMM)], sig=(c == 7))
                    ob = osb[stu % 4]
                    wst = st_w.get(stu - 4)
                    if j < 8:
                        w_e = P.add("act", lambda e, mb=mb, ob=ob: e.activation(out=ob[:, :], in_=mb[:, :], func=AF.Copy, scale=0.125),
                                    waits=[w_m, wst], sig=True)
                        dst = QT
                    elif j < 24:
                        if lean:
                            w_e = P.add("act", lambda e, mb=mb, ob=ob: e.activation(out=ob[:, :], in_=mb[:, :], func=AF.Copy),
                                        waits=[w_m, wst], sig=True)
                        else:
                            w_e = P.add("dve", lambda e, mb=mb, ob=ob: e.tensor_copy(out=ob[:, :], in_=mb[:, :]),
                                        waits=[w_m, wst], sig=True)
                        dst = KT if j < 16 else VT
                    else:
                        w_e = P.add("act", lambda e, mb=mb, ob=ob: e.activation(out=ob[:, :], in_=mb[:, :], func=AF.Silu),
                                    waits=[w_m, wst], sig=True)
                        dst = ZT
                    ev_mm[mmu] = w_e
                    cnt["mmu"] += 1
                    jj = j % 8
                    st_w[stu] = P.dma("sync", dst[jj * 128:(jj + 1) * 128, b * 512:(b + 1) * 512], ob[:, :], c1_st[stu % 4], waits=[w_e])
                    cnt["stu"] += 1
                    if j == 15 and b + 1 < NB:
                        do_transposes(b + 1)
                    if j % 8 == 7:
                        yield

        def phase1_post(li):
            with ExitStack() as st:
                sb, ps = mk_alloc(st)
                zfT = sb("zfT", [H, S], F32)
                Fs = sb("Fs", [H, S], F32)
                Fb = sb("Fb", [H, S], BF16)
                CH = min(2048, S)
                ones = sb("ones", [H, CH], F32)
                fps = ps("fps", [128, NT * H], F32)
                w_ld = P.dma("sync", zfT[:, :], ZF, c1_m)
                w_o = P.add("dve", lambda e: e.memset(ones[:, :], 1.0), sig=True)
                w1 = P.add("act", lambda e: e.activation(out=zfT[:, :], in_=zfT[:, :], func=AF.Exp, scale=-1.0), waits=[w_ld], sig=True)
                w2 = P.add("act", lambda e: e.activation(out=zfT[:, :], in_=zfT[:, :], func=AF.Ln, bias=1.0), waits=[w1], sig=True)
                wprev = w2
                for ci in range(S // CH):
                    init = 0.0 if ci == 0 else Fs[:, ci * CH - 1:ci * CH]
                    wprev = P.add("dve", lambda e, ci=ci, init=init: e.tensor_tensor_scan(
                        out=Fs[:, ci * CH:(ci + 1) * CH], data0=ones[:, :], data1=zfT[:, ci * CH:(ci + 1) * CH],
                        initial=init, op0=ALU.mult, op1=ALU.subtract), waits=[wprev, w_o], sig=True)
                w_fb = P.add("dve", lambda e: e.tensor_copy(out=Fb[:, :], in_=Fs[:, :]), waits=[wprev], sig=True)
                P.dma("sync", FT, Fb[:, :], c1_m, waits=[w_fb])
                for kt in range(NT):
                    w_t = P.add("pe", lambda e, kt=kt: e.transpose(out=fps[:, kt * H:(kt + 1) * H], in_=Fs[:, kt * 128:(kt + 1) * 128],
                                                                 identity=identf[0:H, 0:H]), waits=[wprev], sig=(kt == NT - 1))
                wlast = w_t
                for c0 in range(0, NT * H, 512):
                    c1 = min(NT * H, c0 + 512)
                    wlast = P.add("dve", lambda e, c0=c0, c1=c1: e.tensor_scalar(out=negF[:, c0:c1], in0=fps[:, c0:c1], scalar1=-1.0,
                                                                              scalar2=None, op0=ALU.mult), waits=[w_t, wlast], sig=True)
                P.barrier()
                P.emit()

        def phase2(li, typ):
            j_l = li // 2
            isB = typ == "B"
            with ExitStack() as st:
                sb, ps = mk_alloc(st)
                Ka = [sb("Ka%d" % i, [128, S], BF16) for i in range(2)]
                VTb = [sb("VTb%d" % i, [64, S], BF16) for i in range(2)]
                Vones = [sb("Vones%d" % i, [128, NT, 128], BF16) for i in range(2)]
                Qa = [sb("Qa%d" % i, [128, QB], BF16) for i in range(3)]
                Zb = [sb("Zb%d" % i, [64, QB], BF16) for i in range(3)]
                PW = 1024 if isB else 1280
                PT = [sb("PT%d" % i, [128, PW], BF16) for i in range(3)]
                rinv = sb("rinv", [128, QB], F32)
                t1 = sb("t1", [64, QB], F32)
                Gs = [sb("Gs%d" % i, [64, QB], BF16) for i in range(3)]
                PSW = 1024 if isB else 1536
                NSB = 3 if isB else 2
                if isB:
                    Sps = [ps("Sps%d" % i, [128, 1024], F32) for i in range(3)]
                    Ops = ps("Ops", [128, 1024], F32)
                    Osb = sb("Osb", [128, QB], F32)
                else:
                    Sps = [ps("Sps%d" % i, [128, 1536], F32) for i in range(2)]
                    OpsA = [ps("OpsA%d" % i, [128, 512], F32) for i in range(2)]
                    OsbA = [sb("OsbA%d" % i, [64, QB], F32) for i in range(2)]
                    rsA = [sb("rsA%d" % i, [64, QB], F32) for i in range(2)]
                    Mb = sb("Mb", [128, H, 640], BF16)
                    bstage = [sb("bstage%d" % i, [128, 640], F32) for i in range(2)]
                    maskf = sb("maskf", [128, 640], F32)

                w_init = None
                for i in range(2):
                    P.add("dve", lambda e, i=i: e.memset(Ka[i][64:65, :], 1.0), sig=True)
                    P.add("dve", lambda e, i=i: e.memset(Vones[i][:, :, 64:128], 1.0), sig=True)
                if not isB:
                    for i in range(3):
                        P.add("dve", lambda e, i=i: e.memset(Qa[i][64:65, :], 0.0), sig=True)
                w_init = (P.ech["dve"], P.ech["dve"].count)
                w_mb = None
                if not isB:
                    cbs = sb("cbs", [128, H], F32)
                    P.dma("sync", cbs[:, :], cb_d[j_l], c_misc)
                    w_mk = P.dma("sync", maskf[:, :], maskA_d, c_misc)
                    wb_prev = {}
                    for h in range(H):
                        wl = P.dma("sync", bstage[h % 2][:, :], biasT_d[j_l, h], c_x[h % 2], waits=[wb_prev.get(h - 2)])
                        wb_prev[h] = P.add("dve", lambda e, h=h: e.scalar_tensor_tensor(out=Mb[:, h, :], in0=bstage[h % 2][:, :], scalar=cbs[:, h:h + 1],
                                                                                      in1=maskf[:, :], op0=ALU.subtract, op1=ALU.add),
                                           waits=[wl, w_mk], sig=True)
                    w_mb = wb_prev[H - 1]

                pe_head_done = {}
                w_kv = {}
                w_t1_prev = [None]
                w_g_prev = [None]
                w_t1_blk = {}

                def load_kv(h):
                    hb = h % 2
                    P.dma("sync", Ka[hb][0:64, :], KT[h * 64:(h + 1) * 64, :], c_k[hb], waits=[pe_head_done.get(h - 2), w_init])
                    w_kv[h] = P.dma("sync", VTb[hb][:, :], VT[h * 64:(h + 1) * 64, :], c_k[hb], waits=[pe_head_done.get(h - 2)])

                qstate = {"n": 0}
                w_qload = {}
                w_qfree = {}
                w_gst = {}

                def load_q(h, qi):
                    qn = h * NQB + qi
                    qb = qn % 3
                    q0 = qi * QB
                    wf_ = w_qfree.get(qn - 3)
                    P.dma("sync", Qa[qb][0:64, :], QT[h * 64:(h + 1) * 64, q0:q0 + QB], c_q3[qb], waits=[wf_, w_init])
                    if isB:
                        P.dma("sync", Qa[qb][64:65, :], FT[h:h + 1, q0:q0 + QB], c_q3[qb], waits=[wf_])
                    w_qload[qn] = P.dma("sync", Zb[qb][:, :], ZT[h * 64:(h + 1) * 64, q0:q0 + QB], c_q3[qb], waits=[wf_])

                load_kv(0)
                upe = 0
                w_exp = {}
                w_pv = {}
                w_ops_free = None
                w_opsA = {}
                for h in range(H):
                    hb = h % 2
                    if h + 1 < H:
                        load_kv(h + 1)
                    vviews = [Sps[upe % NSB][:, :].bitcast(BF16), Sps[(upe + 1) % NSB][:, :].bitcast(BF16)]
                    G0 = (PSW * 2) // 64
                    assert NT <= 2 * G0
                    for kt in range(NT):
                        vv = vviews[kt // G0]
                        ko = kt % G0
                        w_t = P.add("pe", lambda e, kt=kt, vv=vv, ko=ko, hb=hb: e.transpose(
                            out=vv[:, ko * 64:(ko + 1) * 64],
                            in_=VTb[hb][0:64, kt * 128:(kt + 1) * 128], identity=identb[0:64, 0:64]),
                            waits=[w_kv[h], w_exp.get(upe - 3), w_exp.get(upe - 2), w_exp.get(upe - 1), pe_head_done.get(h - 2)], sig=(kt == NT - 1))
                    w_vlast = None
                    for g in range(0, NT, 16):
                        n = min(16, NT - g)
                        vv = vviews[g // G0]
                        go = g % G0
                        w_vlast = P.add("dve", lambda e, g=g, n=n, vv=vv, go=go, hb=hb: e.tensor_copy(
                            out=Vones[hb][:, g:g + n, 0:64],
                            in_=vv[:, go * 64:(go + n) * 64].rearrange("p (k d) -> p k d", d=64)),
                            waits=[w_t, w_init], sig=True)

                    units = []
                    if isB:
                        for qi in range(NQB):
                            nsub = QB // 128
                            nk = (qi + 1) * nsub
                            for kt in range(nk):
                                r = kt - qi * nsub
                                c0 = 128 * r if r > 0 else 0
                                u = dict(qi=qi, kt=kt, c0=c0, diag=(r >= 0), first=(kt == 0), last=(kt == nk - 1))
                                units.append(u)
                    else:
                        for m2 in range(NT // 2):
                            m0 = 2 * m2
                            qi = (m0 * 128) // QB
                            qoff = m0 * 128 - qi * QB
                            rows = []
                            for (dk, sc0, ncl, qlo, jlo, bseg) in ((-3, 256, 256, 0, 3, (128, 128)), (-2, 512, 256, 0, 2, None),
                                                                  (-1, 768, 256, 0, 1, (0, 128)), (0, 1024, 256, 0, 0, (0, 256)),
                                                                  (-4, 0, 128, 0, 4, (0, 128)), (1, 128, 128, 128, 0, (0, 128))):
                                kt = m0 + dk
                                if kt < 0:
                                    continue
                                rows.append((kt, sc0, ncl, qlo, jlo, bseg))
                            units.append(dict(qi=qi, qoff=qoff, rows=rows, m2=m2, last=((m0 + 2) * 128) % QB == 0))

                    def qn_of(u):
                        return h * NQB + u["qi"]

                    def emit_qk(ui, u):
                        g = upe + ui
                        sbuf_ = Sps[g % NSB]
                        qn = qn_of(u)
                        qa = Qa[qn % 3]
                        wts = [w_qload[qn], w_vlast, w_exp.get(g - NSB), w_init, w_mb]
                        if isB:
                            kt, c0 = u["kt"], u["c0"]
                            for lo in (0, 512):
                                hi = lo + 512
                                a = max(lo, c0)
                                if a >= hi or a >= QB:
                                    continue
                                hi = min(hi, QB)
                                dg = u["diag"] and (lo <= c0 < hi)
                                wq = P.add("pe", lambda e, sbuf_=sbuf_, kt=kt, a=a, hi=hi, qa=qa, dg=dg, hb=hb: e.matmul(
                                    sbuf_[:, a:hi], lhsT=Ka[hb][0:65, kt * 128:(kt + 1) * 128], rhs=qa[0:65, a:hi], start=True, stop=not dg),
                                    waits=wts, sig=True)
                                if dg:
                                    wq = P.add("pe", lambda e, sbuf_=sbuf_, c0=c0: e.matmul(
                                        sbuf_[:, c0:c0 + 128], lhsT=identb[:, :], rhs=trib[:, :], start=False, stop=True), sig=True)
                            return wq
                        else:
                            wq = None
                            nr = len(u["rows"])
                            for ri, (kt, sc0, ncl, qlo, jlo, bseg) in enumerate(u["rows"]):
                                qc = u["qoff"] + qlo
                                lastrow = ri == nr - 1
                                wq = P.add("pe", lambda e, sbuf_=sbuf_, kt=kt, sc0=sc0, ncl=ncl, qc=qc, qa=qa, hb=hb, bseg=bseg: e.matmul(
                                    sbuf_[:, sc0:sc0 + ncl], lhsT=Ka[hb][0:65, kt * 128:(kt + 1) * 128], rhs=qa[0:65, qc:qc + ncl],
                                    start=True, stop=(bseg is None)), waits=wts, sig=(lastrow and bseg is None))
                                if bseg is not None:
                                    bo, bn = bseg
                                    wq = P.add("pe", lambda e, sbuf_=sbuf_, sc0=sc0, bo=bo, bn=bn, jlo=jlo, h=h: e.matmul(
                                        sbuf_[:, sc0 + bo:sc0 + bo + bn], lhsT=identb[:, :], rhs=Mb[:, h, jlo * 128 + bo: jlo * 128 + bo + bn],
                                        start=False, stop=True), sig=lastrow)
                            return wq

                    def emit_exp(ui, u, wq):
                        g = upe + ui
                        sbuf_ = Sps[g % NSB]
                        pt = PT[g % 3]
                        wts = [wq, w_pv.get(g - 3)]
                        if isB:
                            kt, c0 = u["kt"], u["c0"]
                            w_exp[g] = P.add("act", lambda e, sbuf_=sbuf_, pt=pt, kt=kt, c0=c0, h=h: e.activation(
                                out=pt[:, c0:QB], in_=sbuf_[:, c0:QB], func=AF.Exp, bias=negF[:, kt * H + h: kt * H + h + 1], scale=1.0),
                                waits=wts, sig=True)
                        else:
                            segs = sorted((sc0, sc0 + ncl) for (kt, sc0, ncl, qlo, jlo, bseg) in u["rows"])
                            merged = []
                            for a, b_ in segs:
                                if merged and merged[-1][1] == a:
                                    merged[-1][1] = b_
                                else:
                                    merged.append([a, b_])
                            for a, b_ in merged:
                                w_exp[g] = P.add("act", lambda e, sbuf_=sbuf_, pt=pt, a=a, b_=b_: e.activation(
                                    out=pt[:, a:b_], in_=sbuf_[:, a:b_], func=AF.Exp), waits=wts, sig=True)

                    def emit_pv(ui, u):
                        nonlocal w_ops_free
                        g = upe + ui
                        pt = PT[g % 3]
                        qn = qn_of(u)
                        if isB:
                            kt, c0 = u["kt"], u["c0"]
                            wp = None
                            for lo in (0, 512):
                                hi = min(lo + 512, QB)
                                a = max(lo, c0)
                                if a >= hi:
                                    continue
                                nsub_ = QB // 128
                                last_kt = u["qi"] * nsub_ + min(nsub_, (lo // 512 + 1) * 4) - 1
                                wp = P.add("pe", lambda e, pt=pt, kt=kt, a=a, hi=hi, u=u, hb=hb, last_kt=last_kt: e.matmul(
                                    Ops[:, a:hi], lhsT=Vones[hb][:, kt, :], rhs=pt[:, a:hi], start=u["first"], stop=(kt == last_kt)),
                                    waits=[w_exp[g], w_ops_free if u["first"] else None], sig=True)
                            w_pv[g] = wp
                            if u["last"]:
                                pending.append(lambda u=u, qn=qn, wp=wp: finish_block_B(u, qn, wp))
                        else:
                            ob = u["m2"] % 2
                            rows = u["rows"]
                            wp = None
                            for ri, (kt, sc0, ncl, qlo, jlo, bseg) in enumerate(rows):
                                wp = P.add("pe", lambda e, pt=pt, kt=kt, sc0=sc0, ncl=ncl, qlo=qlo, ob=ob, ri=ri, nr=len(rows), hb=hb: e.matmul(
                                    OpsA[ob][:, qlo: qlo + ncl], lhsT=Vones[hb][:, kt, :], rhs=pt[:, sc0:sc0 + ncl],
                                    start=(ri == 0), stop=(ri == nr - 1)),
                                    waits=[w_exp[g], w_opsA.get(u["m2"] + h * (NT // 2) - 2) if ri == 0 else None], sig=True)
                            w_pv[g] = wp
                            pending.append(lambda u=u, qn=qn, wp=wp, ob=ob: finish_unit_A(u, qn, wp, ob))

                    def finish_block_B(u, qn, wp):
                        nonlocal w_ops_free
                        qb = qn % 3
                        q0 = u["qi"] * QB
                        wcp = P.add("dve", lambda e: e.tensor_copy(out=Osb[:, :], in_=Ops[:, :]), waits=[wp, w_t1_prev[0]], sig=True)
                        w_ops_free = wcp
                        wr = P.add("dve", lambda e: e.reciprocal(out=rinv[0:64, :], in_=Osb[64:128, :]), waits=[wcp], sig=True)
                        wt1 = P.add("dve", lambda e: e.tensor_tensor(out=t1[0:64, :], in0=Osb[0:64, :], in1=rinv[0:64, :], op=ALU.mult),
                                    waits=[wr, w_g_prev[0]], sig=True)
                        w_t1_prev[0] = wt1
                        wg = P.add("dve", lambda e, qb=qb: e.tensor_tensor(out=Gs[qb][:, :], in0=t1[0:64, :], in1=Zb[qb][:, :], op=ALU.mult),
                                   waits=[wt1, w_qload[qn], w_gst.get(qn - 3)], sig=True)
                        w_g_prev[0] = wg
                        w_qfree[qn] = wg
                        w_gst[qn] = P.dma("sync", GT[h * 64:(h + 1) * 64, q0:q0 + QB], Gs[qb][:, :], c_g3[qb], waits=[wg])

                    def finish_unit_A(u, qn, wp, ob):
                        qb = qn % 2
                        q0 = u["qi"] * QB
                        qoff = u["qoff"]
                        gi = u["m2"] + h * (NT // 2)
                        wc1 = P.add("dve", lambda e, ob=ob, qb=qb, qoff=qoff: e.tensor_copy(out=OsbA[qb][0:64, qoff:qoff + 256], in_=OpsA[ob][0:64, 0:256]),
                                    waits=[wp, w_t1_blk.get(qn - 2)], sig=True)
                        wc2 = P.add("dve", lambda e, ob=ob, qb=qb, qoff=qoff: e.tensor_copy(out=rsA[qb][0:64, qoff:qoff + 256], in_=OpsA[ob][64:128, 0:256]),
                                    waits=[wc1], sig=True)
                        w_opsA[gi] = wc2
                        if u["last"]:
                            wr0 = P.add("act", lambda e, qb=qb: e.activation(out=rsA[qb][0:64, :], in_=rsA[qb][0:64, :], func=AF.Ln), waits=[wc2], sig=True)
                            wr = P.add("act", lambda e, qb=qb: e.activation(out=rsA[qb][0:64, :], in_=rsA[qb][0:64, :], func=AF.Exp, scale=-1.0),
                                       waits=[wr0], sig=True)
                            wt1 = P.add("dve", lambda e, qb=qb: e.tensor_tensor(out=t1[0:64, :], in0=OsbA[qb][0:64, :], in1=rsA[qb][0:64, :], op=ALU.mult),
                                        waits=[wr, w_g_prev[0]], sig=True)
                            w_t1_blk[qn] = wt1
                            q3 = qn % 3
                            wg = P.add("pool", lambda e, q3=q3: e.tensor_tensor(out=Gs[q3][:, :], in0=t1[0:64, :], in1=Zb[q3][:, :], op=ALU.mult),
                                       waits=[wt1, w_qload[qn], w_gst.get(qn - 3)], sig=True)
                            w_g_prev[0] = wg
                            w_qfree[qn] = wg
                            w_gst[qn] = P.dma("sync", GT[h * 64:(h + 1) * 64, q0:q0 + QB], Gs[q3][:, :], c_g3[q3], waits=[wg])

                    nU = len(units)
                    wq_list = {}

                    def ensure_q(ui):
                        u = units[ui]
                        qn = h * NQB + u["qi"]
                        for qq in (qn, qn + 1):
                            if qq >= H * NQB or qq in loaded_all:
                                continue
                            if qq - 3 >= 0 and (qq - 3) not in w_qfree:
                                assert qq != qn, "q block needed but predecessor not finished"
                                continue
                            load_q(qq // NQB, qq % NQB)
                            loaded_all.add(qq)

                    for ui in range(min(NSB, nU)):
                        ensure_q(ui)
                        wq_list[ui] = emit_qk(ui, units[ui])
                        emit_exp(ui, units[ui], wq_list[ui])
                    pending = []
                    for ui in range(nU):
                        emit_pv(ui, units[ui])
                        if ui + NSB < nU:
                            ensure_q(ui + NSB)
                            wq_list[ui + NSB] = emit_qk(ui + NSB, units[ui + NSB])
                            emit_exp(ui + NSB, units[ui + NSB], wq_list[ui + NSB])
                        for f_ in pending:
                            f_()
                        pending.clear()
                    upe += nU
                    pe_head_done[h] = w_pv[upe - 1]
                P.barrier()
                P.emit()

        loaded_all = set()

        c3_w = P.chan("c3_w")
        c3_x = [P.chan("c3_x%d" % i) for i in range(3)]
        c3_gt = [P.chan("c3_gt0"), P.chan("c3_gt1")]
        NU3 = 6
        c3_st = [P.chan("c3_st%d" % i) for i in range(NU3)]
        c3_m = P.chan("c3_m")

        def phase3_main(li, typ, x_src, x_dst, st, lean, x_ready):
            j_l = li // 2
            w_out = (w_out_a if typ == "A" else w_out_b)[j_l]
            sb, ps = mk_alloc(st)
            NU = NU3
            NY = 1 if lean else 2
            Wo = sb("Wo", [128, 8, D], BF16)
            GTb = [sb("GTb%d" % i, [128, 8, 512], BF16) for i in range(2)]
            xres = [sb("xres%d" % i, [128, D], F32) for i in range(3)]
            ub = [sb("ub%d" % i, [128, D], F32) for i in range(NU)]
            gsb_ = sb("lng_s", [128, D], F32)
            bsb_ = sb("lnb_s", [128, D], F32)
            sm = [sb("sm%d" % i, [128, 8], F32) for i in range(NU)]
            sn = [sb("sn%d" % i, [128, 9], F32) for i in range(NU)]
            smi = [sb("smi%d" % i, [128, 2], I32) for i in range(NU)]
            sqs = sb("sqs", [128, D], BF16)
            yps = [ps("yps%d" % i, [128, D], F32) for i in range(NY)]
            w_sq_prev = [None]
            w_stA = {}
            w_stB = {}
            w_w = None
            for c in range(8):
                w_w = P.dma("pool", Wo[:, c, :], w_out[c * 128:(c + 1) * 128, :], c3_w)
            P.dma("sync", gsb_[:, :], lng_d[li], c3_m)
            w_gb = P.dma("sync", bsb_[:, :], lnb_d[li], c3_m)
            w_gt = {}
            w_xl = {}
            w_blk_pe = {}
            w_res = {}
            w_u_free = {}
            w_y_free = {}
            NTI = NB * 4

            def load_gt(b):
                w_gt[b] = P.dma("sync", GTb[b % 2][:, :, :], GT[:, b * 512:(b + 1) * 512].rearrange("(c p) s -> p c s", p=128),
                                c3_gt[b % 2], waits=[w_blk_pe.get(b - 2)])

            def load_xt(tu):
                w_xl[tu] = P.dma("sync", xres[tu % 3][:, :], x_src[tu * 128:(tu + 1) * 128, :], c3_x[tu % 3], waits=[w_res.get(tu - 3)])

            def stageA(tu):
                b, t = divmod(tu, 4)
                gt = GTb[b % 2]
                xr = xres[tu % 3]
                yp = yps[tu % NY]
                k = tu % NU
                u_ = ub[k]
                for half in range(2):
                    for c in range(8):
                        w_m = P.add("pe", lambda e, yp=yp, gt=gt, t=t, half=half, c=c: e.matmul(
                            yp[:, half * 512:(half + 1) * 512], lhsT=gt[:, c, t * 128:(t + 1) * 128],
                            rhs=Wo[:, c, half * 512:(half + 1) * 512], start=(c == 0), stop=(c == 7)),
                            waits=[w_w, w_gt[b], w_y_free.get(tu - NY)], sig=(c == 7 and half == 1))
                if t == 3:
                    w_blk_pe[b] = w_m
                    if b + 2 < NB:
                        load_gt(b + 2)
                w_r = P.add("dve", lambda e, u_=u_, xr=xr, yp=yp, k=k: e.scalar_tensor_tensor(
                    out=u_[:, :], in0=xr[:, :], scalar=ALPHA, in1=yp[:, :], op0=ALU.mult, op1=ALU.add, accum_out=sm[k][:, 0:1]),
                    waits=[w_m, w_xl[tu], w_u_free.get(tu - NU)], sig=True)
                w_y_free[tu] = w_r
                w_res[tu] = w_r
                if tu + 3 < NTI:
                    load_xt(tu + 3)
                w_q = P.add("act", lambda e, u_=u_, k=k: e.activation(out=sqs[:, :], in_=u_[:, :], func=AF.Square, accum_out=sm[k][:, 1:2]),
                            waits=[w_r, w_sq_prev[0]], sig=True)
                w_sq_prev[0] = w_q
                w_1 = P.add("dve", lambda e, k=k: e.tensor_scalar(out=sm[k][:, 2:3], in0=sm[k][:, 0:1], scalar1=1.0 / D, scalar2=None, op0=ALU.mult),
                            waits=[w_r], sig=True)
                w_2 = P.add("dve", lambda e, k=k: e.scalar_tensor_tensor(out=sm[k][:, 3:4], in0=sm[k][:, 2:3], scalar=-1.0, in1=sm[k][:, 2:3],
                                                                      op0=ALU.mult, op1=ALU.mult), waits=[w_1], sig=True)
                w_3 = P.add("dve", lambda e, k=k: e.scalar_tensor_tensor(out=sm[k][:, 4:5], in0=sm[k][:, 1:2], scalar=1.0 / D, in1=sm[k][:, 3:4],
                                                                      op0=ALU.mult, op1=ALU.add), waits=[w_2, w_q], sig=True)
                w_4 = P.add("dve", lambda e, k=k: e.tensor_scalar(out=sm[k][:, 4:5], in0=sm[k][:, 4:5], scalar1=EPS, scalar2=None, op0=ALU.add),
                            waits=[w_3], sig=True)
                ve = sm[k][:, 4:5]
                w_5 = P.add("dve", lambda e, k=k, ve=ve: e.tensor_scalar(out=smi[k][:, 0:1], in0=ve.bitcast(I32), scalar1=1, scalar2=None,
                                                                      op0=ALU.logical_shift_right), waits=[w_4], sig=True)
                w_6 = P.add("dve", lambda e, k=k: e.tensor_scalar(out=smi[k][:, 1:2], in0=smi[k][:, 0:1], scalar1=-1.0, scalar2=float(0x5f3759df),
                                                               op0=ALU.mult, op1=ALU.add), waits=[w_5], sig=True)
                ycur = smi[k][:, 1:2].bitcast(F32)
                wprev = w_6
                for it in range(3):
                    ta_ = sn[k][:, 3 * it:3 * it + 1]
                    tb_ = sn[k][:, 3 * it + 1:3 * it + 2]
                    ynew = sm[k][:, 6:7] if it == 2 else sn[k][:, 3 * it + 2:3 * it + 3]
                    wa = P.add("dve", lambda e, ycur=ycur, ve=ve, ta_=ta_: e.scalar_tensor_tensor(out=ta_, in0=ycur, scalar=ve, in1=ycur,
                                                                                              op0=ALU.mult, op1=ALU.mult), waits=[wprev], sig=True)
                    wb = P.add("dve", lambda e, ta_=ta_, tb_=tb_: e.tensor_scalar(out=tb_, in0=ta_, scalar1=-0.5, scalar2=1.5,
                                                                               op0=ALU.mult, op1=ALU.add), waits=[wa], sig=True)
                    wprev = P.add("dve", lambda e, ycur=ycur, tb_=tb_, ynew=ynew: e.tensor_tensor(out=ynew, in0=ycur, in1=tb_, op=ALU.mult),
                                  waits=[wb], sig=True)
                    ycur = ynew
                w_stA[tu] = wprev

            def stageB1(tu):
                k = tu % NU
                u_ = ub[k]
                w_7 = P.add("dve", lambda e, k=k: e.scalar_tensor_tensor(out=sm[k][:, 7:8], in0=sm[k][:, 2:3], scalar=-1.0, in1=sm[k][:, 6:7],
                                                                      op0=ALU.mult, op1=ALU.mult), waits=[w_stA[tu]], sig=True)
                w_stB[tu] = P.add("act", lambda e, u_=u_, k=k: e.activation(out=u_[:, :], in_=u_[:, :], func=AF.Identity,
                                                                            bias=sm[k][:, 7:8], scale=sm[k][:, 6:7]), waits=[w_7], sig=True)

            def stageB2(tu):
                b, t = divmod(tu, 4)
                k = tu % NU
                u_ = ub[k]
                w_g = P.add("dve", lambda e, u_=u_: e.tensor_tensor(out=u_[:, :], in0=u_[:, :], in1=gsb_[:, :], op=ALU.mult),
                            waits=[w_stB[tu], w_gb], sig=True)
                w_b = P.add("pool", lambda e, u_=u_: e.tensor_tensor(out=u_[:, :], in0=u_[:, :], in1=bsb_[:, :], op=ALU.add),
                            waits=[w_g], sig=True)
                w_s = P.dma("sync", x_dst[tu * 128:(tu + 1) * 128, :], u_[:, :], c3_st[k], waits=[w_b])
                w_u_free[tu] = w_s
                x_ready.setdefault(b, []).append(w_s)

            load_gt(0)
            if NB > 1:
                load_gt(1)
            for tu in range(min(3, NTI)):
                load_xt(tu)
            for tu in range(NTI + 2):
                if 0 <= tu - 1 < NTI:
                    stageB1(tu - 1)
                if 0 <= tu - 2 < NTI:
                    stageB2(tu - 2)
                if tu < NTI:
                    stageA(tu)
                yield

        import os
        kstop = int(os.environ.get("KSTOP", "99"))
        nofuse = bool(os.environ.get("KNOFUSE"))

        def run_all(gen):
            for _ in gen:
                pass

        x_src = x_d
        with ExitStack() as st:
            run_all(phase1_main(0, ltypes[0], x_src, st, False))
            P.barrier()
            P.emit()
        if ltypes[0] == "B":
            phase1_post(0)
        for li, typ in enumerate(ltypes):
            last = li == NL - 1
            loaded_all.clear()
            if kstop >= 2:
                phase2(li, typ)
            x_dst = y_d if last else xbuf
            if kstop >= 3:
                if last:
                    with ExitStack() as st:
                        run_all(phase3_main(li, typ, x_src, x_dst, st, False, {}))
                        P.barrier()
                        P.emit()
                elif nofuse:
                    with ExitStack() as st:
                        run_all(phase3_main(li, typ, x_src, x_dst, st, False, {}))
                        P.barrier()
                        P.emit()
                    with ExitStack() as st:
                        run_all(phase1_main(li + 1, ltypes[li + 1], xbuf, st, False))
                        P.barrier()
                        P.emit()
                else:
                    with ExitStack() as st:
                        x_ready = {}
                        g3 = phase3_main(li, typ, x_src, x_dst, st, True, x_ready)
                        g1 = phase1_main(li + 1, ltypes[li + 1], xbuf, st, True, x_ready)
                        LEAD = 14
                        done3 = False
                        done1 = False
                        for _ in range(LEAD):
                            if next(g3, "end") == "end":
                                done3 = True
                                break
                        while not (done1 and done3):
                            if not done1:
                                if next(g1, "end") == "end":
                                    done1 = True
                            if not done3:
                                if next(g3, "end") == "end":
                                    done3 = True
                        P.barrier()
                        P.emit()
                if not last and ltypes[li + 1] == "B":
                    phase1_post(li + 1)
            x_src = xbuf
        print("ops recorded:", P.nops)
    return nc


def host_consts():
    ident = np.eye(128, dtype=np.float32)
    k = np.arange(128)[:, None]
    q = np.arange(128)[None, :]
    tri = np.where(k > q, NEG, 0.0).astype(np.float32)
    maskA = np.zeros((128, 5, 128), np.float32)
    maskA[64:, 0, :64] = NEG
    maskA[:64, 4, 64:] = NEG
    return ident, tri, maskA.reshape(128, 640)


def host_bias_layout(rel_bias_a):
    k = np.arange(128)[:, None, None]
    j = np.arange(5)[None, :, None]
    q = np.arange(128)[None, None, :]
    idx = np.clip(128 * j + q - k, -128, 128) + 128
    out = rel_bias_a[:, :, idx]
    return np.ascontiguousarray(out.reshape(rel_bias_a.shape[0], H, 128, 640)).astype(np.float32)


def make_common(w_in_a, rel_bias_a, w_out_a, w_in_b, w_f_b, b_f_b, w_out_b, ln_g, ln_b):
    ident, tri, maskA = host_consts()
    f = lambda a: np.ascontiguousarray(np.asarray(a, dtype=np.float32))
    return {
        "w_in_a": f(w_in_a), "w_in_b": f(w_in_b), "w_out_a": f(w_out_a), "w_out_b": f(w_out_b),
        "w_f_b": f(w_f_b), "bfb": f(np.asarray(b_f_b)[:, :, None]),
        "biasT": host_bias_layout(np.asarray(rel_bias_a, dtype=np.float32)),
        "lng": f(np.broadcast_to(np.asarray(ln_g)[:, None, :], (4, 128, D))),
        "lnb": f(np.broadcast_to(np.asarray(ln_b)[:, None, :], (4, 128, D))),
        "ident": ident, "tri": tri, "maskA": maskA,
        "cbias": f(np.broadcast_to(np.asarray(rel_bias_a, dtype=np.float32)[:, None, :, 256], (2, 128, H))),
    }


def kernel(x, w_in_a, rel_bias_a, w_out_a, w_in_b, w_f_b, b_f_b, w_out_b, ln_g, ln_b):
    x = np.asarray(x, dtype=np.float32)
    common = make_common(w_in_a, rel_bias_a, w_out_a, w_in_b, w_f_b, b_f_b, w_out_b, ln_g, ln_b)
    nc = build(SEQ, "ABAB")
    in_maps = []
    for c in range(NCORES):
        m = dict(common)
        m["x"] = np.ascontiguousarray(x[c])
        in_maps.append(m)
    res = run_bass_kernel_spmd(nc, in_maps, core_ids=list(range(NCORES)))
    return np.stack([np.asarray(r["y"], dtype=np.float32) for r in res.results], axis=0)
```

```python
import numpy as np
from contextlib import ExitStack
import concourse.bass as bass
import concourse.mybir as mybir
from concourse.bass_utils import run_bass_kernel_spmd

F32 = mybir.dt.float32
BF16 = mybir.dt.bfloat16
I32 = mybir.dt.int32
AF = mybir.ActivationFunctionType
ALU = mybir.AluOpType

D = 1024
H = 16
NEG = -30000.0
ALPHA = 8.0 ** 0.25
EPS = 1e-5
SEQ = 8192
NCORES = 8
import os
KSKIP = os.environ.get("KSKIP", "").split(",")


class Chan:
    def __init__(self, sem, name):
        self.sem = sem
        self.count = 0
        self.name = name


class Prog:
    ENG = ("sync", "act", "pe", "dve", "pool")

    def __init__(self, nc, stack):
        self.nc = nc
        self.stack = stack
        self.q = {e: [] for e in self.ENG}
        self.waited = {}
        self.chans = []
        self.ech = {e: self.chan("e_" + e) for e in ("act", "pe", "dve", "pool")}
        self.nops = 0

    def chan(self, name):
        sem = self.stack.enter_context(self.nc.semaphore(name))
        c = Chan(sem, name)
        self.chans.append(c)
        return c

    def add(self, eng, fn, waits=(), sig=None, inc=1):
        ws = []
        for w in waits:
            if w is None:
                continue
            ch, v = w
            if v <= 0:
                continue
            key = (eng, ch.name)
            if self.waited.get(key, 0) >= v:
                continue
            self.waited[key] = v
            ws.append((ch, v))
        ch = None
        if sig is True:
            ch = self.ech[eng]
        elif sig:
            ch = sig
        ret = None
        if ch is not None:
            ch.count += inc
            ret = (ch, ch.count)
        self.q[eng].append((fn, ws, ch, inc))
        self.nops += 1
        return ret

    def dma(self, eng, out, in_, ch, waits=()):
        return self.add(eng, lambda e: e.dma_start(out=out, in_=in_), waits=waits, sig=ch, inc=16)

    def barrier(self):
        ws = [(c, c.count) for c in self.chans if c.count > 0]
        for e in self.ENG:
            self.add(e, None, waits=ws)

    def emit(self):
        nc = self.nc
        q = self.q
        self.q = {e: [] for e in self.ENG}
        with nc.Block() as block:
            def mk(name):
                def run(eng):
                    for fn, ws, ch, inc in q[name]:
                        for c, v in ws:
                            eng.wait_ge(c.sem, v)
                        if fn is None:
                            continue
                        ins = fn(eng)
                        if ch is not None:
                            ins.then_inc(ch.sem, inc)
                return run
            block.sync(mk("sync"))
            block.scalar(mk("act"))
            block.tensor(mk("pe"))
            block.vector(mk("dve"))
            block.gpsimd(mk("pool"))


def build(S, ltypes, dbg=None):
    nc = bass.Bass("TRN2", target_bir_lowering=False)
    NL = len(ltypes)
    NT = S // 128
    NB = S // 512
    QB = min(1024, S)
    NQB = S // QB

    def din(name, shape, dt=F32):
        return nc.dram_tensor(name, shape, dt, kind="ExternalInput").ap()

    x_d = din("x", [S, D])
    w_in_a = din("w_in_a", [2, D, 4 * D])
    w_in_b = din("w_in_b", [2, D, 4 * D])
    w_out_a = din("w_out_a", [2, D, D])
    w_out_b = din("w_out_b", [2, D, D])
    w_f_b = din("w_f_b", [2, D, H])
    bfb_d = din("bfb", [2, H, 1])
    biasT_d = din("biasT", [2, H, 128, 640])
    lng_d = din("lng", [4, 128, D])
    lnb_d = din("lnb", [4, 128, D])
    ident_d = din("ident", [128, 128])
    tri_d = din("tri", [128, 128])
    maskA_d = din("maskA", [128, 640])
    cb_d = din("cbias", [2, 128, H])
    y_d = nc.dram_tensor("y", [S, D], F32, kind="ExternalOutput").ap()

    xbuf = nc.dram_tensor("xbuf", [S, D], F32).ap()
    QT = nc.dram_tensor("QT", [D, S], BF16).ap()
    KT = nc.dram_tensor("KT", [D, S], BF16).ap()
    VT = nc.dram_tensor("VT", [D, S], BF16).ap()
    ZT = nc.dram_tensor("ZT", [D, S], BF16).ap()
    GT = nc.dram_tensor("GT", [D, S], BF16).ap()
    FT = nc.dram_tensor("FT", [H, S], BF16).ap()

    with ExitStack() as gst:
        uid = [0]

        def mk_alloc(st):
            def sb(name, shape, dt):
                uid[0] += 1
                return st.enter_context(nc.sbuf_tensor("%s_%d" % (name, uid[0]), shape, dt))

            def ps(name, shape, dt):
                uid[0] += 1
                return st.enter_context(nc.psum_tensor("%s_%d" % (name, uid[0]), shape, dt))
            return sb, ps

        gsb, gps = mk_alloc(gst)
        P = Prog(nc, gst)

        identf = gsb("identf", [128, 128], F32)
        identb = gsb("identb", [128, 128], BF16)
        trib = gsb("trib", [128, 128], BF16)
        negF = gsb("negF", [128, NT * H], F32)
        c_const = P.chan("c_const")
        with ExitStack() as st:
            sb, ps = mk_alloc(st)
            trif = sb("trif", [128, 128], F32)
            P.dma("sync", identf[:, :], ident_d, c_const)
            w = P.dma("sync", trif[:, :], tri_d, c_const)
            P.add("dve", lambda e: e.tensor_copy(out=identb[:, :], in_=identf[:, :]), waits=[w], sig=True)
            P.add("dve", lambda e: e.tensor_copy(out=trib[:, :], in_=trif[:, :]), sig=True)
            P.barrier()
            P.emit()

        c_w = P.chan("c_w")
        c_x = [P.chan("c_x0"), P.chan("c_x1")]
        c_g = [P.chan("c_g0"), P.chan("c_g1")]
        c_st = [P.chan("c_st%d" % i) for i in range(4)]
        c_k = [P.chan("c_k0"), P.chan("c_k1")]
        c_q3 = [P.chan("c_q3_%d" % i) for i in range(3)]
        c_g3 = [P.chan("c_g3_%d" % i) for i in range(3)]
        c_misc = P.chan("c_misc")

        ZF = nc.dram_tensor("ZF", [H, S], F32).ap()
        c1_w = P.chan("c1_w")
        c1_x = [P.chan("c1_x0"), P.chan("c1_x1")]
        c1_st = [P.chan("c1_st%d" % i) for i in range(4)]
        c1_zf = [P.chan("c1_zf0"), P.chan("c1_zf1")]
        c1_m = P.chan("c1_m")

        def phase1_main(li, typ, x_src, st, lean, x_ready=None):
            j_l = li // 2
            w_in = (w_in_a if typ == "A" else w_in_b)[j_l]
            isB = typ == "B"
            sb, ps = mk_alloc(st)
            NXIN = 1 if lean else 2
            NMM = 3 if lean else 4
            Wb = sb("Wb", [128, 8, 4 * D], BF16)
            xin = [sb("xin%d" % i, [128, 4, D], F32) for i in range(NXIN)]
            xT = [sb("xT%d" % i, [128, 8, 512], BF16) for i in range(2)]
            osb = [sb("osb%d" % i, [128, 512], BF16) for i in range(4)]
            tp = [ps("tp%d" % i, [128, 512], F32) for i in range(2)]
            mm = [ps("mm%d" % i, [128, 512], F32) for i in range(NMM)]
            if isB:
                xTf = sb("xTf", [128, 8, 512], F32)
                wf = sb("wf", [128, 8, H], F32)
                bfs = sb("bfs", [H, 1], F32)
                zfs = [sb("zfs%d" % i, [H, 512], F32) for i in range(2)]
                zps = ps("zps", [H, 512], F32)
            w_w = None
            for c in range(8):
                for hh in range(2):
                    w_w = P.dma("pool", Wb[:, c, hh * 2048:(hh + 1) * 2048],
                                w_in[c * 128:(c + 1) * 128, hh * 2048:(hh + 1) * 2048], c1_w)
            w_wf = None
            if isB:
                P.dma("sync", wf[:, :, :], w_f_b[j_l].rearrange("(c p) h -> p c h", p=128), c1_m)
                w_wf = P.dma("sync", bfs[:, :], bfb_d[j_l], c1_m)
            w_tr_done = {}
            w_x = {}

            def load_x(b):
                ws = [w_tr_done.get(b - NXIN)]
                if x_ready is not None:
                    assert b in x_ready, "x block %d not yet produced" % b
                    ws = ws + list(x_ready[b])
                w_x[b] = P.dma("sync", xin[b % NXIN][:, :, :],
                               x_src[b * 512:(b + 1) * 512, :].rearrange("(t p) d -> p t d", p=128),
                               c1_x[b % 2], waits=ws)
            for b in range(min(NXIN, NB)):
                load_x(b)
            ev_tp = {}
            cnt = {"tpu": 0, "mmu": 0, "stu": 0}
            ev_mm = {}
            st_w = {}
            zst = {"mm_prev": None, "ev_prev": None}
            w_zf_st = {}
            w_ev_blk = {}

            def do_transposes(b):
                xi = xin[b % NXIN]
                xt = xT[b % 2]
                w_ev_last = None
                w_evf_last = None
                for c in range(8):
                    tpu = cnt["tpu"]
                    tb = tp[tpu % 2]
                    for t in range(4):
                        w_t = P.add("pe", lambda e, tb=tb, xi=xi, t=t, c=c: e.transpose(
                            out=tb[:, t * 128:(t + 1) * 128], in_=xi[:, t, c * 128:(c + 1) * 128], identity=identf[:, :]),
                            waits=[w_x[b]] + (ev_tp.get(tpu - 2) or []), sig=(t == 3))
                    if c == 7:
                        w_tr_done[b] = w_t
                    wl = []
                    w_ev_last = P.add("dve", lambda e, tb=tb, xt=xt, c=c: e.tensor_copy(out=xt[:, c, :], in_=tb[:, :]),
                                      waits=[w_t], sig=True)
                    wl.append(w_ev_last)
                    if isB:
                        w_evf_last = P.add("act", lambda e, tb=tb, c=c: e.activation(out=xTf[:, c, :], in_=tb[:, :], func=AF.Identity),
                                           waits=[w_t, zst["mm_prev"], w_ev_last], sig=True)
                        wl.append(w_evf_last)
                    ev_tp[tpu] = wl
                    cnt["tpu"] += 1
                w_ev_blk[b] = w_ev_last
                if b + NXIN < NB:
                    load_x(b + NXIN)
                if isB:
                    for c in range(8):
                        w_zf_mm = P.add("pe", lambda e, c=c: e.matmul(zps[:, :], lhsT=wf[:, c, :], rhs=xTf[:, c, :],
                                                                     start=(c == 0), stop=(c == 7)),
                                        waits=[w_evf_last, w_wf, zst["ev_prev"]], sig=(c == 7))
                    zst["mm_prev"] = w_zf_mm
                    zs = zfs[b % 2]
                    zst["ev_prev"] = P.add("dve", lambda e, zs=zs: e.tensor_scalar(
                        out=zs[:, :], in0=zps[:, :], scalar1=bfs[:, 0:1], scalar2=None, op0=ALU.add),
                        waits=[w_zf_mm, w_zf_st.get(b - 2)], sig=True)
                    w_zf_st[b] = P.dma("sync", ZF[:, b * 512:(b + 1) * 512], zs[:, :], c1_zf[b % 2], waits=[zst["ev_prev"]])

            do_transposes(0)
            for b in range(NB):
                xt = xT[b % 2]
                for j in range(32):
                    mmu = cnt["mmu"]
                    stu = cnt["stu"]
                    mb = mm[mmu % NMM]
                    for c in range(8):
                        w_m = P.add("pe", lambda e, mb=mb, xt=xt, j=j, c=c: e.matmul(
                            mb[:, :], lhsT=Wb[:, c, j * 128:(j + 1) * 128], rhs=xt[:, c, :], start=(c == 0), stop=(c == 7)),
                            waits=[w_w, w_ev_blk[b], ev_mm.get(mmu - NMM)], sig=(c == 7))
                    ob = osb[stu % 4]
                    wst = st_w.get(stu - 4)
                    if j < 8:
                        w_e = P.add("act", lambda e, mb=mb, ob=ob: e.activation(out=ob[:, :], in_=mb[:, :], func=AF.Copy, scale=0.125),
                                    waits=[w_m, wst], sig=True)
                        dst = QT
                    elif j < 24:
                        if lean:
                            w_e = P.add("act", lambda e, mb=mb, ob=ob: e.activation(out=ob[:, :], in_=mb[:, :], func=AF.Copy),
                                        waits=[w_m, wst], sig=True)
                        else:
                            w_e = P.add("dve", lambda e, mb=mb, ob=ob: e.tensor_copy(out=ob[:, :], in_=mb[:, :]),
                                        waits=[w_m, wst], sig=True)
                        dst = KT if j < 16 else VT
                    else:
                        w_e = P.add("act", lambda e, mb=mb, ob=ob: e.activation(out=ob[:, :], in_=mb[:, :], func=AF.Silu),
                                    waits=[w_m, wst], sig=True)
                        dst = ZT
                    ev_mm[mmu] = w_e
                    cnt["mmu"] += 1
                    jj = j % 8
                    st_w[stu] = P.dma("sync", dst[jj * 128:(jj + 1) * 128, b * 512:(b + 1) * 512], ob[:, :], c1_st[stu % 4], waits=[w_e])
                    cnt["stu"] += 1
                    if j == 15 and b + 1 < NB:
                        do_transposes(b + 1)
                    if j % 8 == 7:
                        yield

        def phase1_post(li):
            with ExitStack() as st:
                sb, ps = mk_alloc(st)
                zfT = sb("zfT", [H, S], F32)
                Fs = sb("Fs", [H, S], F32)
                Fb = sb("Fb", [H, S], BF16)
                CH = min(2048, S)
                ones = sb("ones", [H, CH], F32)
                fps = ps("fps", [128, NT * H], F32)
                w_ld = P.dma("sync", zfT[:, :], ZF, c1_m)
                w_o = P.add("dve", lambda e: e.memset(ones[:, :], 1.0), sig=True)
                w1 = P.add("act", lambda e: e.activation(out=zfT[:, :], in_=zfT[:, :], func=AF.Exp, scale=-1.0), waits=[w_ld], sig=True)
                w2 = P.add("act", lambda e: e.activation(out=zfT[:, :], in_=zfT[:, :], func=AF.Ln, bias=1.0), waits=[w1], sig=True)
                wprev = w2
                for ci in range(S // CH):
                    init = 0.0 if ci == 0 else Fs[:, ci * CH - 1:ci * CH]
                    wprev = P.add("dve", lambda e, ci=ci, init=init: e.tensor_tensor_scan(
                        out=Fs[:, ci * CH:(ci + 1) * CH], data0=ones[:, :], data1=zfT[:, ci * CH:(ci + 1) * CH],
                        initial=init, op0=ALU.mult, op1=ALU.subtract), waits=[wprev, w_o], sig=True)
                w_fb = P.add("dve", lambda e: e.tensor_copy(out=Fb[:, :], in_=Fs[:, :]), waits=[wprev], sig=True)
                P.dma("sync", FT, Fb[:, :], c1_m, waits=[w_fb])
                for kt in range(NT):
                    w_t = P.add("pe", lambda e, kt=kt: e.transpose(out=fps[:, kt * H:(kt + 1) * H], in_=Fs[:, kt * 128:(kt + 1) * 128],
                                                                 identity=identf[0:H, 0:H]), waits=[wprev], sig=(kt == NT - 1))
                wlast = w_t
                for c0 in range(0, NT * H, 512):
                    c1 = min(NT * H, c0 + 512)
                    wlast = P.add("dve", lambda e, c0=c0, c1=c1: e.tensor_scalar(out=negF[:, c0:c1], in0=fps[:, c0:c1], scalar1=-1.0,
                                                                              scalar2=None, op0=ALU.mult), waits=[w_t, wlast], sig=True)
                P.barrier()
                P.emit()

        def phase2(li, typ):
            j_l = li // 2
            isB = typ == "B"
            with ExitStack() as st:
                sb, ps = mk_alloc(st)
                Ka = [sb("Ka%d" % i, [128, S], BF16) for i in range(2)]
                VTb = [sb("VTb%d" % i, [64, S], BF16) for i in range(2)]
                Vones = [sb("Vones%d" % i, [128, NT, 128], BF16) for i in range(2)]
                Qa = [sb("Qa%d" % i, [128, QB], BF16) for i in range(3)]
                Zb = [sb("Zb%d" % i, [64, QB], BF16) for i in range(3)]
                PW = 1024 if isB else 1280
                PT = [sb("PT%d" % i, [128, PW], BF16) for i in range(3)]
                rinv = sb("rinv", [128, QB], F32)
                t1 = sb("t1", [64, QB], F32)
                Gs = [sb("Gs%d" % i, [64, QB], BF16) for i in range(3)]
                PSW = 1024 if isB else 1536
                NSB = 3 if isB else 2
                if isB:
                    Sps = [ps("Sps%d" % i, [128, 1024], F32) for i in range(3)]
                    Ops = ps("Ops", [128, 1024], F32)
                    Osb = sb("Osb", [128, QB], F32)
                else:
                    Sps = [ps("Sps%d" % i, [128, 1536], F32) for i in range(2)]
                    OpsA = [ps("OpsA%d" % i, [128, 512], F32) for i in range(2)]
                    OsbA = [sb("OsbA%d" % i, [64, QB], F32) for i in range(2)]
                    rsA = [sb("rsA%d" % i, [64, QB], F32) for i in range(2)]
                    Mb = sb("Mb", [128, H, 640], BF16)
                    bstage = [sb("bstage%d" % i, [128, 640], F32) for i in range(2)]
                    maskf = sb("maskf", [128, 640], F32)

                w_init = None
                for i in range(2):
                    P.add("dve", lambda e, i=i: e.memset(Ka[i][64:65, :], 1.0), sig=True)
                    P.add("dve", lambda e, i=i: e.memset(Vones[i][:, :, 64:128], 1.0), sig=True)
                if not isB:
                    for i in range(3):
                        P.add("dve", lambda e, i=i: e.memset(Qa[i][64:65, :], 0.0), sig=True)
                w_init = (P.ech["dve"], P.ech["dve"].count)
                w_mb = None
                if not isB:
                    cbs = sb("cbs", [128, H], F32)
                    P.dma("sync", cbs[:, :], cb_d[j_l], c_misc)
                    w_mk = P.dma("sync", maskf[:, :], maskA_d, c_misc)
                    wb_prev = {}
                    for h in range(H):
                        wl = P.dma("sync", bstage[h % 2][:, :], biasT_d[j_l, h], c_x[h % 2], waits=[wb_prev.get(h - 2)])
                        wb_prev[h] = P.add("dve", lambda e, h=h: e.scalar_tensor_tensor(out=Mb[:, h, :], in0=bstage[h % 2][:, :], scalar=cbs[:, h:h + 1],
                                                                                      in1=maskf[:, :], op0=ALU.subtract, op1=ALU.add),
                                           waits=[wl, w_mk], sig=True)
                    w_mb = wb_prev[H - 1]

                pe_head_done = {}
                w_kv = {}
                w_t1_prev = [None]
                w_g_prev = [None]
                w_t1_blk = {}

                def load_kv(h):
                    hb = h % 2
                    P.dma("sync", Ka[hb][0:64, :], KT[h * 64:(h + 1) * 64, :], c_k[hb], waits=[pe_head_done.get(h - 2), w_init])
                    w_kv[h] = P.dma("sync", VTb[hb][:, :], VT[h * 64:(h + 1) * 64, :], c_k[hb], waits=[pe_head_done.get(h - 2)])

                qstate = {"n": 0}
                w_qload = {}
                w_qfree = {}
                w_gst = {}

                def load_q(h, qi):
                    qn = h * NQB + qi
                    qb = qn % 3
                    q0 = qi * QB
                    wf_ = w_qfree.get(qn - 3)
                    P.dma("sync", Qa[qb][0:64, :], QT[h * 64:(h + 1) * 64, q0:q0 + QB], c_q3[qb], waits=[wf_, w_init])
                    if isB:
                        P.dma("sync", Qa[qb][64:65, :], FT[h:h + 1, q0:q0 + QB], c_q3[qb], waits=[wf_])
                    w_qload[qn] = P.dma("sync", Zb[qb][:, :], ZT[h * 64:(h + 1) * 64, q0:q0 + QB], c_q3[qb], waits=[wf_])

                load_kv(0)
                upe = 0
                w_exp = {}
                w_pv = {}
                w_ops_free = None
                w_opsA = {}
                for h in range(H):
                    hb = h % 2
                    if h + 1 < H:
                        load_kv(h + 1)
                    vviews = [Sps[upe % NSB][:, :].bitcast(BF16), Sps[(upe + 1) % NSB][:, :].bitcast(BF16)]
                    G0 = (PSW * 2) // 64
                    assert NT <= 2 * G0
                    for kt in range(NT):
                        vv = vviews[kt // G0]
                        ko = kt % G0
                        w_t = P.add("pe", lambda e, kt=kt, vv=vv, ko=ko, hb=hb: e.transpose(
                            out=vv[:, ko * 64:(ko + 1) * 64],
                            in_=VTb[hb][0:64, kt * 128:(kt + 1) * 128], identity=identb[0:64, 0:64]),
                            waits=[w_kv[h], w_exp.get(upe - 3), w_exp.get(upe - 2), w_exp.get(upe - 1), pe_head_done.get(h - 2)], sig=(kt == NT - 1))
                    w_vlast = None
                    for g in range(0, NT, 16):
                        n = min(16, NT - g)
                        vv = vviews[g // G0]
                        go = g % G0
                        w_vlast = P.add("dve", lambda e, g=g, n=n, vv=vv, go=go, hb=hb: e.tensor_copy(
                            out=Vones[hb][:, g:g + n, 0:64],
                            in_=vv[:, go * 64:(go + n) * 64].rearrange("p (k d) -> p k d", d=64)),
                            waits=[w_t, w_init], sig=True)

                    units = []
                    if isB:
                        for qi in range(NQB):
                            nsub = QB // 128
                            nk = (qi + 1) * nsub
                            for kt in range(nk):
                                r = kt - qi * nsub
                                c0 = 128 * r if r > 0 else 0
                                u = dict(qi=qi, kt=kt, c0=c0, diag=(r >= 0), first=(kt == 0), last=(kt == nk - 1))
                                units.append(u)
                    else:
                        for m2 in range(NT // 2):
                            m0 = 2 * m2
                            qi = (m0 * 128) // QB
                            qoff = m0 * 128 - qi * QB
                            rows = []
                            for (dk, sc0, ncl, qlo, jlo, bseg) in ((-3, 256, 256, 0, 3, (128, 128)), (-2, 512, 256, 0, 2, None),
                                                                  (-1, 768, 256, 0, 1, (0, 128)), (0, 1024, 256, 0, 0, (0, 256)),
                                                                  (-4, 0, 128, 0, 4, (0, 128)), (1, 128, 128, 128, 0, (0, 128))):
                                kt = m0 + dk
                                if kt < 0:
                                    continue
                                rows.append((kt, sc0, ncl, qlo, jlo, bseg))
                            units.append(dict(qi=qi, qoff=qoff, rows=rows, m2=m2, last=((m0 + 2) * 128) % QB == 0))

                    def qn_of(u):
                        return h * NQB + u["qi"]

                    def emit_qk(ui, u):
                        g = upe + ui
                        sbuf_ = Sps[g % NSB]
                        qn = qn_of(u)
                        qa = Qa[qn % 3]
                        wts = [w_qload[qn], w_vlast, w_exp.get(g - NSB), w_init, w_mb]
                        if isB:
                            kt, c0 = u["kt"], u["c0"]
                            for lo in (0, 512):
                                hi = lo + 512
                                a = max(lo, c0)
                                if a >= hi or a >= QB:
                                    continue
                                hi = min(hi, QB)
                                dg = u["diag"] and (lo <= c0 < hi)
                                wq = P.add("pe", lambda e, sbuf_=sbuf_, kt=kt, a=a, hi=hi, qa=qa, dg=dg, hb=hb: e.matmul(
                                    sbuf_[:, a:hi], lhsT=Ka[hb][0:65, kt * 128:(kt + 1) * 128], rhs=qa[0:65, a:hi], start=True, stop=not dg),
                                    waits=wts, sig=True)
                                if dg:
                                    wq = P.add("pe", lambda e, sbuf_=sbuf_, c0=c0: e.matmul(
                                        sbuf_[:, c0:c0 + 128], lhsT=identb[:, :], rhs=trib[:, :], start=False, stop=True), sig=True)
                            return wq
                        else:
                            wq = None
                            nr = len(u["rows"])
                            for ri, (kt, sc0, ncl, qlo, jlo, bseg) in enumerate(u["rows"]):
                                qc = u["qoff"] + qlo
                                lastrow = ri == nr - 1
                                wq = P.add("pe", lambda e, sbuf_=sbuf_, kt=kt, sc0=sc0, ncl=ncl, qc=qc, qa=qa, hb=hb, bseg=bseg: e.matmul(
                                    sbuf_[:, sc0:sc0 + ncl], lhsT=Ka[hb][0:65, kt * 128:(kt + 1) * 128], rhs=qa[0:65, qc:qc + ncl],
                                    start=True, stop=(bseg is None)), waits=wts, sig=(lastrow and bseg is None))
                                if bseg is not None:
                                    bo, bn = bseg
                                    wq = P.add("pe", lambda e, sbuf_=sbuf_, sc0=sc0, bo=bo, bn=bn, jlo=jlo, h=h: e.matmul(
                                        sbuf_[:, sc0 + bo:sc0 + bo + bn], lhsT=identb[:, :], rhs=Mb[:, h, jlo * 128 + bo: jlo * 128 + bo + bn],
                                        start=False, stop=True), sig=lastrow)
                            return wq

                    def emit_exp(ui, u, wq):
                        g = upe + ui
                        sbuf_ = Sps[g % NSB]
                        pt = PT[g % 3]
                        wts = [wq, w_pv.get(g - 3)]
                        if isB:
                            kt, c0 = u["kt"], u["c0"]
                            w_exp[g] = P.add("act", lambda e, sbuf_=sbuf_, pt=pt, kt=kt, c0=c0, h=h: e.activation(
                                out=pt[:, c0:QB], in_=sbuf_[:, c0:QB], func=AF.Exp, bias=negF[:, kt * H + h: kt * H + h + 1], scale=1.0),
                                waits=wts, sig=True)
                        else:
                            segs = sorted((sc0, sc0 + ncl) for (kt, sc0, ncl, qlo, jlo, bseg) in u["rows"])
                            merged = []
                            for a, b_ in segs:
                                if merged and merged[-1][1] == a:
                                    merged[-1][1] = b_
                                else:
                                    merged.append([a, b_])
                            for a, b_ in merged:
                                w_exp[g] = P.add("act", lambda e, sbuf_=sbuf_, pt=pt, a=a, b_=b_: e.activation(
                                    out=pt[:, a:b_], in_=sbuf_[:, a:b_], func=AF.Exp), waits=wts, sig=True)

                    def emit_pv(ui, u):
                        nonlocal w_ops_free
                        g = upe + ui
                        pt = PT[g % 3]
                        qn = qn_of(u)
                        if isB:
                            kt, c0 = u["kt"], u["c0"]
                            wp = None
                            for lo in (0, 512):
                                hi = min(lo + 512, QB)
                                a = max(lo, c0)
                                if a >= hi:
                                    continue
                                nsub_ = QB // 128
                                last_kt = u["qi"] * nsub_ + min(nsub_, (lo // 512 + 1) * 4) - 1
                                wp = P.add("pe", lambda e, pt=pt, kt=kt, a=a, hi=hi, u=u, hb=hb, last_kt=last_kt: e.matmul(
                                    Ops[:, a:hi], lhsT=Vones[hb][:, kt, :], rhs=pt[:, a:hi], start=u["first"], stop=(kt == last_kt)),
                                    waits=[w_exp[g], w_ops_free if u["first"] else None], sig=True)
                            w_pv[g] = wp
                            if u["last"]:
                                pending.append(lambda u=u, qn=qn, wp=wp: finish_block_B(u, qn, wp))
                        else:
                            ob = u["m2"] % 2
                            rows = u["rows"]
                            wp = None
                            for ri, (kt, sc0, ncl, qlo, jlo, bseg) in enumerate(rows):
                                wp = P.add("pe", lambda e, pt=pt, kt=kt, sc0=sc0, ncl=ncl, qlo=qlo, ob=ob, ri=ri, nr=len(rows), hb=hb: e.matmul(
                                    OpsA[ob][:, qlo: qlo + ncl], lhsT=Vones[hb][:, kt, :], rhs=pt[:, sc0:sc0 + ncl],
                                    start=(ri == 0), stop=(ri == nr - 1)),
                                    waits=[w_exp[g], w_opsA.get(u["m2"] + h * (NT // 2) - 2) if ri == 0 else None], sig=True)
                            w_pv[g] = wp
                            pending.append(lambda u=u, qn=qn, wp=wp, ob=ob: finish_unit_A(u, qn, wp, ob))

                    def finish_block_B(u, qn, wp):
                        nonlocal w_ops_free
                        qb = qn % 3
                        q0 = u["qi"] * QB
                        wcp = P.add("dve", lambda e: e.tensor_copy(out=Osb[:, :], in_=Ops[:, :]), waits=[wp, w_t1_prev[0]], sig=True)
                        w_ops_free = wcp
                        wr = P.add("dve", lambda e: e.reciprocal(out=rinv[0:64, :], in_=Osb[64:128, :]), waits=[wcp], sig=True)
                        wt1 = P.add("dve", lambda e: e.tensor_tensor(out=t1[0:64, :], in0=Osb[0:64, :], in1=rinv[0:64, :], op=ALU.mult),
                                    waits=[wr, w_g_prev[0]], sig=True)
                        w_t1_prev[0] = wt1
                        wg = P.add("dve", lambda e, qb=qb: e.tensor_tensor(out=Gs[qb][:, :], in0=t1[0:64, :], in1=Zb[qb][:, :], op=ALU.mult),
                                   waits=[wt1, w_qload[qn], w_gst.get(qn - 3)], sig=True)
                        w_g_prev[0] = wg
                        w_qfree[qn] = wg
                        w_gst[qn] = P.dma("pool", GT[h * 64:(h + 1) * 64, q0:q0 + QB], Gs[qb][:, :], c_g3[qb], waits=[wg])

                    def finish_unit_A(u, qn, wp, ob):
                        qb = qn % 2
                        q0 = u["qi"] * QB
                        qoff = u["qoff"]
                        gi = u["m2"] + h * (NT // 2)
                        wc1 = P.add("dve", lambda e, ob=ob, qb=qb, qoff=qoff: e.tensor_copy(out=OsbA[qb][0:64, qoff:qoff + 256], in_=OpsA[ob][0:64, 0:256]),
                                    waits=[wp, w_t1_blk.get(qn - 2)], sig=True)
                        wc2 = P.add("dve", lambda e, ob=ob, qb=qb, qoff=qoff: e.tensor_copy(out=rsA[qb][0:64, qoff:qoff + 256], in_=OpsA[ob][64:128, 0:256]),
                                    waits=[wc1], sig=True)
                        w_opsA[gi] = wc2
                        for d_ in list(deferredA):
                            d_[0] -= 1
                            if d_[0] <= 0:
                                deferredA.remove(d_)
                                d_[1]()
                        if u["last"]:
                            deferredA.append([2, lambda u=u, qn=qn, qb=qb, q0=q0, wc2=wc2: norm_block_A(u, qn, qb, q0, wc2)])

                    def norm_block_A(u, qn, qb, q0, wc2):
                        if True:
                            wr0 = P.add("act", lambda e, qb=qb: e.activation(out=rsA[qb][0:64, :], in_=rsA[qb][0:64, :], func=AF.Ln), waits=[wc2], sig=True)
                            wr = P.add("act", lambda e, qb=qb: e.activation(out=rsA[qb][0:64, :], in_=rsA[qb][0:64, :], func=AF.Exp, scale=-1.0),
                                       waits=[wr0], sig=True)
                            wt1 = P.add("pool", lambda e, qb=qb: e.tensor_tensor(out=t1[0:64, :], in0=OsbA[qb][0:64, :], in1=rsA[qb][0:64, :], op=ALU.mult),
                                        waits=[wr, w_g_prev[0]], sig=True)
                            w_t1_blk[qn] = wt1
                            q3 = qn % 3
                            wg = P.add("pool", lambda e, q3=q3: e.tensor_tensor(out=Gs[q3][:, :], in0=t1[0:64, :], in1=Zb[q3][:, :], op=ALU.mult),
                                       waits=[wt1, w_qload[qn], w_gst.get(qn - 3)], sig=True)
                            w_g_prev[0] = wg
                            w_qfree[qn] = wg
                            w_gst[qn] = P.dma("pool", GT[h * 64:(h + 1) * 64, q0:q0 + QB], Gs[q3][:, :], c_g3[q3], waits=[wg])

                    nU = len(units)
                    wq_list = {}

                    def ensure_q(ui):
                        u = units[ui]
                        qn = h * NQB + u["qi"]
                        for qq in (qn, qn + 1):
                            if qq >= H * NQB or qq in loaded_all:
                                continue
                            if qq - 3 >= 0 and (qq - 3) not in w_qfree:
                                assert qq != qn, "q block needed but predecessor not finished"
                                continue
                            load_q(qq // NQB, qq % NQB)
                            loaded_all.add(qq)

                    for ui in range(min(NSB, nU)):
                        ensure_q(ui)
                        wq_list[ui] = emit_qk(ui, units[ui])
                        emit_exp(ui, units[ui], wq_list[ui])
                    pending = []
                    deferredA = []
                    for ui in range(nU):
                        emit_pv(ui, units[ui])
                        if ui + NSB < nU:
                            ensure_q(ui + NSB)
                            wq_list[ui + NSB] = emit_qk(ui + NSB, units[ui + NSB])
                            emit_exp(ui + NSB, units[ui + NSB], wq_list[ui + NSB])
                        for f_ in pending:
                            f_()
                        pending.clear()
                    for d_ in deferredA:
                        d_[1]()
                    deferredA = []
                    upe += nU
                    pe_head_done[h] = w_pv[upe - 1]
                P.barrier()
                P.emit()

        loaded_all = set()

        c3_w = P.chan("c3_w")
        c3_x = [P.chan("c3_x%d" % i) for i in range(3)]
        c3_gt = [P.chan("c3_gt0"), P.chan("c3_gt1")]
        NU3 = 6
        c3_st = [P.chan("c3_st%d" % i) for i in range(NU3)]
        c3_m = P.chan("c3_m")

        def phase3_main(li, typ, x_src, x_dst, st, lean, x_ready):
            j_l = li // 2
            w_out = (w_out_a if typ == "A" else w_out_b)[j_l]
            sb, ps = mk_alloc(st)
            NU = NU3
            NY = 1 if lean else 2
            Wo = sb("Wo", [128, 8, D], BF16)
            GTb = [sb("GTb%d" % i, [128, 8, 512], BF16) for i in range(2)]
            xres = [sb("xres%d" % i, [128, D], F32) for i in range(3)]
            ub = [sb("ub%d" % i, [128, D], F32) for i in range(NU)]
            gsb_ = sb("lng_s", [128, D], F32)
            bsb_ = sb("lnb_s", [128, D], F32)
            sm = [sb("sm%d" % i, [128, 8], F32) for i in range(NU)]
            sn = [sb("sn%d" % i, [128, 9], F32) for i in range(NU)]
            smi = [sb("smi%d" % i, [128, 2], I32) for i in range(NU)]
            sqs = sb("sqs", [128, D], BF16)
            yps = [ps("yps%d" % i, [128, D], F32) for i in range(NY)]
            w_sq_prev = [None]
            w_stA = {}
            w_stB = {}
            w_w = None
            for c in range(8):
                w_w = P.dma("pool", Wo[:, c, :], w_out[c * 128:(c + 1) * 128, :], c3_w)
            P.dma("sync", gsb_[:, :], lng_d[li], c3_m)
            w_gb = P.dma("sync", bsb_[:, :], lnb_d[li], c3_m)
            w_gt = {}
            w_xl = {}
            w_blk_pe = {}
            w_res = {}
            w_u_free = {}
            w_y_free = {}
            NTI = NB * 4

            def load_gt(b):
                w_gt[b] = P.dma("sync", GTb[b % 2][:, :, :], GT[:, b * 512:(b + 1) * 512].rearrange("(c p) s -> p c s", p=128),
                                c3_gt[b % 2], waits=[w_blk_pe.get(b - 2)])

            def load_xt(tu):
                w_xl[tu] = P.dma("sync", xres[tu % 3][:, :], x_src[tu * 128:(tu + 1) * 128, :], c3_x[tu % 3], waits=[w_res.get(tu - 3)])

            def stageA(tu):
                b, t = divmod(tu, 4)
                gt = GTb[b % 2]
                xr = xres[tu % 3]
                yp = yps[tu % NY]
                k = tu % NU
                u_ = ub[k]
                for half in range(2):
                    for c in range(8):
                        w_m = P.add("pe", lambda e, yp=yp, gt=gt, t=t, half=half, c=c: e.matmul(
                            yp[:, half * 512:(half + 1) * 512], lhsT=gt[:, c, t * 128:(t + 1) * 128],
                            rhs=Wo[:, c, half * 512:(half + 1) * 512], start=(c == 0), stop=(c == 7)),
                            waits=[w_w, w_gt[b], w_y_free.get(tu - NY)], sig=(c == 7 and half == 1))
                if t == 3:
                    w_blk_pe[b] = w_m
                    if b + 2 < NB:
                        load_gt(b + 2)
                w_r = P.add("dve", lambda e, u_=u_, xr=xr, yp=yp, k=k: e.scalar_tensor_tensor(
                    out=u_[:, :], in0=xr[:, :], scalar=ALPHA, in1=yp[:, :], op0=ALU.mult, op1=ALU.add, accum_out=sm[k][:, 0:1]),
                    waits=[w_m, w_xl[tu], w_u_free.get(tu - NU)], sig=True)
                w_y_free[tu] = w_r
                w_res[tu] = w_r
                if tu + 3 < NTI:
                    load_xt(tu + 3)
                w_q = P.add("act", lambda e, u_=u_, k=k: e.activation(out=sqs[:, :], in_=u_[:, :], func=AF.Square, accum_out=sm[k][:, 1:2]),
                            waits=[w_r, w_sq_prev[0]], sig=True)
                w_sq_prev[0] = w_q
                w_1 = P.add("dve", lambda e, k=k: e.tensor_scalar(out=sm[k][:, 2:3], in0=sm[k][:, 0:1], scalar1=1.0 / D, scalar2=None, op0=ALU.mult),
                            waits=[w_r], sig=True)
                w_2 = P.add("dve", lambda e, k=k: e.scalar_tensor_tensor(out=sm[k][:, 3:4], in0=sm[k][:, 2:3], scalar=-1.0, in1=sm[k][:, 2:3],
                                                                      op0=ALU.mult, op1=ALU.mult), waits=[w_1], sig=True)
                w_3 = P.add("dve", lambda e, k=k: e.scalar_tensor_tensor(out=sm[k][:, 4:5], in0=sm[k][:, 1:2], scalar=1.0 / D, in1=sm[k][:, 3:4],
                                                                      op0=ALU.mult, op1=ALU.add), waits=[w_2, w_q], sig=True)
                w_4 = P.add("dve", lambda e, k=k: e.tensor_scalar(out=sm[k][:, 4:5], in0=sm[k][:, 4:5], scalar1=EPS, scalar2=None, op0=ALU.add),
                            waits=[w_3], sig=True)
                ve = sm[k][:, 4:5]
                w_5 = P.add("dve", lambda e, k=k, ve=ve: e.tensor_scalar(out=smi[k][:, 0:1], in0=ve.bitcast(I32), scalar1=1, scalar2=None,
                                                                      op0=ALU.logical_shift_right), waits=[w_4], sig=True)
                w_6 = P.add("dve", lambda e, k=k: e.tensor_scalar(out=smi[k][:, 1:2], in0=smi[k][:, 0:1], scalar1=-1.0, scalar2=float(0x5f3759df),
                                                               op0=ALU.mult, op1=ALU.add), waits=[w_5], sig=True)
                ycur = smi[k][:, 1:2].bitcast(F32)
                wprev = w_6
                for it in range(3):
                    ta_ = sn[k][:, 3 * it:3 * it + 1]
                    tb_ = sn[k][:, 3 * it + 1:3 * it + 2]
                    ynew = sm[k][:, 6:7] if it == 2 else sn[k][:, 3 * it + 2:3 * it + 3]
                    wa = P.add("dve", lambda e, ycur=ycur, ve=ve, ta_=ta_: e.scalar_tensor_tensor(out=ta_, in0=ycur, scalar=ve, in1=ycur,
                                                                                              op0=ALU.mult, op1=ALU.mult), waits=[wprev], sig=True)
                    wb = P.add("dve", lambda e, ta_=ta_, tb_=tb_: e.tensor_scalar(out=tb_, in0=ta_, scalar1=-0.5, scalar2=1.5,
                                                                               op0=ALU.mult, op1=ALU.add), waits=[wa], sig=True)
                    wprev = P.add("dve", lambda e, ycur=ycur, tb_=tb_, ynew=ynew: e.tensor_tensor(out=ynew, in0=ycur, in1=tb_, op=ALU.mult),
                                  waits=[wb], sig=True)
                    ycur = ynew
                w_stA[tu] = wprev

            def stageB1(tu):
                k = tu % NU
                u_ = ub[k]
                w_7 = P.add("dve", lambda e, k=k: e.scalar_tensor_tensor(out=sm[k][:, 7:8], in0=sm[k][:, 2:3], scalar=-1.0, in1=sm[k][:, 6:7],
                                                                      op0=ALU.mult, op1=ALU.mult), waits=[w_stA[tu]], sig=True)
                w_stB[tu] = P.add("act", lambda e, u_=u_, k=k: e.activation(out=u_[:, :], in_=u_[:, :], func=AF.Identity,
                                                                            bias=sm[k][:, 7:8], scale=sm[k][:, 6:7]), waits=[w_7], sig=True)

            def stageB2(tu):
                b, t = divmod(tu, 4)
                k = tu % NU
                u_ = ub[k]
                w_g = P.add("dve", lambda e, u_=u_: e.tensor_tensor(out=u_[:, :], in0=u_[:, :], in1=gsb_[:, :], op=ALU.mult),
                            waits=[w_stB[tu], w_gb], sig=True)
                w_b = P.add("pool", lambda e, u_=u_: e.tensor_tensor(out=u_[:, :], in0=u_[:, :], in1=bsb_[:, :], op=ALU.add),
                            waits=[w_g], sig=True)
                w_s = P.dma("sync", x_dst[tu * 128:(tu + 1) * 128, :], u_[:, :], c3_st[k], waits=[w_b])
                w_u_free[tu] = w_s
                x_ready.setdefault(b, []).append(w_s)

            load_gt(0)
            if NB > 1:
                load_gt(1)
            for tu in range(min(3, NTI)):
                load_xt(tu)
            for tu in range(NTI + 2):
                if 0 <= tu - 1 < NTI:
                    stageB1(tu - 1)
                if 0 <= tu - 2 < NTI:
                    stageB2(tu - 2)
                if tu < NTI:
                    stageA(tu)
                yield

        import os
        kstop = int(os.environ.get("KSTOP", "99"))
        nofuse = bool(os.environ.get("KNOFUSE"))

        def run_all(gen):
            for _ in gen:
                pass

        x_src = x_d
        with ExitStack() as st:
            run_all(phase1_main(0, ltypes[0], x_src, st, False))
            P.barrier()
            P.emit()
        if ltypes[0] == "B":
            phase1_post(0)
        for li, typ in enumerate(ltypes):
            last = li == NL - 1
            loaded_all.clear()
            if kstop >= 2:
                phase2(li, typ)
            x_dst = y_d if last else xbuf
            if kstop >= 3:
                if last:
                    with ExitStack() as st:
                        run_all(phase3_main(li, typ, x_src, x_dst, st, False, {}))
                        P.barrier()
                        P.emit()
                elif nofuse:
                    with ExitStack() as st:
                        run_all(phase3_main(li, typ, x_src, x_dst, st, False, {}))
                        P.barrier()
                        P.emit()
                    with ExitStack() as st:
                        run_all(phase1_main(li + 1, ltypes[li + 1], xbuf, st, False))
                        P.barrier()
                        P.emit()
                else:
                    with ExitStack() as st:
                        x_ready = {}
                        g3 = phase3_main(li, typ, x_src, x_dst, st, True, x_ready)
                        g1 = phase1_main(li + 1, ltypes[li + 1], xbuf, st, True, x_ready)
                        LEAD = 14
                        done3 = False
                        done1 = False
                        for _ in range(LEAD):
                            if next(g3, "end") == "end":
                                done3 = True
                                break
                        while not (done1 and done3):
                            if not done1:
                                if next(g1, "end") == "end":
                                    done1 = True
                            if not done3:
                                if next(g3, "end") == "end":
                                    done3 = True
                        P.barrier()
                        P.emit()
                if not last and ltypes[li + 1] == "B":
                    phase1_post(li + 1)
            x_src = xbuf
        print("ops recorded:", P.nops)
    return nc


def host_consts():
    ident = np.eye(128, dtype=np.float32)
    k = np.arange(128)[:, None]
    q = np.arange(128)[None, :]
    tri = np.where(k > q, NEG, 0.0).astype(np.float32)
    maskA = np.zeros((128, 5, 128), np.float32)
    maskA[64:, 0, :64] = NEG
    maskA[:64, 4, 64:] = NEG
    return ident, tri, maskA.reshape(128, 640)


def host_bias_layout(rel_bias_a):
    k = np.arange(128)[:, None, None]
    j = np.arange(5)[None, :, None]
    q = np.arange(128)[None, None, :]
    idx = np.clip(128 * j + q - k, -128, 128) + 128
    out = rel_bias_a[:, :, idx]
    return np.ascontiguousarray(out.reshape(rel_bias_a.shape[0], H, 128, 640)).astype(np.float32)


def make_common(w_in_a, rel_bias_a, w_out_a, w_in_b, w_f_b, b_f_b, w_out_b, ln_g, ln_b):
    ident, tri, maskA = host_consts()
    f = lambda a: np.ascontiguousarray(np.asarray(a, dtype=np.float32))
    return {
        "w_in_a": f(w_in_a), "w_in_b": f(w_in_b), "w_out_a": f(w_out_a), "w_out_b": f(w_out_b),
        "w_f_b": f(w_f_b), "bfb": f(np.asarray(b_f_b)[:, :, None]),
        "biasT": host_bias_layout(np.asarray(rel_bias_a, dtype=np.float32)),
        "lng": f(np.broadcast_to(np.asarray(ln_g)[:, None, :], (4, 128, D))),
        "lnb": f(np.broadcast_to(np.asarray(ln_b)[:, None, :], (4, 128, D))),
        "ident": ident, "tri": tri, "maskA": maskA,
        "cbias": f(np.broadcast_to(np.asarray(rel_bias_a, dtype=np.float32)[:, None, :, 256], (2, 128, H))),
    }


def kernel(x, w_in_a, rel_bias_a, w_out_a, w_in_b, w_f_b, b_f_b, w_out_b, ln_g, ln_b):
    x = np.asarray(x, dtype=np.float32)
    common = make_common(w_in_a, rel_bias_a, w_out_a, w_in_b, w_f_b, b_f_b, w_out_b, ln_g, ln_b)
    nc = build(SEQ, "ABAB")
    in_maps = []
    for c in range(NCORES):
        m = dict(common)
        m["x"] = np.ascontiguousarray(x[c])
        in_maps.append(m)
    res = run_bass_kernel_spmd(nc, in_maps, core_ids=list(range(NCORES)))
    return np.stack([np.asarray(r["y"], dtype=np.float32) for r in res.results], axis=0)
```
